# Optimizing a Trainium2 kernel written in Bass

```python
import jax, jax.numpy as jnp
from jax import lax
import numpy as np

D_MODEL = 2048
BATCH = 4
SEQ = 8192
DEPTH = 1

PLE_DIM = 256
M_HEADS = 4
M_QK_DIM = 128
M_V_DIM = 256
M_CONV = 4
M_CHUNK = 64
F_BIAS_LO = 3.0
F_BIAS_HI = 6.0
A_HEADS = 8
A_HEAD_DIM = 128
MOBA_BLOCK = 256
MOBA_TOPK = 3
A_QCHUNK = 16
ROPE_THETA = 500000.0
ROPE_DIM = A_HEAD_DIM // 4
D_FF = 4 * D_MODEL
EPS = 1e-6

M_QK_W = M_HEADS * M_QK_DIM
M_V_W = M_HEADS * M_V_DIM
A_W = A_HEADS * A_HEAD_DIM
SPLITS = (M_QK_W, M_QK_W, M_V_W, M_V_W, M_HEADS, M_HEADS, A_W, A_W, A_W, D_MODEL, D_MODEL)
IN_WIDTH = M_QK_W * 2 + M_V_W * 2 + M_HEADS * 2 + A_W * 3 + D_MODEL * 2

kernel_name = 'hybrid_mlstm_moba_block'


def rmsnorm(x, g):
    x32 = x.astype(jnp.float32)
    y = x32 * lax.rsqrt(jnp.mean(x32 * x32, axis=-1, keepdims=True) + EPS)
    return (y * g.astype(jnp.float32)).astype(x.dtype)


def to_heads(t, n):
    b, s, _ = t.shape
    return t.reshape(b, s, n, -1).transpose(0, 2, 1, 3)


def causal_dwconv(x, w, b):
    y = lax.conv_general_dilated(
        x, w[:, None, :].astype(x.dtype), window_strides=(1,), padding=[(M_CONV - 1, 0)],
        dimension_numbers=('NWC', 'WIO', 'NWC'), feature_group_count=x.shape[-1])
    return y + b.astype(x.dtype)


def partial_rope(x, positions):
    half = ROPE_DIM // 2
    inv_freq = ROPE_THETA ** (-jnp.arange(half, dtype=jnp.float32) * 2.0 / ROPE_DIM)
    ang = positions.astype(jnp.float32)[:, None, :, None] * inv_freq
    cos, sin = jnp.cos(ang), jnp.sin(ang)
    xr = x[..., :ROPE_DIM].astype(jnp.float32)
    x1, x2 = xr[..., :half], xr[..., half:]
    rot = jnp.concatenate([x1 * cos - x2 * sin, x2 * cos + x1 * sin], axis=-1).astype(x.dtype)
    return jnp.concatenate([rot, x[..., ROPE_DIM:]], axis=-1)


def mlstm_chunkwise(q, k, v, i_pre, f_pre):
    B, H, S, dk = q.shape
    dv = v.shape[-1]
    L = M_CHUNK
    N = S // L
    qc = q.reshape(B, H, N, L, dk)
    kc = k.reshape(B, H, N, L, dk)
    vc = v.reshape(B, H, N, L, dv)
    ig = i_pre.reshape(B, H, N, L)
    bcum = jnp.cumsum(jax.nn.log_sigmoid(f_pre).reshape(B, H, N, L), axis=-1)

    def step(carry, xs):
        C, n, m = carry
        q_, k_, v_, ig_, b_ = xs
        qC = jnp.einsum('bhld,bhdv->bhlv', q_, C)
        qn = jnp.einsum('bhld,bhd->bhl', q_, n)
        g = b_[..., -1]
        a = g[..., None] - b_ + ig_
        m_new = jnp.maximum(g + m, jnp.max(a, axis=-1))
        w = jnp.exp(a - m_new[..., None])
        decay = jnp.exp(g + m - m_new)
        C = decay[..., None, None] * C + jnp.einsum('bhl,bhld,bhlv->bhdv', w, k_, v_)
        n = decay[..., None] * n + jnp.einsum('bhl,bhld->bhd', w, k_)
        return (C, n, m_new), (qC, qn, m)

    xs = (jnp.moveaxis(qc, 2, 0), jnp.moveaxis(kc, 2, 0), jnp.moveaxis(vc, 2, 0),
          jnp.moveaxis(ig, 2, 0), jnp.moveaxis(bcum, 2, 0))
    init = (jnp.zeros((B, H, dk, dv), jnp.float32), jnp.zeros((B, H, dk), jnp.float32),
            jnp.zeros((B, H), jnp.float32))
    _, (qC, qn, m_prev) = lax.scan(step, init, xs)
    qC = jnp.moveaxis(qC, 0, 2)
    qn = jnp.moveaxis(qn, 0, 2)
    m_prev = jnp.moveaxis(m_prev, 0, 2)

    causal = jnp.tril(jnp.ones((L, L), dtype=bool))
    dmat = jnp.where(causal, bcum[..., :, None] - bcum[..., None, :] + ig[..., None, :], -jnp.inf)
    inter = bcum + m_prev[..., None]
    m_t = jnp.maximum(inter, jnp.max(dmat, axis=-1))
    s = jnp.einsum('bhntd,bhnsd->bhnts', qc, kc) * jnp.exp(dmat - m_t[..., None])
    e_inter = jnp.exp(inter - m_t)
    num = e_inter[..., None] * qC + jnp.einsum('bhnts,bhnsv->bhntv', s, vc)
    den = e_inter * qn + jnp.sum(s, axis=-1)
    h = num / jnp.maximum(jnp.abs(den), jnp.exp(-m_t))[..., None]
    return h.reshape(B, H, S, dv)


def head_rmsnorm(h, g):
    B, H, S, dv = h.shape
    y = h * lax.rsqrt(jnp.mean(h * h, axis=-1, keepdims=True) + EPS)
    y = y * g.astype(jnp.float32).reshape(H, dv)[None, :, None, :]
    return y.transpose(0, 2, 1, 3).reshape(B, S, H * dv)


def moba_attention(q, k, v):
    B, H, S, Dh = q.shape
    BLK = MOBA_BLOCK
    NB = -(-S // BLK)
    S_pad = NB * BLK
    pad = ((0, 0), (0, 0), (0, S_pad - S), (0, 0))
    kb = jnp.pad(k, pad).reshape(B, H, NB, BLK, Dh)
    vb = jnp.pad(v, pad).reshape(B, H, NB, BLK, Dh)
    k_mean = jnp.mean(kb.astype(jnp.float32), axis=3)
    scores = jnp.einsum('bhsd,bhnd->bhsn', q.astype(jnp.float32), k_mean)
    q_blk = jnp.arange(S) // BLK
    fully_past = jnp.arange(NB)[None, :] < q_blk[:, None]
    scores = jnp.where(fully_past, scores, -jnp.inf)
    K_SEL = min(MOBA_TOPK, NB)
    _, idx = lax.top_k(scores, K_SEL)

    QC = A_QCHUNK
    Nq = S // QC
    q_ch = jnp.moveaxis(q.reshape(B, H, Nq, QC, Dh), 2, 0)
    idx_ch = jnp.moveaxis(idx.reshape(B, H, Nq, QC, K_SEL), 2, 0)
    starts = jnp.arange(Nq, dtype=jnp.int32) * QC
    bi = jnp.arange(B)[:, None, None, None]
    hi = jnp.arange(H)[None, :, None, None]
    scale = Dh ** -0.5

    def one_chunk(args):
        q_c, idx_c, start = args
        c = start // BLK
        k_sel = kb[bi, hi, idx_c]
        v_sel = vb[bi, hi, idx_c]
        k_own = lax.dynamic_index_in_dim(kb, c, axis=2, keepdims=False)
        v_own = lax.dynamic_index_in_dim(vb, c, axis=2, keepdims=False)
        lp = jnp.einsum('bhqd,bhqjld->bhqjl', q_c, k_sel).astype(jnp.float32) * scale
        lp = jnp.where((jnp.arange(K_SEL) < c)[:, None], lp, -jnp.inf)
        lo = jnp.einsum('bhqd,bhld->bhql', q_c, k_own).astype(jnp.float32) * scale
        q_pos = start + jnp.arange(QC)
        k_pos = c * BLK + jnp.arange(BLK)
        lo = jnp.where(k_pos[None, :] <= q_pos[:, None], lo, -jnp.inf)
        logits = jnp.concatenate([lp.reshape(B, H, QC, K_SEL * BLK), lo], axis=-1)
        w = jax.nn.softmax(logits, axis=-1).astype(v.dtype)
        wp = w[..., :K_SEL * BLK].reshape(B, H, QC, K_SEL, BLK)
        wo = w[..., K_SEL * BLK:]
        return (jnp.einsum('bhqjl,bhqjld->bhqd', wp, v_sel)
                + jnp.einsum('bhql,bhld->bhqd', wo, v_own))

    out = lax.map(one_chunk, (q_ch, idx_ch, starts))
    return jnp.moveaxis(out, 0, 2).reshape(B, H, S, Dh)


def setup_inputs(seed: int = 0) -> dict:
    key = jax.random.key(seed)
    ks = jax.random.split(key, 24)

    def nrm(k, shape, scale):
        return jax.random.normal(k, shape, jnp.float32) * scale

    def gain(k, shape):
        return 1.0 + 0.02 * jax.random.normal(k, shape, jnp.float32)

    f_bias = jnp.linspace(F_BIAS_LO, F_BIAS_HI, M_HEADS, dtype=jnp.float32)
    b_if = jnp.stack([0.1 * jax.random.normal(ks[4], (DEPTH, M_HEADS), jnp.float32),
                      f_bias[None, :] + 0.1 * jax.random.normal(ks[5], (DEPTH, M_HEADS), jnp.float32)],
                     axis=1)
    return {
        'x': nrm(ks[0], (BATCH, SEQ, D_MODEL), 1.0),
        'p': nrm(ks[1], (DEPTH, BATCH, SEQ, PLE_DIM), 1.0),
        'positions': jnp.broadcast_to(jnp.arange(SEQ, dtype=jnp.int32), (BATCH, SEQ)),
        'attn_norm': gain(ks[2], (DEPTH, D_MODEL)),
        'w_in': nrm(ks[3], (DEPTH, D_MODEL, IN_WIDTH), D_MODEL ** -0.5),
        'b_if': b_if,
        'conv_w': nrm(ks[6], (DEPTH, M_CONV, 2 * M_QK_W), M_CONV ** -0.5),
        'conv_b': nrm(ks[7], (DEPTH, 2 * M_QK_W), 0.01),
        'm_out_norm': gain(ks[8], (DEPTH, M_V_W)),
        'w_up_m': nrm(ks[9], (DEPTH, M_V_W, D_MODEL), M_V_W ** -0.5),
        'w_up_a': nrm(ks[10], (DEPTH, A_W, D_MODEL), A_W ** -0.5),
        'w_out': nrm(ks[11], (DEPTH, D_MODEL, D_MODEL), D_MODEL ** -0.5),
        'mlp_norm': gain(ks[12], (DEPTH, D_MODEL)),
        'w_ff1': nrm(ks[13], (DEPTH, D_MODEL, D_FF), D_MODEL ** -0.5),
        'w_ff2': nrm(ks[14], (DEPTH, D_FF, D_MODEL), D_FF ** -0.5),
        'ple_norm': gain(ks[15], (DEPTH, D_MODEL)),
        'w_ple_gate': nrm(ks[16], (DEPTH, D_MODEL, D_MODEL), D_MODEL ** -0.5),
        'w_ple_proj': nrm(ks[17], (DEPTH, PLE_DIM, D_MODEL), PLE_DIM ** -0.5),
        'final_norm': gain(ks[18], (D_MODEL,)),
    }


def reference(x, p, positions, attn_norm, w_in, b_if, conv_w, conv_b, m_out_norm, w_up_m, w_up_a,
              w_out, mlp_norm, w_ff1, w_ff2, ple_norm, w_ple_gate, w_ple_proj, final_norm):
    offs = np.cumsum(np.array(SPLITS))[:-1].tolist()
    for i in range(DEPTH):
        h = rmsnorm(x, attn_norm[i])
        z = h @ w_in[i]
        mq, mk, mv, mo, mi, mf, aq, ak, av, gm, ga = jnp.split(z, offs, axis=-1)

        qk = jax.nn.silu(causal_dwconv(jnp.concatenate([mq, mk], axis=-1), conv_w[i], conv_b[i]))
        mq, mk = qk[..., :M_QK_W], qk[..., M_QK_W:]
        i_pre = (mi.astype(jnp.float32) + b_if[i, 0].astype(jnp.float32)).transpose(0, 2, 1)
        f_pre = (mf.astype(jnp.float32) + b_if[i, 1].astype(jnp.float32)).transpose(0, 2, 1)
        hm = mlstm_chunkwise(to_heads(mq, M_HEADS).astype(jnp.float32),
                             to_heads(mk, M_HEADS).astype(jnp.float32) * (M_QK_DIM ** -0.5),
                             to_heads(mv, M_HEADS).astype(jnp.float32), i_pre, f_pre)
        hm = head_rmsnorm(hm, m_out_norm[i]).astype(x.dtype) * jax.nn.sigmoid(mo)

        qa = partial_rope(to_heads(aq, A_HEADS), positions)
        ka = partial_rope(to_heads(ak, A_HEADS), positions)
        va = to_heads(av, A_HEADS)
        ha = moba_attention(qa, ka, va)
        ha = ha.transpose(0, 2, 1, 3).reshape(x.shape[0], x.shape[1], A_W)

        merged = (jax.nn.sigmoid(gm) * (hm @ w_up_m[i])
                  + jax.nn.sigmoid(ga) * (ha @ w_up_a[i]))
        x = x + merged @ w_out[i]

        h = rmsnorm(x, mlp_norm[i])
        x = x + jnp.square(jax.nn.relu(h @ w_ff1[i])) @ w_ff2[i]

        h = rmsnorm(x, ple_norm[i])
        x = x + jax.nn.sigmoid(h @ w_ple_gate[i]) * (p[i] @ w_ple_proj[i])
    return rmsnorm(x, final_norm)
```

```python
import contextlib
import math
import numpy as np
import concourse.bass as bass
import concourse.mybir as mybir
from concourse.bass_utils import run_bass_kernel_spmd

F32 = mybir.dt.float32
BF16 = mybir.dt.bfloat16
I32 = mybir.dt.int32
AF = mybir.ActivationFunctionType
ALU = mybir.AluOpType
AX = mybir.AxisListType

D = 2048
KC = 16
DFF = 8192
INW = 10248
EPS = 1e-6
NEGBIG = -30000.0
O_MQ, O_MK, O_MV, O_MO, O_MI, O_AQ, O_AK, O_AV, O_GM, O_GA = 0, 512, 1024, 2048, 3072, 3080, 4104, 5128, 6152, 8200


class Buf:
    __slots__ = ("name", "w", "r")

    def __init__(self, name=""):
        self.name = name
        self.w = None
        self.r = []


class DramBuf:
    __slots__ = ("name", "ws")

    def __init__(self, name=""):
        self.name = name
        self.ws = {}


class DmaSem:
    __slots__ = ("name", "count", "h")

    def __init__(self, name):
        self.name = name
        self.count = 0
        self.h = None


class Slot:
    __slots__ = ("t", "b", "d")

    def __init__(self, t, b, d):
        self.t, self.b, self.d = t, b, d


class Ring:
    def __init__(self, slots):
        self.slots = slots
        self.i = 0

    def next(self):
        s = self.slots[self.i % len(self.slots)]
        self.i += 1
        return s


def mk(base, off, dims):
    return bass.AP(base.tensor, base.offset + off, [[base.ap[0][0], 128]] + [list(d) for d in dims])


class Prog:
    ENGS = ("pe", "act", "dve", "pool", "sp")

    def __init__(self, nc):
        self.nc = nc
        self.ops = {e: [] for e in self.ENGS}
        self.serial = {e: 0 for e in self.ENGS}
        self.waited = {e: {} for e in self.ENGS}
        self.dsems = []
        self.stack = contextlib.ExitStack()
        self.nn = 0

    def init_arena(self, nbytes_per_part):
        self.arena_n = nbytes_per_part // 2
        self.arena = self.stack.enter_context(self.nc.sbuf_tensor("arena", [128, self.arena_n], BF16))
        self.aoff = 0
        self.apeak = 0

    def carve(self, shape, dt):
        n = 1
        for s_ in shape[1:]:
            n *= s_
        units = n if dt == BF16 else 2 * n
        self.aoff = (self.aoff + 15) // 16 * 16
        a = self.arena[:, self.aoff:self.aoff + units]
        self.aoff += units
        self.apeak = max(self.apeak, self.aoff)
        assert self.aoff <= self.arena_n, "SBUF arena overflow: %d > %d" % (self.aoff * 2, self.arena_n * 2)
        if dt != BF16:
            a = a.bitcast(dt)
        if len(shape) > 2:
            dims = []
            st_ = n
            for s_ in shape[1:]:
                st_ //= s_
                dims.append([st_, s_])
            a = bass.AP(a.tensor, a.offset, [[a.ap[0][0], 128]] + dims)
        return a

    def slot(self, name, shape, dt, dma=False):
        self.nn += 1
        nm = "%s_%d" % (name, self.nn)
        return Slot(self.carve(shape, dt), Buf(nm), self.dsem(nm) if dma else None)

    def ring(self, name, n, shape, dt, dma=False):
        return Ring([self.slot(name, shape, dt, dma) for _ in range(n)])

    def dsem(self, name):
        d = DmaSem(name)
        self.dsems.append(d)
        return d

    def _deps(self, reads, writes):
        deps = {}

        def add(tok):
            if tok is None:
                return
            k, v = tok
            if deps.get(k, 0) < v:
                deps[k] = v

        for b in reads:
            if isinstance(b, DramBuf):
                for k, v in b.ws.items():
                    add((k, v))
            else:
                add(b.w)
        for b in writes:
            if isinstance(b, DramBuf):
                continue
            add(b.w)
            for t in b.r:
                add(t)
        return deps

    def emit(self, eng, fn, reads=(), writes=(), dma=None):
        deps = self._deps(reads, writes)
        waits = []
        wd = self.waited[eng]
        for k, v in deps.items():
            if k == eng and eng in ("pe", "sp"):
                continue
            if wd.get(k, 0) >= v:
                continue
            wd[k] = v
            waits.append((k, v))
        if dma is not None:
            dma.count += 16
            tok = (dma, dma.count)
            ser = None
        else:
            self.serial[eng] += 1
            ser = self.serial[eng]
            tok = (eng, ser)
        self.ops[eng].append((waits, fn, ser, dma))
        for b in reads:
            if not isinstance(b, DramBuf):
                b.r.append(tok)
        for b in writes:
            if isinstance(b, DramBuf):
                k, v = tok
                if b.ws.get(k, 0) < v:
                    b.ws[k] = v
            else:
                b.w = tok
                b.r = []
        return tok

    def op(self, eng, method, reads=(), writes=(), dma=None, **kw):
        return self.emit(eng, lambda e, m=method, kw=kw: getattr(e, m)(**kw), reads, writes, dma)

    def wait_all(self, eng, bufs):
        deps = self._deps(bufs, ())
        waits = []
        wd = self.waited[eng]
        for k, v in deps.items():
            if wd.get(k, 0) >= v:
                continue
            wd[k] = v
            waits.append((k, v))
        self.ops[eng].append((waits, None, None, None))

    def finalize(self):
        nc = self.nc
        needed = {e: set() for e in self.ENGS}
        for e in self.ENGS:
            for waits, fn, ser, dma in self.ops[e]:
                for k, v in waits:
                    if isinstance(k, str):
                        needed[k].add(v)
        vmap = {e: {v: i + 1 for i, v in enumerate(sorted(needed[e]))} for e in self.ENGS}
        esem = {e: self.stack.enter_context(nc.semaphore("s_" + e)) for e in self.ENGS}
        for d in self.dsems:
            if d.count:
                d.h = self.stack.enter_context(nc.semaphore("d_" + d.name))
        handles = {"pe": "tensor", "act": "scalar", "dve": "vector", "pool": "gpsimd", "sp": "sync"}
        stats = {e: len(self.ops[e]) for e in self.ENGS}
        stats["ndsem"] = sum(1 for d in self.dsems if d.count)
        with nc.Block() as block:
            for e in self.ENGS:
                ops = self.ops[e]

                def body(h, ops=ops, e=e):
                    for waits, fn, ser, dma in ops:
                        for k, v in waits:
                            if isinstance(k, str):
                                h.wait_ge(esem[k], vmap[k][v])
                            else:
                                h.wait_ge(k.h, v)
                        if fn is None:
                            continue
                        ins = fn(h)
                        if dma is not None:
                            ins.then_inc(dma.h, 16)
                        elif ser in vmap[e]:
                            ins.then_inc(esem[e], 1)

                getattr(block, handles[e])(body)
        return stats


def ps_alloc(P, name, shape, dt=F32):
    return P.stack.enter_context(P.nc.psum_tensor(name, list(shape), dt))


def barrier(P, bufs):
    deps = {}
    for b in bufs:
        for tok in ([b.w] if b.w else []) + list(b.r):
            k, v = tok
            if deps.get(k, 0) < v:
                deps[k] = v
    for e in P.ENGS:
        waits = []
        wd = P.waited[e]
        for k, v in deps.items():
            if wd.get(k, 0) >= v:
                continue
            wd[k] = v
            waits.append((k, v))
        P.ops[e].append((waits, None, None, None))


class Phase:
    def __init__(self, P):
        self.P = P

    def __enter__(self):
        self.mark = self.P.aoff
        self.bufs = []
        self._slot = self.P.slot
        P = self.P

        def slot(name, shape, dt, dma=False, _o=self._slot, _s=self):
            s = _o(name, shape, dt, dma)
            _s.bufs.append(s.b)
            return s
        P.slot = slot
        return self

    def __exit__(self, *a):
        self.P.slot = self._slot
        barrier(self.P, self.bufs)
        self.P.aoff = self.mark
        return False


C_GAT, C_GMLP, C_GPLE, C_GMO, C_CW, C_CB, C_BIF, C_INVF, C_PVAL, C_TRI, C_ONES, C_PMASK = 0, 16, 32, 48, 56, 88, 96, 104, 120, 121, 249, 377
CB_ID, CB_TRI, CB_TOT = 0, 128, 256
CC_CM, CC_CMQ, CC_ESEL = 0, 512, 1024
CC_TOT = 1024 + 33 * 128


def host_consts(T):
    tri = (np.arange(128)[:, None] <= np.arange(128)[None, :]).astype(np.float32)
    cb = np.zeros((128, CB_TOT), np.float32)
    cb[:, CB_ID:CB_ID + 128] = np.eye(128, dtype=np.float32)
    cb[:, CB_TRI:CB_TRI + 128] = tri
    cc = np.zeros((128, CC_TOT), np.float32)
    for kt in range(2):
        key = kt * 128 + np.arange(128)[:, None]
        q = np.arange(256)[None, :]
        cc[:, CC_CM + kt * 256:CC_CM + (kt + 1) * 256] = np.where(key <= q, 0.0, NEGBIG)
        qq = kt * 128 + np.arange(128)[:, None]
        key2 = np.arange(256)[None, :]
        cc[:, CC_CMQ + kt * 256:CC_CMQ + (kt + 1) * 256] = np.where(key2 <= qq, 0.0, NEGBIG)
    for n in range(33):
        cc[n, CC_ESEL + n * 128:CC_ESEL + (n + 1) * 128] = 1.0
    return tri, cb, cc


def build(T, debug=None):
    S2 = 2 * T
    NT = S2 // 128
    NTO = T // 128
    TB = 1024
    NBLK = S2 // TB
    PBLK = NBLK // 2
    NB = S2 // 256
    NBP = NB // 2
    NBO = NB - NBP
    assert NB <= 32
    NCST = C_PMASK + NBO * NB
    okind = "ExternalOutput" if debug else "Internal"

    nc = bass.Bass("TRN2", target_bir_lowering=False)
    dt_in = lambda n, s, d=F32: nc.dram_tensor(n, list(s), d, kind="ExternalInput")
    xs = dt_in("xs", [S2, D]); pp_in = dt_in("p", [T, 256]); pos = dt_in("pos", [128, NT], I32)
    cst = dt_in("cst", [128, NCST]); cstb = dt_in("cstb", [128, CB_TOT]); cstc = dt_in("cstc", [128, CC_TOT])
    gfin = dt_in("gfin", [128, D])
    w_in = dt_in("w_in", [D, INW]); w_up_m = dt_in("w_up_m", [1024, D]); w_up_a = dt_in("w_up_a", [1024, D])
    w_out = dt_in("w_out", [D, D]); w_ff1 = dt_in("w_ff1", [D, DFF]); w_ff2 = dt_in("w_ff2", [DFF, D])
    w_pg = dt_in("w_pg", [D, D]); w_pp = dt_in("w_pp", [256, D])
    out = nc.dram_tensor("out", [T, D], F32, kind="ExternalOutput")
    scr = lambda n, s: nc.dram_tensor(n, list(s), BF16, kind=okind)
    mqT_s = scr("mqT_s", [512, T]); mkT_s = scr("mkT_s", [512, S2]); mv_s = scr("mv_s", [S2, 1024])
    smo_s = scr("smo_s", [T, 1024]); aqT_s = scr("aqT_s", [1024, T]); akT_s = scr("akT_s", [1024, S2])
    av_s = scr("av_s", [S2, 1024]); sgmT_s = scr("sgmT_s", [D, T]); sgaT_s = scr("sgaT_s", [D, T])
    hmT_s = scr("hmT_s", [1024, T]); haT_s = scr("haT_s", [1024, T])
    D_mqT, D_mkT, D_mv, D_smo, D_aqT, D_akT, D_av, D_sgm, D_sga, D_hmT, D_haT, D_out = [DramBuf(n) for n in
        ("mqT", "mkT", "mv", "smo", "aqT", "akT", "av", "sgm", "sga", "hmT", "haT", "out")]
    dbg_out = {}
    wsrc = dict(up_m=w_up_m, up_a=w_up_a, out=w_out, ff1=w_ff1, ff2=w_ff2, pg=w_pg, pp=w_pp)
    wb = {}
    D_wb = {}
    conv_jobs = []
    for nm_, W_ in wsrc.items():
        R_, C_ = W_.shape
        wb[nm_] = nc.dram_tensor("wb_" + nm_, [R_, C_], BF16, kind="Internal")
        D_wb[nm_] = DramBuf("wb_" + nm_)
        a_ = C_ // 2048
        srcf = W_.ap().rearrange("r (a b) -> (r a) b", b=2048) if a_ > 1 else W_.ap()
        dstf = wb[nm_].ap().rearrange("r (a b) -> (r a) b", b=2048) if a_ > 1 else wb[nm_].ap()
        rows = R_ * a_
        for r0 in range(0, rows, 512):
            r1 = min(rows, r0 + 512)
            conv_jobs.append((srcf[r0:r1, :], dstf[r0:r1, :], D_wb[nm_]))

    P = Prog(nc)
    op = P.op
    P.init_arena(172 * 1024)
    cv_sems = [P.dsem("cv%d" % i) for i in range(4)]
    cv_state = {"i": 0}

    def emit_conv_job():
        i = cv_state["i"]
        if i >= len(conv_jobs):
            return False
        src_, dst_, db_ = conv_jobs[i]
        cv_state["i"] = i + 1
        op("pool", "dma_start", writes=[db_], dma=cv_sems[i % 4], out=dst_, in_=src_)
        return True

    cs = P.slot("cst", [128, NCST], F32, dma=True)
    cbs = P.slot("cstb", [128, CB_TOT], BF16, dma=True)
    CS, CBt = cs.t, cbs.t
    ident = CBt[:, CB_ID:CB_ID + 128]
    tri_bf = CBt[:, CB_TRI:CB_TRI + 128]
    tri_f = CS[:, C_TRI:C_TRI + 128]
    ones_f = CS[:, C_ONES:C_ONES + 128]
    G = P.slot("G", [128, NT, 8], F32)
    kmean = P.slot("kmean", [128, 8, NB], F32)
    kmean_b = P.slot("kmean_b", [128, 8, NB], BF16)
    halo = P.slot("halo", [128, 8, 3], F32)
    sm = P.ring("sm", 4, [128, 4], F32)
    psf = ps_alloc(P, "psf", [128, 6 * 512], F32)
    bank = lambda i: psf[:, i * 512:(i + 1) * 512]
    acc = Ring([Slot(bank(i), Buf("acc%d" % i), None) for i in range(4)])
    aux = [Slot(bank(4 + i), Buf("aux%d" % i), None) for i in range(2)]
    psb = ps_alloc(P, "psb", [128, 2048], BF16)[:, :]
    tps = Ring([Slot(psb[:, i * 1024:i * 1024 + 512], Buf("tp%d" % i), None) for i in range(2)])

    op("sp", "dma_start", writes=[cs.b], dma=cs.d, out=CS, in_=cst[:, :])
    op("pool", "dma_start", writes=[cbs.b], dma=cbs.d, out=CBt, in_=cstb[:, :])
    op("dve", "memset", writes=[halo.b], ap=halo.t, constant=0.0)
    op("dve", "memset", writes=[kmean.b], ap=kmean.t, constant=0.0)

    def load_w(ring, W, r0, nk, c0, ncol, dep=None, q="pool"):
        s = ring.next()
        for k0 in range(0, nk, 4):
            k1 = min(nk, k0 + 4)
            src = W[r0 + k0 * 128:r0 + k1 * 128, c0:c0 + ncol].rearrange("(k p) c -> p k c", p=128)
            op(q, "dma_start", reads=([dep] if dep is not None else []), writes=[s.b], dma=s.d, out=s.t[:, k0:k1, 0:ncol], in_=src)
        return s

    def norm_stats(xa, xb_, width, junk_ap, junk_b):
        s = sm.next()
        op("act", "activation", reads=[xb_], writes=[junk_b, s.b], out=junk_ap, in_=xa, func=AF.Square, accum_out=s.t[:, 0:1])
        op("dve", "tensor_scalar", reads=[s.b], writes=[s.b], out=s.t[:, 1:2], in0=s.t[:, 0:1], scalar1=1.0 / width, scalar2=EPS, op0=ALU.mult, op1=ALU.add)
        op("act", "activation", reads=[s.b], writes=[s.b], out=s.t[:, 2:3], in_=s.t[:, 1:2], func=AF.Sqrt)
        op("dve", "reciprocal", reads=[s.b], writes=[s.b], out=s.t[:, 3:4], in_=s.t[:, 2:3])
        return s

    def norm_transpose(tiles, gcol, hT, xn, grp):
        xns = []
        nt = len(tiles)
        for t, (xa, xb_) in enumerate(tiles):
            n_ = xn.next()
            s = norm_stats(xa, xb_, D, n_.t, n_.b)
            op("dve", "tensor_scalar", reads=[xb_, s.b], writes=[n_.b], out=n_.t, in0=xa, scalar1=s.t[:, 3:4], scalar2=None, op0=ALU.mult)
            xns.append(n_)
            if len(xns) == grp or t == nt - 1:
                t0 = t + 1 - len(xns)
                w = len(xns) * 128
                for k in range(KC):
                    tp = tps.next()
                    for i, n2 in enumerate(xns):
                        op("pe", "transpose", reads=[n2.b, cbs.b], writes=[tp.b], out=tp.t[:, i * 128:(i + 1) * 128],
                           in_=n2.t[:, k * 128:(k + 1) * 128], identity=ident)
                    op("dve", "tensor_tensor", reads=[tp.b, cs.b], writes=[hT.b], out=hT.t[:, k, t0 * 128:t0 * 128 + w],
                       in0=tp.t[:, 0:w], in1=mk(CS, gcol + k, [[0, w]]), op=ALU.mult)
                xns = []

    with Phase(P):
        NF = NT * 16
        sincos = P.slot("sincos", [128, 2, NF], F32)
        with Phase(P):
            posi = P.slot("posi", [128, NT], I32, dma=True)
            posf = P.slot("posf", [128, NT], F32)
            tb = P.slot("tab", [128, 4, NF], F32)
            TT = tb.t
            op("sp", "dma_start", writes=[posi.b], dma=posi.d, out=posi.t, in_=pos[:, :])
            op("dve", "tensor_copy", reads=[posi.b], writes=[posf.b], out=posf.t, in_=posi.t)
            a_pos = mk(posf.t, 0, [[1, NT], [0, 16]])
            a_inv = mk(CS, C_INVF, [[0, NT], [1, 16]])
            op("dve", "tensor_tensor", reads=[posf.b, cs.b], writes=[tb.b], out=mk(TT, 0, [[16, NT], [1, 16]]), in0=a_pos, in1=a_inv, op=ALU.mult)
            TWO_PI = 2.0 * math.pi
            C1 = 6.28125
            C2 = TWO_PI - C1
            MAGIC = 12582912.0
            PI_LO = 3.1415925
            for which, shift in ((0, 0.0), (1, math.pi / 2)):
                rw = dict(reads=[tb.b], writes=[tb.b])
                op("dve", "tensor_scalar", out=TT[:, 3, :], in0=TT[:, 0, :], scalar1=shift, scalar2=None, op0=ALU.add, **rw)
                op("dve", "tensor_scalar", out=TT[:, 1, :], in0=TT[:, 3, :], scalar1=1.0 / TWO_PI, scalar2=None, op0=ALU.mult, **rw)
                op("dve", "tensor_scalar", out=TT[:, 2, :], in0=TT[:, 1, :], scalar1=MAGIC, scalar2=None, op0=ALU.add, **rw)
                op("dve", "tensor_scalar", out=TT[:, 1, :], in0=TT[:, 2, :], scalar1=-MAGIC, scalar2=None, op0=ALU.add, **rw)
                op("dve", "scalar_tensor_tensor", out=TT[:, 2, :], in0=TT[:, 1, :], scalar=-C1, in1=TT[:, 3, :], op0=ALU.mult, op1=ALU.add, **rw)
                op("dve", "scalar_tensor_tensor", out=TT[:, 3, :], in0=TT[:, 1, :], scalar=-C2, in1=TT[:, 2, :], op0=ALU.mult, op1=ALU.add, **rw)
                op("dve", "tensor_scalar", out=TT[:, 3, :], in0=TT[:, 3, :], scalar1=PI_LO, scalar2=-PI_LO, op0=ALU.min, op1=ALU.max, **rw)
                op("act", "activation", reads=[tb.b], writes=[sincos.b], out=sincos.t[:, which, :], in_=TT[:, 3, :], func=AF.Sin)

        if debug == "A0":
            dbg = nc.dram_tensor("dbg_sincos", [128, 2 * NF], F32, kind="ExternalOutput")
            dd = P.dsem("dbg")
            op("sp", "dma_start", reads=[sincos.b], writes=[D_out], dma=dd, out=dbg[:, :], in_=mk(sincos.t, 0, [[1, 2 * NF]]))
            return finish(P, nc, [D_out])

        def tabv(which, tile, nh):
            return mk(sincos.t, which * NF + tile * 16, [[0, nh], [1, 16]])

        xt = P.ring("xt", 2, [128, D], F32, dma=True)
        xn = P.ring("xn", 4, [128, D], BF16)
        hT = P.slot("hT", [128, KC, TB], BF16)
        wt = P.ring("wt", 2, [128, KC, 512], BF16, dma=True)
        wg = P.ring("wg", 2, [128, KC, 8], BF16, dma=True)
        st = P.ring("st", 4, [128, 512], BF16, dma=True)
        rf = P.ring("rf", 2, [128, 512], F32)
        rtmp = P.ring("rtmp", 2, [128, 4, 4, 16], F32)
        rb = P.ring("rb", 2, [128, 512], BF16)
        qk_st = P.ring("qkst", 2, [128, 4, TB], BF16, dma=True)
        zp = P.ring("zp", 2, [128, 516], F32)
        cy = P.ring("cy", 2, [128, 512], F32)
        csg = P.ring("csg", 2, [128, 512], F32)
        cob = P.ring("cob", 2, [128, 512], BF16, dma=True)
        pending = []

        def flush_pending():
            while pending:
                pending.pop(0)()

        pre_x = []
        for blk in range(NBLK):
            own = blk >= PBLK
            tok0 = blk * TB
            otok0 = tok0 - T
            tiles = []
            for t in range(TB // 128):
                if pre_x:
                    xs_ = pre_x.pop(0)
                else:
                    xs_ = xt.next()
                    op("sp", "dma_start", writes=[xs_.b], dma=xs_.d, out=xs_.t, in_=xs[tok0 + t * 128:tok0 + (t + 1) * 128, :])
                tiles.append((xs_.t, xs_.b))
                if debug == "A1a" and len(tiles) == 2:
                    dbg = nc.dram_tensor("dbg_x", [128, D], F32, kind="ExternalOutput")
                    dd = P.dsem("dbg")
                    op("sp", "dma_start", reads=[xs_.b], writes=[D_out], dma=dd, out=dbg[:, :], in_=xs_.t)
                    return finish(P, nc, [D_out])
                if len(tiles) == 2:
                    t0 = t - 1
                    norm_transpose(tiles, C_GAT, Slot(mk(hT.t, t0 * 128, [[TB, KC], [1, 256]]), hT.b, None), xn, 2)
                    tiles = []
                    if debug == "A1c":
                        dbg = nc.dram_tensor("dbg_hT", [128, KC * TB], BF16, kind="ExternalOutput")
                        dd = P.dsem("dbg")
                        for k_ in range(KC):
                            op("sp", "dma_start", reads=[hT.b], writes=[D_out], dma=dd, out=dbg[:, k_ * TB:(k_ + 1) * TB], in_=hT.t[:, k_, :])
                        return finish(P, nc, [D_out])
            if debug == "A1":
                dbg = nc.dram_tensor("dbg_hT", [128, KC * TB], BF16, kind="ExternalOutput")
                dd = P.dsem("dbg")
                op("sp", "dma_start", reads=[hT.b], writes=[D_out], dma=dd, out=dbg[:, :], in_=mk(hT.t, 0, [[1, KC * TB]]))
                return finish(P, nc, [D_out])
            groups = []
            if blk >= PBLK - 1:
                groups.append(("conv", O_MQ, 0))
            groups.append(("conv", O_MK, 4))
            groups += [("v", O_MV, mv_s, D_mv, 0), ("v", O_MV + 512, mv_s, D_mv, 512)]
            if own:
                groups += [("sig", O_MO, smo_s, D_smo, 0), ("sig", O_MO + 512, smo_s, D_smo, 512)]
                groups += [("rope", O_AQ, aqT_s, D_aqT, 0, False), ("rope", O_AQ + 512, aqT_s, D_aqT, 4, False)]
            groups += [("rope", O_AK, akT_s, D_akT, 0, True), ("rope", O_AK + 512, akT_s, D_akT, 4, True)]
            groups += [("v", O_AV, av_s, D_av, 0), ("v", O_AV + 512, av_s, D_av, 512)]
            if own:
                for i in range(4):
                    groups.append(("sigT", O_GM + i * 512, sgmT_s, D_sgm, i * 512))
                for i in range(4):
                    groups.append(("sigT", O_GA + i * 512, sgaT_s, D_sga, i * 512))
            import os
            _kf = os.environ.get("KGROUPS")
            if _kf:
                groups = [g for g in groups if g[0] in _kf.split(",")]
            for g in groups:
                kind, c0 = g[0], g[1]
                w = load_w(wt, w_in, 0, KC, c0, 512)
                if g is groups[-1] and blk + 1 < NBLK:
                    for t_ in range(2):
                        xs_ = xt.next()
                        op("sp", "dma_start", writes=[xs_.b], dma=xs_.d, out=xs_.t, in_=xs[tok0 + TB + t_ * 128:tok0 + TB + (t_ + 1) * 128, :])
                        pre_x.append(xs_)
                if kind in ("v", "sig", "rope"):
                    gates = (kind == "v" and g[2] is mv_s and g[4] == 0)
                    if gates:
                        wgs = load_w(wg, w_in, 0, KC, O_MI, 8)
                    if kind == "rope":
                        qs = qk_st.next()
                    for t in range(TB // 128):
                        gt = blk * (TB // 128) + t
                        a = acc.next()
                        for k in range(KC):
                            op("pe", "matmul", reads=[hT.b, w.b], writes=[a.b], out=a.t, lhsT=hT.t[:, k, t * 128:(t + 1) * 128],
                               rhs=w.t[:, k, :], start=(k == 0), stop=(k == KC - 1))
                        if gates:
                            a2 = acc.next()
                            for k in range(KC):
                                op("pe", "matmul", reads=[hT.b, wgs.b], writes=[a2.b], out=a2.t[:, 0:8], lhsT=hT.t[:, k, t * 128:(t + 1) * 128],
                                   rhs=wgs.t[:, k, :], start=(k == 0), stop=(k == KC - 1))
                            op("dve", "tensor_tensor", reads=[a2.b, cs.b], writes=[G.b], out=G.t[:, gt, :], in0=a2.t[:, 0:8],
                               in1=CS[:, C_BIF:C_BIF + 8], op=ALU.add)
                        flush_pending()
                        if kind == "v":
                            s = st.next()
                            op("dve", "tensor_copy", reads=[a.b], writes=[s.b], out=s.t, in_=a.t)
                            op("sp", "dma_start", reads=[s.b], writes=[g[3]], dma=s.d,
                               out=g[2][tok0 + t * 128:tok0 + (t + 1) * 128, g[4]:g[4] + 512], in_=s.t)
                        elif kind == "sig":
                            s = st.next()
                            op("act", "activation", reads=[a.b], writes=[s.b], out=s.t, in_=a.t, func=AF.Sigmoid)
                            op("sp", "dma_start", reads=[s.b], writes=[g[3]], dma=s.d,
                               out=g[2][otok0 + t * 128:otok0 + (t + 1) * 128, g[4]:g[4] + 512], in_=s.t)
                        else:
                            r = rf.next(); tm = rtmp.next(); rbb = rb.next()
                            op("dve", "tensor_copy", reads=[a.b], writes=[r.b], out=r.t, in_=a.t)
                            x1 = mk(r.t, 0, [[128, 4], [1, 16]])
                            x2 = mk(r.t, 16, [[128, 4], [1, 16]])
                            cosv, sinv = tabv(1, gt, 4), tabv(0, gt, 4)
                            rr = dict(reads=[r.b, sincos.b], writes=[tm.b])
                            op("dve", "tensor_tensor", out=tm.t[:, 0], in0=x1, in1=cosv, op=ALU.mult, **rr)
                            op("dve", "tensor_tensor", out=tm.t[:, 1], in0=x2, in1=sinv, op=ALU.mult, **rr)
                            op("dve", "tensor_tensor", out=tm.t[:, 2], in0=x2, in1=cosv, op=ALU.mult, **rr)
                            op("dve", "tensor_tensor", out=tm.t[:, 3], in0=x1, in1=sinv, op=ALU.mult, **rr)
                            op("dve", "tensor_tensor", reads=[tm.b], writes=[r.b], out=x1, in0=tm.t[:, 0], in1=tm.t[:, 1], op=ALU.subtract)
                            op("dve", "tensor_tensor", reads=[tm.b], writes=[r.b], out=x2, in0=tm.t[:, 2], in1=tm.t[:, 3], op=ALU.add)
                            op("act", "activation", reads=[r.b], writes=[rbb.b], out=rbb.t, in_=r.t, func=AF.Copy)

                            def do_tr(rbb=rbb, qs=qs, t=t):
                                tp = tps.next()
                                for hh in range(4):
                                    op("pe", "transpose", reads=[rbb.b, cbs.b], writes=[tp.b], out=tp.t[:, hh * 128:(hh + 1) * 128],
                                       in_=rbb.t[:, hh * 128:(hh + 1) * 128], identity=ident)
                                dst = mk(qs.t, t * 128, [[TB, 4], [1, 128]])
                                src = mk(tp.t, 0, [[128, 4], [1, 128]])
                                op("act", "activation", reads=[tp.b], writes=[qs.b], out=dst, in_=src, func=AF.Copy)
                            pending.append(do_tr)
                    if kind == "rope":
                        flush_pending()
                        hb, is_k = g[4], g[5]
                        t0_ = tok0 if is_k else otok0
                        if is_k:
                            b0 = blk * 4
                            kmv = mk(kmean.t, hb * NB + b0, [[NB, 4], [1, 4]])
                            src = mk(qs.t, 0, [[TB, 4], [256, 4], [1, 256]])
                            op("dve", "tensor_reduce", reads=[qs.b], writes=[kmean.b], out=kmv, in_=src, axis=AX.X, op=ALU.add)
                        dstd = g[2][hb * 128:(hb + 4) * 128, t0_:t0_ + TB].rearrange("(h p) t -> p h t", p=128)
                        op("sp", "dma_start", reads=[qs.b], writes=[g[3]], dma=qs.d, out=dstd, in_=qs.t)
                else:
                    for cc in range(4):
                        for th in range(TB // 512):
                            a = acc.next()
                            for k in range(KC):
                                op("pe", "matmul", reads=[hT.b, w.b], writes=[a.b], out=a.t, lhsT=w.t[:, k, cc * 128:(cc + 1) * 128],
                                   rhs=hT.t[:, k, th * 512:(th + 1) * 512], start=(k == 0), stop=(k == KC - 1))
                            if kind == "sigT":
                                s = st.next()
                                op("act", "activation", reads=[a.b], writes=[s.b], out=s.t, in_=a.t, func=AF.Sigmoid)
                                r0 = g[4] + cc * 128
                                op("sp", "dma_start", reads=[s.b], writes=[g[3]], dma=s.d,
                                   out=g[2][r0:r0 + 128, otok0 + th * 512:otok0 + (th + 1) * 512], in_=s.t)
                            else:
                                hc = g[2] + cc
                                z = zp.next(); y = cy.next(); sg = csg.next(); ob = cob.next()
                                op("dve", "tensor_copy", reads=[halo.b], writes=[z.b], out=z.t[:, 0:3], in_=halo.t[:, hc, :])
                                op("dve", "tensor_copy", reads=[a.b], writes=[z.b], out=z.t[:, 3:515], in_=a.t)
                                op("dve", "tensor_copy", reads=[z.b], writes=[halo.b], out=halo.t[:, hc, :], in_=z.t[:, 512:515])
                                cwc = lambda j, hc=hc: CS[:, C_CW + j * 8 + hc:C_CW + j * 8 + hc + 1]
                                op("dve", "tensor_scalar", reads=[z.b, cs.b], writes=[y.b], out=y.t, in0=z.t[:, 0:512], scalar1=cwc(0),
                                   scalar2=CS[:, C_CB + hc:C_CB + hc + 1], op0=ALU.mult, op1=ALU.add)
                                for j in range(1, 4):
                                    op("dve", "scalar_tensor_tensor", reads=[z.b, cs.b, y.b], writes=[y.b], out=y.t, in0=z.t[:, j:j + 512],
                                       scalar=cwc(j), in1=y.t, op0=ALU.mult, op1=ALU.add)
                                op("act", "activation", reads=[y.b], writes=[sg.b], out=sg.t, in_=y.t, func=AF.Sigmoid)
                                scl = 1.0 if hc < 4 else 128.0 ** -0.5
                                op("dve", "scalar_tensor_tensor", reads=[y.b, sg.b], writes=[ob.b], out=ob.t, in0=y.t, scalar=scl,
                                   in1=sg.t, op0=ALU.mult, op1=ALU.mult)
                                if hc < 4:
                                    if own:
                                        op("sp", "dma_start", reads=[ob.b], writes=[D_mqT], dma=ob.d,
                                           out=mqT_s[hc * 128:(hc + 1) * 128, otok0 + th * 512:otok0 + (th + 1) * 512], in_=ob.t)
                                else:
                                    op("sp", "dma_start", reads=[ob.b], writes=[D_mkT], dma=ob.d,
                                       out=mkT_s[(hc - 4) * 128:(hc - 3) * 128, tok0 + th * 512:tok0 + (th + 1) * 512], in_=ob.t)
        op("dve", "tensor_copy", reads=[kmean.b], writes=[kmean_b.b], out=kmean_b.t, in_=kmean.t)
    if debug == "A":
        return finish(P, nc, [D_mqT, D_mkT, D_mv, D_smo, D_aqT, D_akT, D_av, D_sgm, D_sga])

    with Phase(P):
        NG = NT * 4
        gl = P.slot("gl", [128, NG], F32)
        gb = P.slot("gb", [128, NG], F32)
        gw = P.slot("gw", [128, NG], F32)
        gw2 = P.slot("gw2", [128, NG], F32)
        get = P.slot("get", [128, NG], F32)
        geL = P.slot("geL", [128, NG], F32)
        Gi = mk(G.t, 0, [[8, NT], [1, 4]])
        Gf = mk(G.t, 4, [[8, NT], [1, 4]])
        v2 = lambda s_: mk(s_.t, 0, [[4, NT], [1, 4]])
        op("act", "activation", reads=[G.b], writes=[gl.b], out=v2(gl), in_=Gf, func=AF.Exp, scale=-1.0)
        op("act", "activation", reads=[gl.b], writes=[gl.b], out=gl.t, in_=gl.t, func=AF.Ln, bias=1.0)
        a = acc.next()
        op("pe", "matmul", reads=[gl.b, cs.b], writes=[a.b], out=a.t[:, 0:NG], lhsT=tri_f, rhs=gl.t, start=True, stop=True)
        op("dve", "tensor_scalar", reads=[a.b], writes=[gb.b], out=gb.t, in0=a.t[:, 0:NG], scalar1=-1.0, scalar2=None, op0=ALU.mult)
        a = acc.next()
        op("pe", "matmul", reads=[gl.b, cs.b], writes=[a.b], out=a.t[:, 0:NG], lhsT=ones_f, rhs=gl.t, start=True, stop=True)
        op("act", "activation", reads=[a.b], writes=[geL.b], out=geL.t, in_=a.t[:, 0:NG], func=AF.Exp, scale=-1.0)
        op("act", "activation", reads=[gb.b], writes=[get.b], out=get.t, in_=gb.t, func=AF.Exp)
        op("dve", "tensor_tensor", reads=[G.b, gb.b], writes=[gw.b], out=v2(gw), in0=Gi, in1=v2(gb), op=ALU.subtract)
        op("act", "activation", reads=[gw.b], writes=[gw.b], out=gw.t, in_=gw.t, func=AF.Exp)
        op("dve", "tensor_scalar", reads=[gw.b, cs.b], writes=[gw.b], out=gw.t[:, 0:NG // 2], in0=gw.t[:, 0:NG // 2],
           scalar1=CS[:, C_PVAL:C_PVAL + 1], scalar2=None, op0=ALU.mult)
        op("dve", "tensor_tensor", reads=[gw.b, geL.b], writes=[gw2.b], out=gw2.t, in0=gw.t, in1=geL.t, op=ALU.mult)

        Cf = P.slot("Cf", [128, 4, 258], F32)
        Cb = P.slot("Cb", [128, 4, 258], BF16)
        op("dve", "memset", writes=[Cf.b], ap=Cf.t, constant=0.0)
        op("dve", "memset", writes=[Cb.b], ap=Cb.t, constant=0.0)
        CH = 8
        kTr = P.ring("kTr", 2, [128, 4, CH * 128], BF16, dma=True)
        qTr = P.ring("qTr", 2, [128, 4, CH * 128], BF16, dma=True)
        Vr = P.ring("Vr", 2, [128, CH, 4, 258], BF16, dma=True)
        smr = P.ring("smr", 2, [128, CH, 1024], BF16, dma=True)
        hst = P.ring("hst", 2, [128, 8, CH * 128], BF16, dma=True)
        for s_ in Vr.slots:
            op("dve", "memset", writes=[s_.b], ap=s_.t, constant=1.0)
        ktok = P.ring("ktok", 3, [128, 128], BF16)
        V1 = P.ring("V1", 3, [128, 258], BF16)
        V2 = P.ring("V2", 3, [128, 258], BF16)
        Smr = P.ring("Sm", 3, [128, 128], BF16)
        hmb = P.ring("hmb", 10, [128, 256], BF16)
        rawsb = P.ring("rawsb", 3, [128, 4, 258], F32)
        jk = P.ring("jk", 2, [128, 256], BF16)
        sc8 = P.ring("sc8", 4, [128, 32], F32)
        pendB = []

        def flushB(keep=0):
            while len(pendB) > keep:
                pendB.pop(0)()

        for cg in range(NT // CH):
            ownc = cg * CH >= NT // 2
            t0 = cg * CH * 128
            ot0 = t0 - T
            kT = kTr.next(); Vt = Vr.next()
            op("sp", "dma_start", reads=[D_mkT], writes=[kT.b], dma=kT.d, out=kT.t,
               in_=mkT_s[:, t0:t0 + CH * 128].rearrange("(h p) t -> p h t", p=128))
            for n_ in range(CH):
                op("sp", "dma_start", reads=[D_mv], writes=[Vt.b], dma=Vt.d, out=Vt.t[:, n_, :, 0:256],
                   in_=mv_s[t0 + n_ * 128:t0 + (n_ + 1) * 128, :].rearrange("p (h c) -> p h c", h=4))
            if ownc:
                qT = qTr.next(); smo = smr.next(); hs = hst.next()
                op("sp", "dma_start", reads=[D_mqT], writes=[qT.b], dma=qT.d, out=qT.t,
                   in_=mqT_s[:, ot0:ot0 + CH * 128].rearrange("(h p) t -> p h t", p=128))
                for n_ in range(CH):
                    op("sp", "dma_start", reads=[D_smo], writes=[smo.b], dma=smo.d, out=smo.t[:, n_, :],
                       in_=smo_s[ot0 + n_ * 128:ot0 + (n_ + 1) * 128, :])
            for cl in range(CH):
                c = cg * CH + cl
                if ownc:
                    rwc = rawsb.next(); s8c = sc8.next()
                for h in range(4):
                    col = c * 4 + h
                    flushB(4)
                    gcol = lambda s_, col=col: s_.t[:, col:col + 1]
                    ksl = kT.t[:, h, cl * 128:(cl + 1) * 128]
                    tp = tps.next()
                    op("pe", "transpose", reads=[kT.b, cbs.b], writes=[tp.b], out=tp.t[:, 0:128], in_=ksl, identity=ident)
                    kk = ktok.next()
                    op("act", "activation", reads=[tp.b], writes=[kk.b], out=kk.t, in_=tp.t[:, 0:128], func=AF.Copy)
                    vv2 = V2.next()
                    op("dve", "tensor_scalar", reads=[Vt.b, gw2.b], writes=[vv2.b], out=vv2.t[:, 0:257], in0=Vt.t[:, cl, h, 0:257],
                       scalar1=gcol(gw2), scalar2=None, op0=ALU.mult)
                    if ownc:
                        qsl = qT.t[:, h, cl * 128:(cl + 1) * 128]
                        vv1 = V1.next()
                        op("dve", "tensor_scalar", reads=[Vt.b, gw.b], writes=[vv1.b], out=vv1.t[:, 0:257], in0=Vt.t[:, cl, h, 0:257],
                           scalar1=gcol(gw), scalar2=None, op0=ALU.mult)
                        a = acc.next()
                        op("pe", "matmul", reads=[kT.b, qT.b], writes=[a.b], out=a.t[:, 0:128], lhsT=ksl, rhs=qsl, start=True, stop=True)
                        sm_ = Smr.next()
                        op("dve", "tensor_tensor", reads=[a.b, cbs.b], writes=[sm_.b], out=sm_.t, in0=a.t[:, 0:128], in1=tri_bf, op=ALU.mult)
                        a2 = acc.next()
                        op("pe", "matmul", reads=[qT.b, Cb.b], writes=[a2.b], out=a2.t[:, 0:257], lhsT=qsl, rhs=Cb.t[:, h, 0:257], start=True, stop=False)
                        op("pe", "matmul", reads=[sm_.b, vv1.b], writes=[a2.b], out=a2.t[:, 0:257], lhsT=sm_.t, rhs=vv1.t[:, 0:257], start=False, stop=True)
                        pass
                    a3 = acc.next()
                    op("pe", "matmul", reads=[kk.b, vv2.b], writes=[a3.b], out=a3.t[:, 0:257], lhsT=kk.t, rhs=vv2.t[:, 0:257], start=True, stop=True)
                    op("dve", "scalar_tensor_tensor", reads=[Cf.b, geL.b, a3.b], writes=[Cf.b], out=Cf.t[:, h, 0:257], in0=Cf.t[:, h, 0:257],
                       scalar=gcol(geL), in1=a3.t[:, 0:257], op0=ALU.mult, op1=ALU.add)
                    op("act", "activation", reads=[Cf.b], writes=[Cb.b], out=Cb.t[:, h, 0:257], in_=Cf.t[:, h, 0:257], func=AF.Copy)
                    if ownc:
                        op("dve", "tensor_copy", reads=[a2.b], writes=[rwc.b], out=rwc.t[:, h, 0:257], in_=a2.t[:, 0:257])
                        j_ = jk.next()
                        op("act", "activation", reads=[rwc.b], writes=[j_.b, s8c.b], out=j_.t, in_=rwc.t[:, h, 0:256], func=AF.Square,
                           accum_out=s8c.t[:, h:h + 1])
                if ownc:
                    R = lambda r: s8c.t[:, r * 4:(r + 1) * 4]
                    den4 = mk(rwc.t, 256, [[258, 4]])
                    get4 = get.t[:, c * 4:(c + 1) * 4]
                    rw8 = dict(reads=[s8c.b], writes=[s8c.b])
                    op("dve", "tensor_tensor", reads=[rwc.b, get.b], writes=[s8c.b], out=R(1), in0=den4, in1=get4, op=ALU.mult)
                    op("dve", "tensor_scalar", out=R(2), in0=R(1), scalar1=-1.0, scalar2=None, op0=ALU.mult, **rw8)
                    op("dve", "tensor_tensor", out=R(1), in0=R(1), in1=R(2), op=ALU.max, **rw8)
                    op("dve", "tensor_scalar", out=R(1), in0=R(1), scalar1=1.0, scalar2=None, op0=ALU.max, **rw8)
                    op("dve", "reciprocal", out=R(2), in_=R(1), **rw8)
                    op("dve", "tensor_tensor", reads=[s8c.b, get.b], writes=[s8c.b], out=R(3), in0=R(2), in1=get4, op=ALU.mult)
                    op("dve", "tensor_tensor", out=R(4), in0=R(0), in1=R(3), op=ALU.mult, **rw8)
                    op("dve", "tensor_tensor", out=R(4), in0=R(4), in1=R(3), op=ALU.mult, **rw8)
                    op("dve", "tensor_scalar", out=R(4), in0=R(4), scalar1=1.0 / 256, scalar2=EPS, op0=ALU.mult, op1=ALU.add, **rw8)
                    op("act", "activation", out=R(5), in_=R(4), func=AF.Sqrt, **rw8)
                    op("dve", "reciprocal", out=R(6), in_=R(5), **rw8)
                    op("dve", "tensor_tensor", out=R(7), in0=R(6), in1=R(3), op=ALU.mult, **rw8)
                    for h in range(4):
                        hb_ = hmb.next()
                        op("dve", "scalar_tensor_tensor", reads=[rwc.b, s8c.b, smo.b], writes=[hb_.b], out=hb_.t, in0=rwc.t[:, h, 0:256],
                           scalar=s8c.t[:, 28 + h:29 + h], in1=smo.t[:, cl, h * 256:(h + 1) * 256], op0=ALU.mult, op1=ALU.mult)

                        def _tr(hb_=hb_, hs=hs, h=h, cl=cl):
                            tp2 = tps.next()
                            for j in range(2):
                                op("pe", "transpose", reads=[hb_.b, cbs.b], writes=[tp2.b], out=tp2.t[:, j * 128:(j + 1) * 128],
                                   in_=hb_.t[:, j * 128:(j + 1) * 128], identity=ident)
                            for j in range(2):
                                fc = h * 2 + j
                                op("dve", "tensor_tensor", reads=[tp2.b, cs.b], writes=[hs.b], out=hs.t[:, fc, cl * 128:(cl + 1) * 128],
                                   in0=tp2.t[:, j * 128:(j + 1) * 128], in1=mk(CS, C_GMO + fc, [[0, 128]]), op=ALU.mult)
                        pendB.append(_tr)
            flushB()
            if ownc:
                for fc in range(8):
                    op("sp", "dma_start", reads=[hs.b], writes=[D_hmT], dma=hs.d,
                       out=hmT_s[fc * 128:(fc + 1) * 128, ot0:ot0 + CH * 128], in_=hs.t[:, fc, :])
    if debug == "B":
        return finish(P, nc, [D_hmT])

    with Phase(P):
        ccs = P.slot("cstc", [128, CC_TOT], BF16, dma=True)
        op("pool", "dma_start", writes=[ccs.b], dma=ccs.d, out=ccs.t, in_=cstc[:, :])
        CM = lambda kt: ccs.t[:, CC_CM + kt * 256:CC_CM + (kt + 1) * 256]
        CMQ = lambda hf: ccs.t[:, CC_CMQ + hf * 256:CC_CMQ + (hf + 1) * 256]
        ESEL = lambda n: ccs.t[:, CC_ESEL + n * 128:CC_ESEL + (n + 1) * 128]
        kTh = P.ring("kTh", 2, [128, S2], BF16, dma=True)
        qTh = P.ring("qTh", 2, [128, T], BF16, dma=True)
        Vh = P.ring("Vh", 2, [128, NT, 130], BF16, dma=True)
        for s_ in Vh.slots:
            op("dve", "memset", writes=[s_.b], ap=s_.t, constant=1.0)
        hast = P.ring("hast", 2, [128, T], BF16, dma=True)
        Bsb = P.ring("Bsb", 2, [128, 256], BF16)
        for s_ in Bsb.slots:
            op("dve", "memset", writes=[s_.b], ap=s_.t, constant=0.0)
        Bq = P.ring("Bq", 4, [128, 64], BF16)
        for s_ in Bq.slots:
            op("dve", "memset", writes=[s_.b], ap=s_.t, constant=0.0)
        scm = P.ring("scm", 4, [128, 32], F32)
        som = P.ring("som", 4, [128, 256], F32)
        c8 = P.ring("c8", 8, [128, 16], F32)
        nmb = P.ring("nmb", 8, [128, 2], BF16)
        Pt = P.ring("Pt", 3, [128, 512], BF16)
        hq = P.ring("hq", 2, [128, 128], BF16)
        oacc = [aux[0], aux[1]]
        SCALE = 128.0 ** -0.5
        heads = {}

        def load_head(h):
            kT = kTh.next(); qT = qTh.next(); Vt = Vh.next(); ha = hast.next()
            op("sp", "dma_start", reads=[D_akT], writes=[kT.b], dma=kT.d, out=kT.t, in_=akT_s[h * 128:(h + 1) * 128, :])
            op("sp", "dma_start", reads=[D_aqT], writes=[qT.b], dma=qT.d, out=qT.t, in_=aqT_s[h * 128:(h + 1) * 128, :])
            for n0 in range(0, NT, 4):
                op("sp", "dma_start", reads=[D_av], writes=[Vt.b], dma=Vt.d, out=Vt.t[:, n0:n0 + 4, 0:128],
                   in_=av_s[n0 * 128:(n0 + 4) * 128, h * 128:(h + 1) * 128].rearrange("(n p) c -> p n c", p=128))
            heads[h] = (kT, qT, Vt, ha)

        def prologue(h, j):
            kT, qT, Vt, ha = heads[h]
            q0 = j * 256
            Bs = Bsb.next()
            bqs = []
            for hf in range(2):
                qsl = qT.t[:, q0 + hf * 128:q0 + (hf + 1) * 128]
                a = acc.next()
                op("pe", "matmul", reads=[qT.b, kmean_b.b], writes=[a.b], out=a.t[:, 0:NB], lhsT=qsl, rhs=kmean_b.t[:, h, :], start=True, stop=True)
                a2 = acc.next()
                okeys = kT.t[:, (NBP + j) * 256:(NBP + j + 1) * 256]
                op("pe", "matmul", reads=[qT.b, kT.b], writes=[a2.b], out=a2.t[:, 0:256], lhsT=qsl, rhs=okeys, start=True, stop=True)
                sc = scm.next(); cc8 = c8.next(); nm = nmb.next(); bq = Bq.next()
                op("dve", "tensor_tensor", reads=[a.b, cs.b], writes=[sc.b], out=sc.t[:, 0:NB], in0=a.t[:, 0:NB],
                   in1=CS[:, C_PMASK + j * NB:C_PMASK + (j + 1) * NB], op=ALU.add)
                op("dve", "max", reads=[sc.b], writes=[cc8.b], out=cc8.t[:, 0:8], in_=sc.t[:, 0:NB])
                op("dve", "tensor_scalar", reads=[cc8.b], writes=[cc8.b], out=cc8.t[:, 8:9], in0=cc8.t[:, 2:3], scalar1=-1e29, scalar2=None, op0=ALU.max)
                op("dve", "tensor_scalar", reads=[sc.b, cc8.b], writes=[sc.b], out=sc.t[:, 0:NB], in0=sc.t[:, 0:NB], scalar1=cc8.t[:, 8:9],
                   scalar2=None, op0=ALU.is_ge)
                so = som.next()
                op("dve", "tensor_tensor", reads=[a2.b, ccs.b], writes=[so.b], out=so.t, in0=a2.t[:, 0:256], in1=CMQ(hf), op=ALU.add)
                op("dve", "tensor_reduce", reads=[so.b], writes=[cc8.b], out=cc8.t[:, 9:10], in_=so.t, axis=AX.X, op=ALU.max)
                op("dve", "tensor_scalar", reads=[cc8.b], writes=[nm.b], out=nm.t[:, 0:1], in0=cc8.t[:, 9:10], scalar1=-1.0, scalar2=None, op0=ALU.mult)
                op("dve", "tensor_scalar", reads=[nm.b], writes=[cc8.b], out=cc8.t[:, 10:11], in0=nm.t[:, 0:1], scalar1=NEGBIG, scalar2=None, op0=ALU.add)
                op("dve", "tensor_scalar", reads=[sc.b, cc8.b], writes=[bq.b], out=bq.t[:, 0:NB], in0=sc.t[:, 0:NB], scalar1=-NEGBIG,
                   scalar2=cc8.t[:, 10:11], op0=ALU.mult, op1=ALU.add)
                op("dve", "tensor_copy", reads=[nm.b], writes=[bq.b], out=bq.t[:, 32:33], in_=nm.t[:, 0:1])
                bqs.append(bq)

            def part2(bqs=bqs, Bs=Bs):
                for hf, bq in enumerate(bqs):
                    tp = tps.next()
                    op("pe", "transpose", reads=[bq.b, cbs.b], writes=[tp.b], out=tp.t[0:64, 0:128], in_=bq.t, identity=ident)
                    op("act", "activation", reads=[tp.b], writes=[Bs.b], out=Bs.t[0:64, hf * 128:(hf + 1) * 128], in_=tp.t[0:64, 0:128], func=AF.Copy)
            return Bs, part2

        seq = [(h, j) for h in range(8) for j in range(NBO)]
        load_head(0)
        Bs_cur, p2 = prologue(0, 0)
        p2()
        for idx, (h, j) in enumerate(seq):
            if j == 0 and h + 1 < 8:
                load_head(h + 1)
            kT, qT, Vt, ha = heads[h]
            q0 = j * 256
            Bs = Bs_cur
            nblk = NBP + j + 1
            o0, o1 = oacc[0], oacc[1]
            qs2 = qT.t[:, q0:q0 + 256]
            nxt = seq[idx + 1] if idx + 1 < len(seq) else None
            nxt_state = {}

            def emit_S(n, kT=kT, qT=qT, Bs=Bs, nblk=nblk, qs2=qs2):
                is_own = n == nblk - 1
                a = acc.next()
                for kt in range(2):
                    kti = n * 2 + kt
                    o_ = a.t[:, kt * 256:(kt + 1) * 256]
                    op("pe", "matmul", reads=[kT.b, qT.b], writes=[a.b], out=o_, lhsT=kT.t[:, kti * 128:(kti + 1) * 128], rhs=qs2, start=True, stop=False)
                    if is_own:
                        op("pe", "matmul", reads=[cbs.b, ccs.b], writes=[a.b], out=o_, lhsT=ident, rhs=CM(kt), start=False, stop=False)
                    op("pe", "matmul", reads=[ccs.b, Bs.b], writes=[a.b], out=o_, lhsT=ESEL(32 if is_own else n), rhs=Bs.t, start=False, stop=True)
                pt = Pt.next()
                op("act", "activation", reads=[a.b], writes=[pt.b], out=pt.t, in_=a.t, func=AF.Exp, scale=SCALE)
                return pt

            def emit_PV(n, pt, Vt=Vt, nblk=nblk, o0=o0, o1=o1):
                is_own = n == nblk - 1
                for kt in range(2):
                    kti = n * 2 + kt
                    first = (n == 0 and kt == 0)
                    last = (is_own and kt == 1)
                    for hf, o in ((0, o0), (1, o1)):
                        op("pe", "matmul", reads=[pt.b, Vt.b], writes=[o.b], out=o.t[:, 0:129], lhsT=pt.t[:, kt * 256 + hf * 128:kt * 256 + (hf + 1) * 128],
                           rhs=Vt.t[:, kti, 0:129], start=first, stop=last)

            emit_conv_job()
            prev = None
            for n in range(nblk):
                pt = emit_S(n)
                if n == 1 and nxt is not None:
                    Bs_cur, nxt_state["p2"] = prologue(*nxt)
                if n == min(6, nblk - 1) and "p2" in nxt_state:
                    nxt_state.pop("p2")()
                if prev is not None:
                    emit_PV(*prev)
                prev = (n, pt)
            emit_PV(*prev)
            if "p2" in nxt_state:
                nxt_state.pop("p2")()
            for hf, o in ((0, o0), (1, o1)):
                cc8 = c8.next(); hq_ = hq.next()
                op("dve", "reciprocal", reads=[o.b], writes=[cc8.b], out=cc8.t[:, 0:1], in_=o.t[:, 128:129])
                op("dve", "tensor_scalar", reads=[o.b, cc8.b], writes=[hq_.b], out=hq_.t, in0=o.t[:, 0:128], scalar1=cc8.t[:, 0:1], scalar2=None, op0=ALU.mult)
                tp = tps.next()
                op("pe", "transpose", reads=[hq_.b, cbs.b], writes=[tp.b], out=tp.t[:, 0:128], in_=hq_.t, identity=ident)
                op("act", "activation", reads=[tp.b], writes=[ha.b], out=ha.t[:, q0 + hf * 128:q0 + (hf + 1) * 128], in_=tp.t[:, 0:128], func=AF.Copy)
            if j == NBO - 1:
                op("sp", "dma_start", reads=[ha.b], writes=[D_haT], dma=ha.d, out=haT_s[h * 128:(h + 1) * 128, :], in_=ha.t)
    while emit_conv_job():
        pass
    if debug == "C":
        return finish(P, nc, [D_haT, D_hmT])

    with Phase(P):
        TD = 512
        NTD = TD // 128
        x2 = [P.slot("x2", [128, D], F32, dma=True) for _ in range(NTD)]
        hT = P.slot("hTd", [128, KC, TD], BF16)
        wt = P.ring("wtd", 3, [128, KC, 512], BF16, dma=True)
        sgr = P.ring("sgr", 4, [128, 512], BF16, dma=True)
        tmpf = P.ring("tmpf", 3, [128, 512], F32)
        rlu = P.ring("rlu", 2, [128, 512], BF16)
        xn = P.ring("xnd", 2, [128, D], BF16)
        gf = P.slot("gfin", [128, D], F32, dma=True)
        pT = P.slot("pT", [128, 2, TD], BF16)
        pin = P.ring("pin", 2, [128, 256], F32, dma=True)
        pbf = P.ring("pbf", 2, [128, 256], BF16)
        op("pool", "dma_start", writes=[gf.b], dma=gf.d, out=gf.t, in_=gfin[:, :])
        for blk in range(T // TD):
            o0 = blk * TD
            for t in range(NTD):
                op("pool", "dma_start", writes=[x2[t].b], dma=x2[t].d, out=x2[t].t, in_=xs[T + o0 + t * 128:T + o0 + (t + 1) * 128, :])
            sub1 = Phase(P)
            sub1.__enter__()
            mT = P.slot("mT", [128, KC, TD], BF16)
            hmT = P.slot("hmTd", [128, 8, TD], BF16, dma=True)
            haT = P.slot("haTd", [128, 8, TD], BF16, dma=True)
            for c0_ in (0, 4):
                op("pool", "dma_start", reads=[D_hmT], writes=[hmT.b], dma=hmT.d, out=hmT.t[:, c0_:c0_ + 4, :],
                   in_=hmT_s[c0_ * 128:(c0_ + 4) * 128, o0:o0 + TD].rearrange("(c p) t -> p c t", p=128))
                op("pool", "dma_start", reads=[D_haT], writes=[haT.b], dma=haT.d, out=haT.t[:, c0_:c0_ + 4, :],
                   in_=haT_s[c0_ * 128:(c0_ + 4) * 128, o0:o0 + TD].rearrange("(c p) t -> p c t", p=128))
            for wgi in range(4):
                wm = load_w(wt, wb["up_m"], 0, 8, wgi * 512, 512, dep=D_wb["up_m"], q="sp")
                wa = load_w(wt, wb["up_a"], 0, 8, wgi * 512, 512, dep=D_wb["up_a"], q="sp")
                for cc in range(4):
                    fc = wgi * 4 + cc
                    am = acc.next()
                    for k in range(8):
                        op("pe", "matmul", reads=[wm.b, hmT.b], writes=[am.b], out=am.t, lhsT=wm.t[:, k, cc * 128:(cc + 1) * 128], rhs=hmT.t[:, k, :], start=(k == 0), stop=(k == 7))
                    aa = acc.next()
                    for k in range(8):
                        op("pe", "matmul", reads=[wa.b, haT.b], writes=[aa.b], out=aa.t, lhsT=wa.t[:, k, cc * 128:(cc + 1) * 128], rhs=haT.t[:, k, :], start=(k == 0), stop=(k == 7))
                    s1 = sgr.next(); s2 = sgr.next(); t1 = tmpf.next(); t2 = tmpf.next()
                    op("pool", "dma_start", reads=[D_sgm], writes=[s1.b], dma=s1.d, out=s1.t, in_=sgmT_s[fc * 128:(fc + 1) * 128, o0:o0 + TD])
                    op("pool", "dma_start", reads=[D_sga], writes=[s2.b], dma=s2.d, out=s2.t, in_=sgaT_s[fc * 128:(fc + 1) * 128, o0:o0 + TD])
                    op("dve", "tensor_tensor", reads=[am.b, s1.b], writes=[t1.b], out=t1.t, in0=am.t, in1=s1.t, op=ALU.mult)
                    op("dve", "tensor_tensor", reads=[aa.b, s2.b], writes=[t2.b], out=t2.t, in0=aa.t, in1=s2.t, op=ALU.mult)
                    op("dve", "tensor_tensor", reads=[t1.b, t2.b], writes=[mT.b], out=mT.t[:, fc, :], in0=t1.t, in1=t2.t, op=ALU.add)
            for cgi in range(4):
                w = load_w(wt, wb["out"], 0, KC, cgi * 512, 512, dep=D_wb["out"], q="sp")
                for t in range(NTD):
                    a = acc.next()
                    for k in range(KC):
                        op("pe", "matmul", reads=[mT.b, w.b], writes=[a.b], out=a.t, lhsT=mT.t[:, k, t * 128:(t + 1) * 128], rhs=w.t[:, k, :], start=(k == 0), stop=(k == KC - 1))
                    xs_ = x2[t].t[:, cgi * 512:(cgi + 1) * 512]
                    op("dve", "tensor_tensor", reads=[a.b, x2[t].b], writes=[x2[t].b], out=xs_, in0=a.t, in1=xs_, op=ALU.add)
            sub1.__exit__(None, None, None)
            norm_transpose([(x2[t].t, x2[t].b) for t in range(NTD)], C_GMLP, hT, xn, 2)
            sub2 = Phase(P)
            sub2.__enter__()
            uT = P.slot("uT", [128, KC, TD], BF16)
            for qf in range(4):
                for wgi in range(4):
                    w1 = load_w(wt, wb["ff1"], 0, KC, qf * 2048 + wgi * 512, 512, dep=D_wb["ff1"], q="sp")
                    for cc in range(4):
                        fcl = wgi * 4 + cc
                        a = acc.next()
                        for k in range(KC):
                            op("pe", "matmul", reads=[w1.b, hT.b], writes=[a.b], out=a.t, lhsT=w1.t[:, k, cc * 128:(cc + 1) * 128], rhs=hT.t[:, k, :], start=(k == 0), stop=(k == KC - 1))
                        r_ = rlu.next()
                        op("act", "activation", reads=[a.b], writes=[r_.b], out=r_.t, in_=a.t, func=AF.Relu)
                        op("act", "activation", reads=[r_.b], writes=[uT.b], out=uT.t[:, fcl, :], in_=r_.t, func=AF.Square)
                for cgi in range(4):
                    w2 = load_w(wt, wb["ff2"], qf * 2048, KC, cgi * 512, 512, dep=D_wb["ff2"], q="sp")
                    for t in range(NTD):
                        a = acc.next()
                        for k in range(KC):
                            op("pe", "matmul", reads=[uT.b, w2.b], writes=[a.b], out=a.t, lhsT=uT.t[:, k, t * 128:(t + 1) * 128], rhs=w2.t[:, k, :], start=(k == 0), stop=(k == KC - 1))
                        xs_ = x2[t].t[:, cgi * 512:(cgi + 1) * 512]
                        op("dve", "tensor_tensor", reads=[a.b, x2[t].b], writes=[x2[t].b], out=xs_, in0=a.t, in1=xs_, op=ALU.add)
            sub2.__exit__(None, None, None)
            norm_transpose([(x2[t].t, x2[t].b) for t in range(NTD)], C_GPLE, hT, xn, 2)
            for t in range(NTD):
                pi = pin.next(); pb = pbf.next()
                op("pool", "dma_start", writes=[pi.b], dma=pi.d, out=pi.t, in_=pp_in[o0 + t * 128:o0 + (t + 1) * 128, :])
                op("act", "activation", reads=[pi.b], writes=[pb.b], out=pb.t, in_=pi.t, func=AF.Copy)
                tp = tps.next()
                for k in range(2):
                    op("pe", "transpose", reads=[pb.b, cbs.b], writes=[tp.b], out=tp.t[:, k * 128:(k + 1) * 128], in_=pb.t[:, k * 128:(k + 1) * 128], identity=ident)
                op("dve", "tensor_copy", reads=[tp.b], writes=[pT.b], out=mk(pT.t, t * 128, [[TD, 2], [1, 128]]), in_=mk(tp.t, 0, [[128, 2], [1, 128]]))
            for cgi in range(4):
                wgt = load_w(wt, wb["pg"], 0, KC, cgi * 512, 512, dep=D_wb["pg"], q="sp")
                wpp = load_w(wt, wb["pp"], 0, 2, cgi * 512, 512, dep=D_wb["pp"], q="sp")
                for t in range(NTD):
                    ag = acc.next()
                    for k in range(KC):
                        op("pe", "matmul", reads=[hT.b, wgt.b], writes=[ag.b], out=ag.t, lhsT=hT.t[:, k, t * 128:(t + 1) * 128], rhs=wgt.t[:, k, :], start=(k == 0), stop=(k == KC - 1))
                    ap_ = acc.next()
                    for k in range(2):
                        op("pe", "matmul", reads=[pT.b, wpp.b], writes=[ap_.b], out=ap_.t, lhsT=pT.t[:, k, t * 128:(t + 1) * 128], rhs=wpp.t[:, k, :], start=(k == 0), stop=(k == 1))
                    t1 = tmpf.next(); t2 = tmpf.next()
                    op("act", "activation", reads=[ag.b], writes=[t1.b], out=t1.t, in_=ag.t, func=AF.Sigmoid)
                    op("dve", "tensor_tensor", reads=[ap_.b, t1.b], writes=[t2.b], out=t2.t, in0=ap_.t, in1=t1.t, op=ALU.mult)
                    xs_ = x2[t].t[:, cgi * 512:(cgi + 1) * 512]
                    op("dve", "tensor_tensor", reads=[t2.b, x2[t].b], writes=[x2[t].b], out=xs_, in0=t2.t, in1=xs_, op=ALU.add)
            for t in range(NTD):
                n_ = xn.next()
                s = norm_stats(x2[t].t, x2[t].b, D, n_.t, n_.b)
                op("dve", "scalar_tensor_tensor", reads=[x2[t].b, s.b, gf.b], writes=[x2[t].b], out=x2[t].t, in0=x2[t].t, scalar=s.t[:, 3:4],
                   in1=gf.t, op0=ALU.mult, op1=ALU.mult)
                op("pool", "dma_start", reads=[x2[t].b], writes=[D_out], dma=x2[t].d, out=out[o0 + t * 128:o0 + (t + 1) * 128, :], in_=x2[t].t)
    return finish(P, nc, [D_out])


def finish(P, nc, dbufs):
    P.wait_all("sp", dbufs)
    stats = P.finalize()
    stats["arena_peak_bytes"] = P.apeak * 2
    P.stack.close()
    nc._stats = stats
    return nc


def host_inputs(T, x_b, p_b, pos_b, half, prm):
    S2 = 2 * T
    NT = S2 // 128
    NB = S2 // 256
    NBP = NB // 2
    NBO = NB - NBP
    if half == 1:
        xs = np.ascontiguousarray(x_b)
        ps = pos_b
    else:
        xs = np.concatenate([np.zeros((T, D), np.float32), x_b[:T]], axis=0)
        ps = np.concatenate([np.zeros((T,), np.int32), pos_b[:T]])
    p_own = np.ascontiguousarray(p_b[half * T:(half + 1) * T])
    NCST = C_PMASK + NBO * NB
    cst = np.zeros((128, NCST), np.float32)
    cst[:, C_GAT:C_GAT + 16] = prm["attn_norm"].reshape(16, 128).T
    cst[:, C_GMLP:C_GMLP + 16] = prm["mlp_norm"].reshape(16, 128).T
    cst[:, C_GPLE:C_GPLE + 16] = prm["ple_norm"].reshape(16, 128).T
    cst[:, C_GMO:C_GMO + 8] = prm["m_out_norm"].reshape(8, 128).T
    cw = prm["conv_w"].reshape(4, 8, 128)
    cst[:, C_CW:C_CW + 32] = cw.transpose(2, 0, 1).reshape(128, 32)
    cst[:, C_CB:C_CB + 8] = prm["conv_b"].reshape(8, 128).T
    cst[:, C_BIF:C_BIF + 8] = np.broadcast_to(prm["b_if"].reshape(1, 8), (128, 8))
    half_ = 16
    invf = (np.float32(500000.0) ** (-np.arange(half_, dtype=np.float32) * np.float32(2.0) / np.float32(32))).astype(np.float32)
    cst[:, C_INVF:C_INVF + 16] = invf[None, :]
    cst[:, C_PVAL] = float(half)
    tri, cb, cc = host_consts(T)
    cst[:, C_TRI:C_TRI + 128] = tri
    cst[:, C_ONES:C_ONES + 128] = 1.0
    pm = np.zeros((NBO, NB), np.float32)
    for j in range(NBO):
        pm[j, NBP + j:] = -1e30
        if half == 0:
            pm[j, :NBP] = -1e30
    cst[:, C_PMASK:] = pm.reshape(1, -1)
    d = dict(xs=xs, p=p_own, pos=np.ascontiguousarray(ps.reshape(NT, 128).T.astype(np.int32)), cst=cst, cstb=cb, cstc=cc,
             gfin=np.ascontiguousarray(np.broadcast_to(prm["final_norm"].reshape(1, D), (128, D))).astype(np.float32))
    return d


_NC_CACHE = {}


def kernel(x, p, positions, attn_norm, w_in, b_if, conv_w, conv_b, m_out_norm, w_up_m, w_up_a,
           w_out, mlp_norm, w_ff1, w_ff2, ple_norm, w_ple_gate, w_ple_proj, final_norm):
    x = np.asarray(x, np.float32); p = np.asarray(p, np.float32); positions = np.asarray(positions, np.int32)
    B, S, _ = x.shape
    T = S // 2
    prm = dict(attn_norm=np.asarray(attn_norm, np.float32)[0], mlp_norm=np.asarray(mlp_norm, np.float32)[0],
               ple_norm=np.asarray(ple_norm, np.float32)[0], m_out_norm=np.asarray(m_out_norm, np.float32)[0],
               conv_w=np.asarray(conv_w, np.float32)[0], conv_b=np.asarray(conv_b, np.float32)[0],
               b_if=np.asarray(b_if, np.float32)[0], final_norm=np.asarray(final_norm, np.float32))
    wts = dict(w_in=np.ascontiguousarray(np.asarray(w_in, np.float32)[0]), w_up_m=np.ascontiguousarray(np.asarray(w_up_m, np.float32)[0]),
               w_up_a=np.ascontiguousarray(np.asarray(w_up_a, np.float32)[0]), w_out=np.ascontiguousarray(np.asarray(w_out, np.float32)[0]),
               w_ff1=np.ascontiguousarray(np.asarray(w_ff1, np.float32)[0]), w_ff2=np.ascontiguousarray(np.asarray(w_ff2, np.float32)[0]),
               w_pg=np.ascontiguousarray(np.asarray(w_ple_gate, np.float32)[0]), w_pp=np.ascontiguousarray(np.asarray(w_ple_proj, np.float32)[0]))
    ncores = 2 * B
    if T not in _NC_CACHE:
        _NC_CACHE[T] = build(T)
    nc = _NC_CACHE[T]
    in_maps = []
    for c in range(ncores):
        b, half = c // 2, c % 2
        d = host_inputs(T, x[b], p[0, b], positions[b], half, prm)
        d.update(wts)
        in_maps.append(d)
    res = run_bass_kernel_spmd(nc, in_maps, core_ids=list(range(ncores)))
    outp = np.empty((B, S, D), np.float32)
    for c in range(ncores):
        b, half = c // 2, c % 2
        outp[b, half * T:(half + 1) * T] = res.results[c]["out"]
    return outp
```

```python
import contextlib
import math
import numpy as np
import concourse.bass as bass
import concourse.mybir as mybir
from concourse.bass_utils import run_bass_kernel_spmd

F32 = mybir.dt.float32
BF16 = mybir.dt.bfloat16
I32 = mybir.dt.int32
AF = mybir.ActivationFunctionType
ALU = mybir.AluOpType
AX = mybir.AxisListType

D = 2048
KC = 16
DFF = 8192
INW = 10248
EPS = 1e-6
NEGBIG = -30000.0
O_MQ, O_MK, O_MV, O_MO, O_MI, O_AQ, O_AK, O_AV, O_GM, O_GA = 0, 512, 1024, 2048, 3072, 3080, 4104, 5128, 6152, 8200


class Buf:
    __slots__ = ("name", "w", "r")

    def __init__(self, name=""):
        self.name = name
        self.w = None
        self.r = []


class DramBuf:
    __slots__ = ("name", "ws")

    def __init__(self, name=""):
        self.name = name
        self.ws = {}


class DmaSem:
    __slots__ = ("name", "count", "h")

    def __init__(self, name):
        self.name = name
        self.count = 0
        self.h = None


class Slot:
    __slots__ = ("t", "b", "d")

    def __init__(self, t, b, d):
        self.t, self.b, self.d = t, b, d


class Ring:
    def __init__(self, slots):
        self.slots = slots
        self.i = 0

    def next(self):
        s = self.slots[self.i % len(self.slots)]
        self.i += 1
        return s


def mk(base, off, dims):
    return bass.AP(base.tensor, base.offset + off, [[base.ap[0][0], 128]] + [list(d) for d in dims])


class Prog:
    ENGS = ("pe", "act", "dve", "pool", "sp")

    def __init__(self, nc):
        self.nc = nc
        self.ops = {e: [] for e in self.ENGS}
        self.serial = {e: 0 for e in self.ENGS}
        self.waited = {e: {} for e in self.ENGS}
        self.dsems = []
        self.stack = contextlib.ExitStack()
        self.nn = 0

    def init_arena(self, nbytes_per_part):
        self.arena_n = nbytes_per_part // 2
        self.arena = self.stack.enter_context(self.nc.sbuf_tensor("arena", [128, self.arena_n], BF16))
        self.aoff = 0
        self.apeak = 0

    def carve(self, shape, dt):
        n = 1
        for s_ in shape[1:]:
            n *= s_
        units = n if dt == BF16 else 2 * n
        self.aoff = (self.aoff + 15) // 16 * 16
        a = self.arena[:, self.aoff:self.aoff + units]
        self.aoff += units
        self.apeak = max(self.apeak, self.aoff)
        assert self.aoff <= self.arena_n, "SBUF arena overflow: %d > %d" % (self.aoff * 2, self.arena_n * 2)
        if dt != BF16:
            a = a.bitcast(dt)
        if len(shape) > 2:
            dims = []
            st_ = n
            for s_ in shape[1:]:
                st_ //= s_
                dims.append([st_, s_])
            a = bass.AP(a.tensor, a.offset, [[a.ap[0][0], 128]] + dims)
        return a

    def slot(self, name, shape, dt, dma=False):
        self.nn += 1
        nm = "%s_%d" % (name, self.nn)
        return Slot(self.carve(shape, dt), Buf(nm), self.dsem(nm) if dma else None)

    def ring(self, name, n, shape, dt, dma=False):
        return Ring([self.slot(name, shape, dt, dma) for _ in range(n)])

    def dsem(self, name):
        d = DmaSem(name)
        self.dsems.append(d)
        return d

    def _deps(self, reads, writes):
        deps = {}

        def add(tok):
            if tok is None:
                return
            k, v = tok
            if deps.get(k, 0) < v:
                deps[k] = v

        for b in reads:
            if isinstance(b, DramBuf):
                for k, v in b.ws.items():
                    add((k, v))
            else:
                add(b.w)
        for b in writes:
            if isinstance(b, DramBuf):
                continue
            add(b.w)
            for t in b.r:
                add(t)
        return deps

    def emit(self, eng, fn, reads=(), writes=(), dma=None):
        deps = self._deps(reads, writes)
        waits = []
        wd = self.waited[eng]
        for k, v in deps.items():
            if k == eng and eng in ("pe", "sp"):
                continue
            if wd.get(k, 0) >= v:
                continue
            wd[k] = v
            waits.append((k, v))
        if dma is not None:
            dma.count += 16
            tok = (dma, dma.count)
            ser = None
        else:
            self.serial[eng] += 1
            ser = self.serial[eng]
            tok = (eng, ser)
        self.ops[eng].append((waits, fn, ser, dma))
        for b in reads:
            if not isinstance(b, DramBuf):
                b.r.append(tok)
        for b in writes:
            if isinstance(b, DramBuf):
                k, v = tok
                if b.ws.get(k, 0) < v:
                    b.ws[k] = v
            else:
                b.w = tok
                b.r = []
        return tok

    def op(self, eng, method, reads=(), writes=(), dma=None, **kw):
        return self.emit(eng, lambda e, m=method, kw=kw: getattr(e, m)(**kw), reads, writes, dma)

    def wait_all(self, eng, bufs):
        deps = self._deps(bufs, ())
        waits = []
        wd = self.waited[eng]
        for k, v in deps.items():
            if wd.get(k, 0) >= v:
                continue
            wd[k] = v
            waits.append((k, v))
        self.ops[eng].append((waits, None, None, None))

    def finalize(self):
        nc = self.nc
        needed = {e: set() for e in self.ENGS}
        for e in self.ENGS:
            for waits, fn, ser, dma in self.ops[e]:
                for k, v in waits:
                    if isinstance(k, str):
                        needed[k].add(v)
        vmap = {e: {v: i + 1 for i, v in enumerate(sorted(needed[e]))} for e in self.ENGS}
        esem = {e: self.stack.enter_context(nc.semaphore("s_" + e)) for e in self.ENGS}
        for d in self.dsems:
            if d.count:
                d.h = self.stack.enter_context(nc.semaphore("d_" + d.name))
        handles = {"pe": "tensor", "act": "scalar", "dve": "vector", "pool": "gpsimd", "sp": "sync"}
        stats = {e: len(self.ops[e]) for e in self.ENGS}
        stats["ndsem"] = sum(1 for d in self.dsems if d.count)
        with nc.Block() as block:
            for e in self.ENGS:
                ops = self.ops[e]

                def body(h, ops=ops, e=e):
                    for waits, fn, ser, dma in ops:
                        for k, v in waits:
                            if isinstance(k, str):
                                h.wait_ge(esem[k], vmap[k][v])
                            else:
                                h.wait_ge(k.h, v)
                        if fn is None:
                            continue
                        ins = fn(h)
                        if dma is not None:
                            ins.then_inc(dma.h, 16)
                        elif ser in vmap[e]:
                            ins.then_inc(esem[e], 1)

                getattr(block, handles[e])(body)
        return stats


def ps_alloc(P, name, shape, dt=F32):
    return P.stack.enter_context(P.nc.psum_tensor(name, list(shape), dt))


def barrier(P, bufs):
    deps = {}
    for b in bufs:
        for tok in ([b.w] if b.w else []) + list(b.r):
            k, v = tok
            if deps.get(k, 0) < v:
                deps[k] = v
    for e in P.ENGS:
        waits = []
        wd = P.waited[e]
        for k, v in deps.items():
            if wd.get(k, 0) >= v:
                continue
            wd[k] = v
            waits.append((k, v))
        P.ops[e].append((waits, None, None, None))


class Phase:
    def __init__(self, P):
        self.P = P

    def __enter__(self):
        self.mark = self.P.aoff
        self.bufs = []
        self._slot = self.P.slot
        P = self.P

        def slot(name, shape, dt, dma=False, _o=self._slot, _s=self):
            s = _o(name, shape, dt, dma)
            _s.bufs.append(s.b)
            return s
        P.slot = slot
        return self

    def __exit__(self, *a):
        self.P.slot = self._slot
        barrier(self.P, self.bufs)
        self.P.aoff = self.mark
        return False


C_GAT, C_GMLP, C_GPLE, C_GMO, C_CW, C_CB, C_BIF, C_INVF, C_PVAL, C_TRI, C_ONES, C_PMASK = 0, 16, 32, 48, 56, 88, 96, 104, 120, 121, 249, 377
CB_ID, CB_TRI, CB_TOT = 0, 128, 256
CC_CM, CC_CMQ, CC_ESEL = 0, 512, 1024
CC_TOT = 1024 + 33 * 128


def host_consts(T):
    tri = (np.arange(128)[:, None] <= np.arange(128)[None, :]).astype(np.float32)
    cb = np.zeros((128, CB_TOT), np.float32)
    cb[:, CB_ID:CB_ID + 128] = np.eye(128, dtype=np.float32)
    cb[:, CB_TRI:CB_TRI + 128] = tri
    cc = np.zeros((128, CC_TOT), np.float32)
    for kt in range(2):
        key = kt * 128 + np.arange(128)[:, None]
        q = np.arange(256)[None, :]
        cc[:, CC_CM + kt * 256:CC_CM + (kt + 1) * 256] = np.where(key <= q, 0.0, NEGBIG)
        qq = kt * 128 + np.arange(128)[:, None]
        key2 = np.arange(256)[None, :]
        cc[:, CC_CMQ + kt * 256:CC_CMQ + (kt + 1) * 256] = np.where(key2 <= qq, 0.0, NEGBIG)
    for n in range(33):
        cc[n, CC_ESEL + n * 128:CC_ESEL + (n + 1) * 128] = 1.0
    return tri, cb, cc


def build(T, debug=None):
    S2 = 2 * T
    NT = S2 // 128
    NTO = T // 128
    TB = 1024
    NBLK = S2 // TB
    PBLK = NBLK // 2
    NB = S2 // 256
    NBP = NB // 2
    NBO = NB - NBP
    assert NB <= 32
    NCST = C_PMASK + NBO * NB
    okind = "ExternalOutput" if debug else "Internal"

    nc = bass.Bass("TRN2", target_bir_lowering=False)
    dt_in = lambda n, s, d=F32: nc.dram_tensor(n, list(s), d, kind="ExternalInput")
    xs = dt_in("xs", [S2, D]); pp_in = dt_in("p", [T, 256]); pos = dt_in("pos", [128, NT], I32)
    cst = dt_in("cst", [128, NCST]); cstb = dt_in("cstb", [128, CB_TOT]); cstc = dt_in("cstc", [128, CC_TOT])
    gfin = dt_in("gfin", [128, D])
    w_in = dt_in("w_in", [D, INW]); w_up_m = dt_in("w_up_m", [1024, D]); w_up_a = dt_in("w_up_a", [1024, D])
    w_out = dt_in("w_out", [D, D]); w_ff1 = dt_in("w_ff1", [D, DFF]); w_ff2 = dt_in("w_ff2", [DFF, D])
    w_pg = dt_in("w_pg", [D, D]); w_pp = dt_in("w_pp", [256, D])
    out = nc.dram_tensor("out", [T, D], F32, kind="ExternalOutput")
    scr = lambda n, s: nc.dram_tensor(n, list(s), BF16, kind=okind)
    mqT_s = scr("mqT_s", [512, T]); mkT_s = scr("mkT_s", [512, S2]); mv_s = scr("mv_s", [S2, 1024])
    smo_s = scr("smo_s", [T, 1024]); aqT_s = scr("aqT_s", [1024, T]); akT_s = scr("akT_s", [1024, S2])
    av_s = scr("av_s", [S2, 1024]); sgmT_s = scr("sgmT_s", [D, T]); sgaT_s = scr("sgaT_s", [D, T])
    hmT_s = scr("hmT_s", [1024, T]); haT_s = scr("haT_s", [1024, T])
    D_mqT, D_mkT, D_mv, D_smo, D_aqT, D_akT, D_av, D_sgm, D_sga, D_hmT, D_haT, D_out = [DramBuf(n) for n in
        ("mqT", "mkT", "mv", "smo", "aqT", "akT", "av", "sgm", "sga", "hmT", "haT", "out")]
    dbg_out = {}
    wsrc = dict(up_m=w_up_m, up_a=w_up_a, out=w_out, ff1=w_ff1, ff2=w_ff2, pg=w_pg, pp=w_pp)
    wb = {}
    D_wb = {}
    conv_jobs = []
    for nm_, W_ in wsrc.items():
        R_, C_ = W_.shape
        wb[nm_] = nc.dram_tensor("wb_" + nm_, [R_, C_], BF16, kind="Internal")
        D_wb[nm_] = DramBuf("wb_" + nm_)
        a_ = C_ // 2048
        srcf = W_.ap().rearrange("r (a b) -> (r a) b", b=2048) if a_ > 1 else W_.ap()
        dstf = wb[nm_].ap().rearrange("r (a b) -> (r a) b", b=2048) if a_ > 1 else wb[nm_].ap()
        rows = R_ * a_
        for r0 in range(0, rows, 512):
            r1 = min(rows, r0 + 512)
            conv_jobs.append((srcf[r0:r1, :], dstf[r0:r1, :], D_wb[nm_]))

    P = Prog(nc)
    op = P.op
    P.init_arena(172 * 1024)
    cv_sems = [P.dsem("cv%d" % i) for i in range(4)]
    cv_state = {"i": 0}

    def emit_conv_job(gate=()):
        i = cv_state["i"]
        if i >= len(conv_jobs):
            return False
        src_, dst_, db_ = conv_jobs[i]
        cv_state["i"] = i + 1
        op("pool", "dma_start", reads=list(gate), writes=[db_], dma=cv_sems[i % 4], out=dst_, in_=src_)
        return True

    cs = P.slot("cst", [128, NCST], F32, dma=True)
    cbs = P.slot("cstb", [128, CB_TOT], BF16, dma=True)
    CS, CBt = cs.t, cbs.t
    ident = CBt[:, CB_ID:CB_ID + 128]
    tri_bf = CBt[:, CB_TRI:CB_TRI + 128]
    tri_f = CS[:, C_TRI:C_TRI + 128]
    ones_f = CS[:, C_ONES:C_ONES + 128]
    G = P.slot("G", [128, NT, 8], F32)
    kmean = P.slot("kmean", [128, 8, NB], F32)
    kmean_b = P.slot("kmean_b", [128, 8, NB], BF16)
    halo = P.slot("halo", [128, 8, 3], F32)
    sm = P.ring("sm", 4, [128, 4], F32)
    psf = ps_alloc(P, "psf", [128, 6 * 512], F32)
    bank = lambda i: psf[:, i * 512:(i + 1) * 512]
    acc = Ring([Slot(bank(i), Buf("acc%d" % i), None) for i in range(4)])
    aux = [Slot(bank(4 + i), Buf("aux%d" % i), None) for i in range(2)]
    psb = ps_alloc(P, "psb", [128, 2048], BF16)[:, :]
    tps = Ring([Slot(psb[:, i * 1024:i * 1024 + 512], Buf("tp%d" % i), None) for i in range(2)])

    op("sp", "dma_start", writes=[cs.b], dma=cs.d, out=CS, in_=cst[:, :])
    op("pool", "dma_start", writes=[cbs.b], dma=cbs.d, out=CBt, in_=cstb[:, :])
    op("dve", "memset", writes=[halo.b], ap=halo.t, constant=0.0)
    op("dve", "memset", writes=[kmean.b], ap=kmean.t, constant=0.0)

    def load_w(ring, W, r0, nk, c0, ncol, dep=None, q="pool"):
        s = ring.next()
        for k0 in range(0, nk, 4):
            k1 = min(nk, k0 + 4)
            src = W[r0 + k0 * 128:r0 + k1 * 128, c0:c0 + ncol].rearrange("(k p) c -> p k c", p=128)
            op(q, "dma_start", reads=([dep] if dep is not None else []), writes=[s.b], dma=s.d, out=s.t[:, k0:k1, 0:ncol], in_=src)
        return s

    def norm_stats(xa, xb_, width, junk_ap, junk_b):
        s = sm.next()
        op("act", "activation", reads=[xb_], writes=[junk_b, s.b], out=junk_ap, in_=xa, func=AF.Square, accum_out=s.t[:, 0:1])
        op("dve", "tensor_scalar", reads=[s.b], writes=[s.b], out=s.t[:, 1:2], in0=s.t[:, 0:1], scalar1=1.0 / width, scalar2=EPS, op0=ALU.mult, op1=ALU.add)
        op("act", "activation", reads=[s.b], writes=[s.b], out=s.t[:, 2:3], in_=s.t[:, 1:2], func=AF.Sqrt)
        op("dve", "reciprocal", reads=[s.b], writes=[s.b], out=s.t[:, 3:4], in_=s.t[:, 2:3])
        return s

    def norm_transpose(tiles, gcol, hT, xn, grp):
        xns = []
        nt = len(tiles)
        for t, (xa, xb_) in enumerate(tiles):
            n_ = xn.next()
            s = norm_stats(xa, xb_, D, n_.t, n_.b)
            op("dve", "tensor_scalar", reads=[xb_, s.b], writes=[n_.b], out=n_.t, in0=xa, scalar1=s.t[:, 3:4], scalar2=None, op0=ALU.mult)
            xns.append(n_)
            if len(xns) == grp or t == nt - 1:
                t0 = t + 1 - len(xns)
                w = len(xns) * 128
                for k in range(KC):
                    tp = tps.next()
                    for i, n2 in enumerate(xns):
                        op("pe", "transpose", reads=[n2.b, cbs.b], writes=[tp.b], out=tp.t[:, i * 128:(i + 1) * 128],
                           in_=n2.t[:, k * 128:(k + 1) * 128], identity=ident)
                    op("dve", "tensor_tensor", reads=[tp.b, cs.b], writes=[hT.b], out=hT.t[:, k, t0 * 128:t0 * 128 + w],
                       in0=tp.t[:, 0:w], in1=mk(CS, gcol + k, [[0, w]]), op=ALU.mult)
                xns = []

    with Phase(P):
        NF = NT * 16
        sincos = P.slot("sincos", [128, 2, NF], F32)
        with Phase(P):
            posi = P.slot("posi", [128, NT], I32, dma=True)
            posf = P.slot("posf", [128, NT], F32)
            tb = P.slot("tab", [128, 4, NF], F32)
            TT = tb.t
            op("sp", "dma_start", writes=[posi.b], dma=posi.d, out=posi.t, in_=pos[:, :])
            op("dve", "tensor_copy", reads=[posi.b], writes=[posf.b], out=posf.t, in_=posi.t)
            a_pos = mk(posf.t, 0, [[1, NT], [0, 16]])
            a_inv = mk(CS, C_INVF, [[0, NT], [1, 16]])
            op("dve", "tensor_tensor", reads=[posf.b, cs.b], writes=[tb.b], out=mk(TT, 0, [[16, NT], [1, 16]]), in0=a_pos, in1=a_inv, op=ALU.mult)
            TWO_PI = 2.0 * math.pi
            C1 = 6.28125
            C2 = TWO_PI - C1
            MAGIC = 12582912.0
            PI_LO = 3.1415925
            for which, shift in ((0, 0.0), (1, math.pi / 2)):
                rw = dict(reads=[tb.b], writes=[tb.b])
                op("dve", "tensor_scalar", out=TT[:, 3, :], in0=TT[:, 0, :], scalar1=shift, scalar2=None, op0=ALU.add, **rw)
                op("dve", "tensor_scalar", out=TT[:, 1, :], in0=TT[:, 3, :], scalar1=1.0 / TWO_PI, scalar2=None, op0=ALU.mult, **rw)
                op("dve", "tensor_scalar", out=TT[:, 2, :], in0=TT[:, 1, :], scalar1=MAGIC, scalar2=None, op0=ALU.add, **rw)
                op("dve", "tensor_scalar", out=TT[:, 1, :], in0=TT[:, 2, :], scalar1=-MAGIC, scalar2=None, op0=ALU.add, **rw)
                op("dve", "scalar_tensor_tensor", out=TT[:, 2, :], in0=TT[:, 1, :], scalar=-C1, in1=TT[:, 3, :], op0=ALU.mult, op1=ALU.add, **rw)
                op("dve", "scalar_tensor_tensor", out=TT[:, 3, :], in0=TT[:, 1, :], scalar=-C2, in1=TT[:, 2, :], op0=ALU.mult, op1=ALU.add, **rw)
                op("dve", "tensor_scalar", out=TT[:, 3, :], in0=TT[:, 3, :], scalar1=PI_LO, scalar2=-PI_LO, op0=ALU.min, op1=ALU.max, **rw)
                op("act", "activation", reads=[tb.b], writes=[sincos.b], out=sincos.t[:, which, :], in_=TT[:, 3, :], func=AF.Sin)

        if debug == "A0":
            dbg = nc.dram_tensor("dbg_sincos", [128, 2 * NF], F32, kind="ExternalOutput")
            dd = P.dsem("dbg")
            op("sp", "dma_start", reads=[sincos.b], writes=[D_out], dma=dd, out=dbg[:, :], in_=mk(sincos.t, 0, [[1, 2 * NF]]))
            return finish(P, nc, [D_out])

        def tabv(which, tile, nh):
            return mk(sincos.t, which * NF + tile * 16, [[0, nh], [1, 16]])

        xt = P.ring("xt", 2, [128, D], F32, dma=True)
        xn = P.ring("xn", 4, [128, D], BF16)
        hT = P.slot("hT", [128, KC, TB], BF16)
        wt = P.ring("wt", 2, [128, KC, 512], BF16, dma=True)
        wg = P.ring("wg", 2, [128, KC, 8], BF16, dma=True)
        st = P.ring("st", 4, [128, 512], BF16, dma=True)
        rf = P.ring("rf", 2, [128, 512], F32)
        rtmp = P.ring("rtmp", 2, [128, 4, 4, 16], F32)
        rb = P.ring("rb", 2, [128, 512], BF16)
        qk_st = P.ring("qkst", 2, [128, 4, TB], BF16, dma=True)
        zp = P.ring("zp", 2, [128, 516], F32)
        cy = P.ring("cy", 2, [128, 512], F32)
        csg = P.ring("csg", 2, [128, 512], F32)
        cob = P.ring("cob", 2, [128, 512], BF16, dma=True)
        pending = []

        def flush_pending():
            while pending:
                pending.pop(0)()

        pre_x = []
        for blk in range(NBLK):
            own = blk >= PBLK
            tok0 = blk * TB
            otok0 = tok0 - T
            tiles = []
            for t in range(TB // 128):
                if pre_x:
                    xs_ = pre_x.pop(0)
                else:
                    xs_ = xt.next()
                    op("sp", "dma_start", writes=[xs_.b], dma=xs_.d, out=xs_.t, in_=xs[tok0 + t * 128:tok0 + (t + 1) * 128, :])
                tiles.append((xs_.t, xs_.b))
                if debug == "A1a" and len(tiles) == 2:
                    dbg = nc.dram_tensor("dbg_x", [128, D], F32, kind="ExternalOutput")
                    dd = P.dsem("dbg")
                    op("sp", "dma_start", reads=[xs_.b], writes=[D_out], dma=dd, out=dbg[:, :], in_=xs_.t)
                    return finish(P, nc, [D_out])
                if len(tiles) == 2:
                    t0 = t - 1
                    norm_transpose(tiles, C_GAT, Slot(mk(hT.t, t0 * 128, [[TB, KC], [1, 256]]), hT.b, None), xn, 2)
                    tiles = []
                    if debug == "A1c":
                        dbg = nc.dram_tensor("dbg_hT", [128, KC * TB], BF16, kind="ExternalOutput")
                        dd = P.dsem("dbg")
                        for k_ in range(KC):
                            op("sp", "dma_start", reads=[hT.b], writes=[D_out], dma=dd, out=dbg[:, k_ * TB:(k_ + 1) * TB], in_=hT.t[:, k_, :])
                        return finish(P, nc, [D_out])
            if debug == "A1":
                dbg = nc.dram_tensor("dbg_hT", [128, KC * TB], BF16, kind="ExternalOutput")
                dd = P.dsem("dbg")
                op("sp", "dma_start", reads=[hT.b], writes=[D_out], dma=dd, out=dbg[:, :], in_=mk(hT.t, 0, [[1, KC * TB]]))
                return finish(P, nc, [D_out])
            groups = []
            if blk >= PBLK - 1:
                groups.append(("conv", O_MQ, 0))
            groups.append(("conv", O_MK, 4))
            groups += [("v", O_MV, mv_s, D_mv, 0), ("v", O_MV + 512, mv_s, D_mv, 512)]
            if own:
                groups += [("sig", O_MO, smo_s, D_smo, 0), ("sig", O_MO + 512, smo_s, D_smo, 512)]
                groups += [("rope", O_AQ, aqT_s, D_aqT, 0, False), ("rope", O_AQ + 512, aqT_s, D_aqT, 4, False)]
            groups += [("rope", O_AK, akT_s, D_akT, 0, True), ("rope", O_AK + 512, akT_s, D_akT, 4, True)]
            groups += [("v", O_AV, av_s, D_av, 0), ("v", O_AV + 512, av_s, D_av, 512)]
            if own:
                for i in range(4):
                    groups.append(("sigT", O_GM + i * 512, sgmT_s, D_sgm, i * 512))
                for i in range(4):
                    groups.append(("sigT", O_GA + i * 512, sgaT_s, D_sga, i * 512))
            import os
            _kf = os.environ.get("KGROUPS")
            if _kf:
                groups = [g for g in groups if g[0] in _kf.split(",")]
            for g in groups:
                kind, c0 = g[0], g[1]
                w = load_w(wt, w_in, 0, KC, c0, 512)
                if g is groups[-1] and blk + 1 < NBLK:
                    for t_ in range(2):
                        xs_ = xt.next()
                        op("sp", "dma_start", writes=[xs_.b], dma=xs_.d, out=xs_.t, in_=xs[tok0 + TB + t_ * 128:tok0 + TB + (t_ + 1) * 128, :])
                        pre_x.append(xs_)
                if kind in ("v", "sig", "rope"):
                    gates = (kind == "v" and g[2] is mv_s and g[4] == 0)
                    if gates:
                        wgs = load_w(wg, w_in, 0, KC, O_MI, 8)
                    if kind == "rope":
                        qs = qk_st.next()
                    for t in range(TB // 128):
                        gt = blk * (TB // 128) + t
                        a = acc.next()
                        for k in range(KC):
                            op("pe", "matmul", reads=[hT.b, w.b], writes=[a.b], out=a.t, lhsT=hT.t[:, k, t * 128:(t + 1) * 128],
                               rhs=w.t[:, k, :], start=(k == 0), stop=(k == KC - 1))
                        if gates:
                            a2 = acc.next()
                            for k in range(KC):
                                op("pe", "matmul", reads=[hT.b, wgs.b], writes=[a2.b], out=a2.t[:, 0:8], lhsT=hT.t[:, k, t * 128:(t + 1) * 128],
                                   rhs=wgs.t[:, k, :], start=(k == 0), stop=(k == KC - 1))
                            op("dve", "tensor_tensor", reads=[a2.b, cs.b], writes=[G.b], out=G.t[:, gt, :], in0=a2.t[:, 0:8],
                               in1=CS[:, C_BIF:C_BIF + 8], op=ALU.add)
                        flush_pending()
                        if kind == "v":
                            s = st.next()
                            op("dve", "tensor_copy", reads=[a.b], writes=[s.b], out=s.t, in_=a.t)
                            op("sp", "dma_start", reads=[s.b], writes=[g[3]], dma=s.d,
                               out=g[2][tok0 + t * 128:tok0 + (t + 1) * 128, g[4]:g[4] + 512], in_=s.t)
                        elif kind == "sig":
                            s = st.next()
                            op("act", "activation", reads=[a.b], writes=[s.b], out=s.t, in_=a.t, func=AF.Sigmoid)
                            op("sp", "dma_start", reads=[s.b], writes=[g[3]], dma=s.d,
                               out=g[2][otok0 + t * 128:otok0 + (t + 1) * 128, g[4]:g[4] + 512], in_=s.t)
                        else:
                            r = rf.next(); tm = rtmp.next(); rbb = rb.next()
                            op("dve", "tensor_copy", reads=[a.b], writes=[r.b], out=r.t, in_=a.t)
                            x1 = mk(r.t, 0, [[128, 4], [1, 16]])
                            x2 = mk(r.t, 16, [[128, 4], [1, 16]])
                            cosv, sinv = tabv(1, gt, 4), tabv(0, gt, 4)
                            rr = dict(reads=[r.b, sincos.b], writes=[tm.b])
                            op("dve", "tensor_tensor", out=tm.t[:, 0], in0=x1, in1=cosv, op=ALU.mult, **rr)
                            op("dve", "tensor_tensor", out=tm.t[:, 1], in0=x2, in1=sinv, op=ALU.mult, **rr)
                            op("dve", "tensor_tensor", out=tm.t[:, 2], in0=x2, in1=cosv, op=ALU.mult, **rr)
                            op("dve", "tensor_tensor", out=tm.t[:, 3], in0=x1, in1=sinv, op=ALU.mult, **rr)
                            op("dve", "tensor_tensor", reads=[tm.b], writes=[r.b], out=x1, in0=tm.t[:, 0], in1=tm.t[:, 1], op=ALU.subtract)
                            op("dve", "tensor_tensor", reads=[tm.b], writes=[r.b], out=x2, in0=tm.t[:, 2], in1=tm.t[:, 3], op=ALU.add)
                            op("act", "activation", reads=[r.b], writes=[rbb.b], out=rbb.t, in_=r.t, func=AF.Copy)

                            def do_tr(rbb=rbb, qs=qs, t=t):
                                tp = tps.next()
                                for hh in range(4):
                                    op("pe", "transpose", reads=[rbb.b, cbs.b], writes=[tp.b], out=tp.t[:, hh * 128:(hh + 1) * 128],
                                       in_=rbb.t[:, hh * 128:(hh + 1) * 128], identity=ident)
                                dst = mk(qs.t, t * 128, [[TB, 4], [1, 128]])
                                src = mk(tp.t, 0, [[128, 4], [1, 128]])
                                op("act", "activation", reads=[tp.b], writes=[qs.b], out=dst, in_=src, func=AF.Copy)
                            pending.append(do_tr)
                    if kind == "rope":
                        flush_pending()
                        hb, is_k = g[4], g[5]
                        t0_ = tok0 if is_k else otok0
                        if is_k:
                            b0 = blk * 4
                            kmv = mk(kmean.t, hb * NB + b0, [[NB, 4], [1, 4]])
                            src = mk(qs.t, 0, [[TB, 4], [256, 4], [1, 256]])
                            op("dve", "tensor_reduce", reads=[qs.b], writes=[kmean.b], out=kmv, in_=src, axis=AX.X, op=ALU.add)
                        dstd = g[2][hb * 128:(hb + 4) * 128, t0_:t0_ + TB].rearrange("(h p) t -> p h t", p=128)
                        op("sp", "dma_start", reads=[qs.b], writes=[g[3]], dma=qs.d, out=dstd, in_=qs.t)
                else:
                    for cc in range(4):
                        for th in range(TB // 512):
                            a = acc.next()
                            for k in range(KC):
                                op("pe", "matmul", reads=[hT.b, w.b], writes=[a.b], out=a.t, lhsT=w.t[:, k, cc * 128:(cc + 1) * 128],
                                   rhs=hT.t[:, k, th * 512:(th + 1) * 512], start=(k == 0), stop=(k == KC - 1))
                            if kind == "sigT":
                                s = st.next()
                                op("act", "activation", reads=[a.b], writes=[s.b], out=s.t, in_=a.t, func=AF.Sigmoid)
                                r0 = g[4] + cc * 128
                                op("sp", "dma_start", reads=[s.b], writes=[g[3]], dma=s.d,
                                   out=g[2][r0:r0 + 128, otok0 + th * 512:otok0 + (th + 1) * 512], in_=s.t)
                            else:
                                hc = g[2] + cc
                                z = zp.next(); y = cy.next(); sg = csg.next(); ob = cob.next()
                                op("dve", "tensor_copy", reads=[halo.b], writes=[z.b], out=z.t[:, 0:3], in_=halo.t[:, hc, :])
                                op("dve", "tensor_copy", reads=[a.b], writes=[z.b], out=z.t[:, 3:515], in_=a.t)
                                op("dve", "tensor_copy", reads=[z.b], writes=[halo.b], out=halo.t[:, hc, :], in_=z.t[:, 512:515])
                                cwc = lambda j, hc=hc: CS[:, C_CW + j * 8 + hc:C_CW + j * 8 + hc + 1]
                                op("dve", "tensor_scalar", reads=[z.b, cs.b], writes=[y.b], out=y.t, in0=z.t[:, 0:512], scalar1=cwc(0),
                                   scalar2=CS[:, C_CB + hc:C_CB + hc + 1], op0=ALU.mult, op1=ALU.add)
                                for j in range(1, 4):
                                    op("dve", "scalar_tensor_tensor", reads=[z.b, cs.b, y.b], writes=[y.b], out=y.t, in0=z.t[:, j:j + 512],
                                       scalar=cwc(j), in1=y.t, op0=ALU.mult, op1=ALU.add)
                                op("act", "activation", reads=[y.b], writes=[sg.b], out=sg.t, in_=y.t, func=AF.Sigmoid)
                                scl = 1.0 if hc < 4 else 128.0 ** -0.5
                                op("dve", "scalar_tensor_tensor", reads=[y.b, sg.b], writes=[ob.b], out=ob.t, in0=y.t, scalar=scl,
                                   in1=sg.t, op0=ALU.mult, op1=ALU.mult)
                                if hc < 4:
                                    if own:
                                        op("sp", "dma_start", reads=[ob.b], writes=[D_mqT], dma=ob.d,
                                           out=mqT_s[hc * 128:(hc + 1) * 128, otok0 + th * 512:otok0 + (th + 1) * 512], in_=ob.t)
                                else:
                                    op("sp", "dma_start", reads=[ob.b], writes=[D_mkT], dma=ob.d,
                                       out=mkT_s[(hc - 4) * 128:(hc - 3) * 128, tok0 + th * 512:tok0 + (th + 1) * 512], in_=ob.t)
        op("dve", "tensor_copy", reads=[kmean.b], writes=[kmean_b.b], out=kmean_b.t, in_=kmean.t)
    if debug == "A":
        return finish(P, nc, [D_mqT, D_mkT, D_mv, D_smo, D_aqT, D_akT, D_av, D_sgm, D_sga])

    with Phase(P):
        NG = NT * 4
        gl = P.slot("gl", [128, NG], F32)
        gb = P.slot("gb", [128, NG], F32)
        gw = P.slot("gw", [128, NG], F32)
        gw2 = P.slot("gw2", [128, NG], F32)
        get = P.slot("get", [128, NG], F32)
        geL = P.slot("geL", [128, NG], F32)
        Gi = mk(G.t, 0, [[8, NT], [1, 4]])
        Gf = mk(G.t, 4, [[8, NT], [1, 4]])
        v2 = lambda s_: mk(s_.t, 0, [[4, NT], [1, 4]])
        op("act", "activation", reads=[G.b], writes=[gl.b], out=v2(gl), in_=Gf, func=AF.Exp, scale=-1.0)
        op("act", "activation", reads=[gl.b], writes=[gl.b], out=gl.t, in_=gl.t, func=AF.Ln, bias=1.0)
        a = acc.next()
        op("pe", "matmul", reads=[gl.b, cs.b], writes=[a.b], out=a.t[:, 0:NG], lhsT=tri_f, rhs=gl.t, start=True, stop=True)
        op("dve", "tensor_scalar", reads=[a.b], writes=[gb.b], out=gb.t, in0=a.t[:, 0:NG], scalar1=-1.0, scalar2=None, op0=ALU.mult)
        a = acc.next()
        op("pe", "matmul", reads=[gl.b, cs.b], writes=[a.b], out=a.t[:, 0:NG], lhsT=ones_f, rhs=gl.t, start=True, stop=True)
        op("act", "activation", reads=[a.b], writes=[geL.b], out=geL.t, in_=a.t[:, 0:NG], func=AF.Exp, scale=-1.0)
        op("act", "activation", reads=[gb.b], writes=[get.b], out=get.t, in_=gb.t, func=AF.Exp)
        op("dve", "tensor_tensor", reads=[G.b, gb.b], writes=[gw.b], out=v2(gw), in0=Gi, in1=v2(gb), op=ALU.subtract)
        op("act", "activation", reads=[gw.b], writes=[gw.b], out=gw.t, in_=gw.t, func=AF.Exp)
        op("dve", "tensor_scalar", reads=[gw.b, cs.b], writes=[gw.b], out=gw.t[:, 0:NG // 2], in0=gw.t[:, 0:NG // 2],
           scalar1=CS[:, C_PVAL:C_PVAL + 1], scalar2=None, op0=ALU.mult)
        op("dve", "tensor_tensor", reads=[gw.b, geL.b], writes=[gw2.b], out=gw2.t, in0=gw.t, in1=geL.t, op=ALU.mult)

        Cf = P.slot("Cf", [128, 4, 258], F32)
        Cb = P.slot("Cb", [128, 4, 258], BF16)
        op("dve", "memset", writes=[Cf.b], ap=Cf.t, constant=0.0)
        op("dve", "memset", writes=[Cb.b], ap=Cb.t, constant=0.0)
        CH = 8
        kTr = P.ring("kTr", 2, [128, 4, CH * 128], BF16, dma=True)
        qTr = P.ring("qTr", 2, [128, 4, CH * 128], BF16, dma=True)
        Vr = P.ring("Vr", 2, [128, CH, 4, 258], BF16, dma=True)
        smr = P.ring("smr", 2, [128, CH, 1024], BF16, dma=True)
        hst = P.ring("hst", 2, [128, 8, CH * 128], BF16, dma=True)
        for s_ in Vr.slots:
            op("dve", "memset", writes=[s_.b], ap=s_.t, constant=1.0)
        ktok = P.ring("ktok", 3, [128, 128], BF16)
        V1 = P.ring("V1", 3, [128, 258], BF16)
        V2 = P.ring("V2", 3, [128, 258], BF16)
        Smr = P.ring("Sm", 3, [128, 128], BF16)
        hmb = P.ring("hmb", 10, [128, 256], BF16)
        rawsb = P.ring("rawsb", 3, [128, 4, 258], F32)
        jk = P.ring("jk", 2, [128, 256], BF16)
        sc8 = P.ring("sc8", 4, [128, 32], F32)
        pendB = []

        def flushB(keep=0):
            while len(pendB) > keep:
                pendB.pop(0)()

        for cg in range(NT // CH):
            ownc = cg * CH >= NT // 2
            t0 = cg * CH * 128
            ot0 = t0 - T
            kT = kTr.next(); Vt = Vr.next()
            op("sp", "dma_start", reads=[D_mkT], writes=[kT.b], dma=kT.d, out=kT.t,
               in_=mkT_s[:, t0:t0 + CH * 128].rearrange("(h p) t -> p h t", p=128))
            for n_ in range(CH):
                op("sp", "dma_start", reads=[D_mv], writes=[Vt.b], dma=Vt.d, out=Vt.t[:, n_, :, 0:256],
                   in_=mv_s[t0 + n_ * 128:t0 + (n_ + 1) * 128, :].rearrange("p (h c) -> p h c", h=4))
            if ownc:
                qT = qTr.next(); smo = smr.next(); hs = hst.next()
                op("sp", "dma_start", reads=[D_mqT], writes=[qT.b], dma=qT.d, out=qT.t,
                   in_=mqT_s[:, ot0:ot0 + CH * 128].rearrange("(h p) t -> p h t", p=128))
                for n_ in range(CH):
                    op("sp", "dma_start", reads=[D_smo], writes=[smo.b], dma=smo.d, out=smo.t[:, n_, :],
                       in_=smo_s[ot0 + n_ * 128:ot0 + (n_ + 1) * 128, :])
            for cl in range(CH):
                c = cg * CH + cl
                if ownc:
                    rwc = rawsb.next(); s8c = sc8.next()
                for h in range(4):
                    col = c * 4 + h
                    flushB(4)
                    gcol = lambda s_, col=col: s_.t[:, col:col + 1]
                    ksl = kT.t[:, h, cl * 128:(cl + 1) * 128]
                    tp = tps.next()
                    op("pe", "transpose", reads=[kT.b, cbs.b], writes=[tp.b], out=tp.t[:, 0:128], in_=ksl, identity=ident)
                    kk = ktok.next()
                    op("act", "activation", reads=[tp.b], writes=[kk.b], out=kk.t, in_=tp.t[:, 0:128], func=AF.Copy)
                    vv2 = V2.next()
                    op("dve", "tensor_scalar", reads=[Vt.b, gw2.b], writes=[vv2.b], out=vv2.t[:, 0:257], in0=Vt.t[:, cl, h, 0:257],
                       scalar1=gcol(gw2), scalar2=None, op0=ALU.mult)
                    if ownc:
                        qsl = qT.t[:, h, cl * 128:(cl + 1) * 128]
                        vv1 = V1.next()
                        op("dve", "tensor_scalar", reads=[Vt.b, gw.b], writes=[vv1.b], out=vv1.t[:, 0:257], in0=Vt.t[:, cl, h, 0:257],
                           scalar1=gcol(gw), scalar2=None, op0=ALU.mult)
                        a = acc.next()
                        op("pe", "matmul", reads=[kT.b, qT.b], writes=[a.b], out=a.t[:, 0:128], lhsT=ksl, rhs=qsl, start=True, stop=True)
                        sm_ = Smr.next()
                        op("dve", "tensor_tensor", reads=[a.b, cbs.b], writes=[sm_.b], out=sm_.t, in0=a.t[:, 0:128], in1=tri_bf, op=ALU.mult)
                        a2 = acc.next()
                        op("pe", "matmul", reads=[qT.b, Cb.b], writes=[a2.b], out=a2.t[:, 0:257], lhsT=qsl, rhs=Cb.t[:, h, 0:257], start=True, stop=False)
                        op("pe", "matmul", reads=[sm_.b, vv1.b], writes=[a2.b], out=a2.t[:, 0:257], lhsT=sm_.t, rhs=vv1.t[:, 0:257], start=False, stop=True)
                        pass
                    a3 = acc.next()
                    op("pe", "matmul", reads=[kk.b, vv2.b], writes=[a3.b], out=a3.t[:, 0:257], lhsT=kk.t, rhs=vv2.t[:, 0:257], start=True, stop=True)
                    op("dve", "scalar_tensor_tensor", reads=[Cf.b, geL.b, a3.b], writes=[Cf.b], out=Cf.t[:, h, 0:257], in0=Cf.t[:, h, 0:257],
                       scalar=gcol(geL), in1=a3.t[:, 0:257], op0=ALU.mult, op1=ALU.add)
                    op("act", "activation", reads=[Cf.b], writes=[Cb.b], out=Cb.t[:, h, 0:257], in_=Cf.t[:, h, 0:257], func=AF.Copy)
                    if ownc:
                        op("dve", "tensor_copy", reads=[a2.b], writes=[rwc.b], out=rwc.t[:, h, 0:257], in_=a2.t[:, 0:257])
                        j_ = jk.next()
                        op("act", "activation", reads=[rwc.b], writes=[j_.b, s8c.b], out=j_.t, in_=rwc.t[:, h, 0:256], func=AF.Square,
                           accum_out=s8c.t[:, h:h + 1])
                if ownc:
                    R = lambda r: s8c.t[:, r * 4:(r + 1) * 4]
                    den4 = mk(rwc.t, 256, [[258, 4]])
                    get4 = get.t[:, c * 4:(c + 1) * 4]
                    rw8 = dict(reads=[s8c.b], writes=[s8c.b])
                    op("dve", "tensor_tensor", reads=[rwc.b, get.b], writes=[s8c.b], out=R(1), in0=den4, in1=get4, op=ALU.mult)
                    op("dve", "tensor_scalar", out=R(2), in0=R(1), scalar1=-1.0, scalar2=None, op0=ALU.mult, **rw8)
                    op("dve", "tensor_tensor", out=R(1), in0=R(1), in1=R(2), op=ALU.max, **rw8)
                    op("dve", "tensor_scalar", out=R(1), in0=R(1), scalar1=1.0, scalar2=None, op0=ALU.max, **rw8)
                    op("dve", "reciprocal", out=R(2), in_=R(1), **rw8)
                    op("dve", "tensor_tensor", reads=[s8c.b, get.b], writes=[s8c.b], out=R(3), in0=R(2), in1=get4, op=ALU.mult)
                    op("dve", "tensor_tensor", out=R(4), in0=R(0), in1=R(3), op=ALU.mult, **rw8)
                    op("dve", "tensor_tensor", out=R(4), in0=R(4), in1=R(3), op=ALU.mult, **rw8)
                    op("dve", "tensor_scalar", out=R(4), in0=R(4), scalar1=1.0 / 256, scalar2=EPS, op0=ALU.mult, op1=ALU.add, **rw8)
                    op("act", "activation", out=R(5), in_=R(4), func=AF.Sqrt, **rw8)
                    op("dve", "reciprocal", out=R(6), in_=R(5), **rw8)
                    op("dve", "tensor_tensor", out=R(7), in0=R(6), in1=R(3), op=ALU.mult, **rw8)
                    for h in range(4):
                        hb_ = hmb.next()
                        op("dve", "scalar_tensor_tensor", reads=[rwc.b, s8c.b, smo.b], writes=[hb_.b], out=hb_.t, in0=rwc.t[:, h, 0:256],
                           scalar=s8c.t[:, 28 + h:29 + h], in1=smo.t[:, cl, h * 256:(h + 1) * 256], op0=ALU.mult, op1=ALU.mult)

                        def _tr(hb_=hb_, hs=hs, h=h, cl=cl):
                            tp2 = tps.next()
                            for j in range(2):
                                op("pe", "transpose", reads=[hb_.b, cbs.b], writes=[tp2.b], out=tp2.t[:, j * 128:(j + 1) * 128],
                                   in_=hb_.t[:, j * 128:(j + 1) * 128], identity=ident)
                            for j in range(2):
                                fc = h * 2 + j
                                op("dve", "tensor_tensor", reads=[tp2.b, cs.b], writes=[hs.b], out=hs.t[:, fc, cl * 128:(cl + 1) * 128],
                                   in0=tp2.t[:, j * 128:(j + 1) * 128], in1=mk(CS, C_GMO + fc, [[0, 128]]), op=ALU.mult)
                        pendB.append(_tr)
            flushB()
            if ownc:
                for fc in range(8):
                    op("sp", "dma_start", reads=[hs.b], writes=[D_hmT], dma=hs.d,
                       out=hmT_s[fc * 128:(fc + 1) * 128, ot0:ot0 + CH * 128], in_=hs.t[:, fc, :])
    if debug == "B":
        return finish(P, nc, [D_hmT])

    with Phase(P):
        ccs = P.slot("cstc", [128, CC_TOT], BF16, dma=True)
        op("pool", "dma_start", writes=[ccs.b], dma=ccs.d, out=ccs.t, in_=cstc[:, :])
        CM = lambda kt: ccs.t[:, CC_CM + kt * 256:CC_CM + (kt + 1) * 256]
        CMQ = lambda hf: ccs.t[:, CC_CMQ + hf * 256:CC_CMQ + (hf + 1) * 256]
        ESEL = lambda n: ccs.t[:, CC_ESEL + n * 128:CC_ESEL + (n + 1) * 128]
        kTh = P.ring("kTh", 2, [128, S2], BF16, dma=True)
        qTh = P.ring("qTh", 2, [128, T], BF16, dma=True)
        Vh = P.ring("Vh", 2, [128, NT, 130], BF16, dma=True)
        for s_ in Vh.slots:
            op("dve", "memset", writes=[s_.b], ap=s_.t, constant=1.0)
        hast = P.ring("hast", 2, [128, T], BF16, dma=True)
        Bsb = P.ring("Bsb", 2, [128, 256], BF16)
        for s_ in Bsb.slots:
            op("dve", "memset", writes=[s_.b], ap=s_.t, constant=0.0)
        Bq = P.ring("Bq", 4, [128, 64], BF16)
        for s_ in Bq.slots:
            op("dve", "memset", writes=[s_.b], ap=s_.t, constant=0.0)
        scm = P.ring("scm", 4, [128, 32], F32)
        som = P.ring("som", 4, [128, 256], F32)
        c8 = P.ring("c8", 8, [128, 16], F32)
        nmb = P.ring("nmb", 8, [128, 2], BF16)
        Pt = P.ring("Pt", 3, [128, 512], BF16)
        hq = P.ring("hq", 2, [128, 128], BF16)
        oacc = [aux[0], aux[1]]
        SCALE = 128.0 ** -0.5
        heads = {}

        def load_head(h):
            kT = kTh.next(); qT = qTh.next(); Vt = Vh.next(); ha = hast.next()
            op("sp", "dma_start", reads=[D_akT], writes=[kT.b], dma=kT.d, out=kT.t, in_=akT_s[h * 128:(h + 1) * 128, :])
            op("sp", "dma_start", reads=[D_aqT], writes=[qT.b], dma=qT.d, out=qT.t, in_=aqT_s[h * 128:(h + 1) * 128, :])
            for n0 in range(0, NT, 4):
                op("sp", "dma_start", reads=[D_av], writes=[Vt.b], dma=Vt.d, out=Vt.t[:, n0:n0 + 4, 0:128],
                   in_=av_s[n0 * 128:(n0 + 4) * 128, h * 128:(h + 1) * 128].rearrange("(n p) c -> p n c", p=128))
            heads[h] = (kT, qT, Vt, ha)

        def prologue(h, j):
            kT, qT, Vt, ha = heads[h]
            q0 = j * 256
            Bs = Bsb.next()
            bqs = []
            for hf in range(2):
                qsl = qT.t[:, q0 + hf * 128:q0 + (hf + 1) * 128]
                a = acc.next()
                op("pe", "matmul", reads=[qT.b, kmean_b.b], writes=[a.b], out=a.t[:, 0:NB], lhsT=qsl, rhs=kmean_b.t[:, h, :], start=True, stop=True)
                a2 = acc.next()
                okeys = kT.t[:, (NBP + j) * 256:(NBP + j + 1) * 256]
                op("pe", "matmul", reads=[qT.b, kT.b], writes=[a2.b], out=a2.t[:, 0:256], lhsT=qsl, rhs=okeys, start=True, stop=True)
                sc = scm.next(); cc8 = c8.next(); nm = nmb.next(); bq = Bq.next()
                op("dve", "tensor_tensor", reads=[a.b, cs.b], writes=[sc.b], out=sc.t[:, 0:NB], in0=a.t[:, 0:NB],
                   in1=CS[:, C_PMASK + j * NB:C_PMASK + (j + 1) * NB], op=ALU.add)
                op("dve", "max", reads=[sc.b], writes=[cc8.b], out=cc8.t[:, 0:8], in_=sc.t[:, 0:NB])
                op("dve", "tensor_scalar", reads=[cc8.b], writes=[cc8.b], out=cc8.t[:, 8:9], in0=cc8.t[:, 2:3], scalar1=-1e29, scalar2=None, op0=ALU.max)
                op("dve", "tensor_scalar", reads=[sc.b, cc8.b], writes=[sc.b], out=sc.t[:, 0:NB], in0=sc.t[:, 0:NB], scalar1=cc8.t[:, 8:9],
                   scalar2=None, op0=ALU.is_ge)
                so = som.next()
                op("dve", "tensor_tensor", reads=[a2.b, ccs.b], writes=[so.b], out=so.t, in0=a2.t[:, 0:256], in1=CMQ(hf), op=ALU.add)
                op("dve", "tensor_reduce", reads=[so.b], writes=[cc8.b], out=cc8.t[:, 9:10], in_=so.t, axis=AX.X, op=ALU.max)
                op("dve", "tensor_scalar", reads=[cc8.b], writes=[nm.b], out=nm.t[:, 0:1], in0=cc8.t[:, 9:10], scalar1=-1.0, scalar2=None, op0=ALU.mult)
                op("dve", "tensor_scalar", reads=[nm.b], writes=[cc8.b], out=cc8.t[:, 10:11], in0=nm.t[:, 0:1], scalar1=NEGBIG, scalar2=None, op0=ALU.add)
                op("dve", "tensor_scalar", reads=[sc.b, cc8.b], writes=[bq.b], out=bq.t[:, 0:NB], in0=sc.t[:, 0:NB], scalar1=-NEGBIG,
                   scalar2=cc8.t[:, 10:11], op0=ALU.mult, op1=ALU.add)
                op("dve", "tensor_copy", reads=[nm.b], writes=[bq.b], out=bq.t[:, 32:33], in_=nm.t[:, 0:1])
                bqs.append(bq)

            def part2(bqs=bqs, Bs=Bs):
                for hf, bq in enumerate(bqs):
                    tp = tps.next()
                    op("pe", "transpose", reads=[bq.b, cbs.b], writes=[tp.b], out=tp.t[0:64, 0:128], in_=bq.t, identity=ident)
                    op("act", "activation", reads=[tp.b], writes=[Bs.b], out=Bs.t[0:64, hf * 128:(hf + 1) * 128], in_=tp.t[0:64, 0:128], func=AF.Copy)
            return Bs, part2

        seq = [(h, j) for h in range(8) for j in range(NBO)]
        load_head(0)
        Bs_cur, p2 = prologue(0, 0)
        p2()
        for idx, (h, j) in enumerate(seq):
            if j == 0 and h + 1 < 8:
                load_head(h + 1)
            kT, qT, Vt, ha = heads[h]
            q0 = j * 256
            Bs = Bs_cur
            nblk = NBP + j + 1
            o0, o1 = oacc[0], oacc[1]
            qs2 = qT.t[:, q0:q0 + 256]
            nxt = seq[idx + 1] if idx + 1 < len(seq) else None
            nxt_state = {}

            def emit_S(n, kT=kT, qT=qT, Bs=Bs, nblk=nblk, qs2=qs2):
                is_own = n == nblk - 1
                a = acc.next()
                for kt in range(2):
                    kti = n * 2 + kt
                    o_ = a.t[:, kt * 256:(kt + 1) * 256]
                    op("pe", "matmul", reads=[kT.b, qT.b], writes=[a.b], out=o_, lhsT=kT.t[:, kti * 128:(kti + 1) * 128], rhs=qs2, start=True, stop=False)
                    if is_own:
                        op("pe", "matmul", reads=[cbs.b, ccs.b], writes=[a.b], out=o_, lhsT=ident, rhs=CM(kt), start=False, stop=False)
                    op("pe", "matmul", reads=[ccs.b, Bs.b], writes=[a.b], out=o_, lhsT=ESEL(32 if is_own else n), rhs=Bs.t, start=False, stop=True)
                pt = Pt.next()
                op("act", "activation", reads=[a.b], writes=[pt.b], out=pt.t, in_=a.t, func=AF.Exp, scale=SCALE)
                return pt

            def emit_PV(n, pt, Vt=Vt, nblk=nblk, o0=o0, o1=o1):
                is_own = n == nblk - 1
                for kt in range(2):
                    kti = n * 2 + kt
                    first = (n == 0 and kt == 0)
                    last = (is_own and kt == 1)
                    for hf, o in ((0, o0), (1, o1)):
                        op("pe", "matmul", reads=[pt.b, Vt.b], writes=[o.b], out=o.t[:, 0:129], lhsT=pt.t[:, kt * 256 + hf * 128:kt * 256 + (hf + 1) * 128],
                           rhs=Vt.t[:, kti, 0:129], start=first, stop=last)

            emit_conv_job(gate=[Bs.b])
            prev = None
            for n in range(nblk):
                pt = emit_S(n)
                if n == 1 and nxt is not None:
                    Bs_cur, nxt_state["p2"] = prologue(*nxt)
                if n == min(6, nblk - 1) and "p2" in nxt_state:
                    nxt_state.pop("p2")()
                if prev is not None:
                    emit_PV(*prev)
                prev = (n, pt)
            emit_PV(*prev)
            if "p2" in nxt_state:
                nxt_state.pop("p2")()
            for hf, o in ((0, o0), (1, o1)):
                cc8 = c8.next(); hq_ = hq.next()
                op("dve", "reciprocal", reads=[o.b], writes=[cc8.b], out=cc8.t[:, 0:1], in_=o.t[:, 128:129])
                op("dve", "tensor_scalar", reads=[o.b, cc8.b], writes=[hq_.b], out=hq_.t, in0=o.t[:, 0:128], scalar1=cc8.t[:, 0:1], scalar2=None, op0=ALU.mult)
                tp = tps.next()
                op("pe", "transpose", reads=[hq_.b, cbs.b], writes=[tp.b], out=tp.t[:, 0:128], in_=hq_.t, identity=ident)
                op("act", "activation", reads=[tp.b], writes=[ha.b], out=ha.t[:, q0 + hf * 128:q0 + (hf + 1) * 128], in_=tp.t[:, 0:128], func=AF.Copy)
            if j == NBO - 1:
                op("sp", "dma_start", reads=[ha.b], writes=[D_haT], dma=ha.d, out=haT_s[h * 128:(h + 1) * 128, :], in_=ha.t)
    while emit_conv_job():
        pass
    if debug == "C":
        return finish(P, nc, [D_haT, D_hmT])

    with Phase(P):
        TD = 512
        NTD = TD // 128
        x2 = [P.slot("x2", [128, D], F32, dma=True) for _ in range(NTD)]
        hT = P.slot("hTd", [128, KC, TD], BF16)
        wt = P.ring("wtd", 3, [128, KC, 512], BF16, dma=True)
        sgr = P.ring("sgr", 4, [128, 512], BF16, dma=True)
        tmpf = P.ring("tmpf", 3, [128, 512], F32)
        rlu = P.ring("rlu", 2, [128, 512], BF16)
        xn = P.ring("xnd", 2, [128, D], BF16)
        gf = P.slot("gfin", [128, D], F32, dma=True)
        pT = P.slot("pT", [128, 2, TD], BF16)
        pin = P.ring("pin", 2, [128, 256], F32, dma=True)
        pbf = P.ring("pbf", 2, [128, 256], BF16)
        op("pool", "dma_start", writes=[gf.b], dma=gf.d, out=gf.t, in_=gfin[:, :])
        for blk in range(T // TD):
            o0 = blk * TD
            for t in range(NTD):
                op("pool", "dma_start", writes=[x2[t].b], dma=x2[t].d, out=x2[t].t, in_=xs[T + o0 + t * 128:T + o0 + (t + 1) * 128, :])
            sub1 = Phase(P)
            sub1.__enter__()
            mT = P.slot("mT", [128, KC, TD], BF16)
            hmT = P.slot("hmTd", [128, 8, TD], BF16, dma=True)
            haT = P.slot("haTd", [128, 8, TD], BF16, dma=True)
            for c0_ in (0, 4):
                op("pool", "dma_start", reads=[D_hmT], writes=[hmT.b], dma=hmT.d, out=hmT.t[:, c0_:c0_ + 4, :],
                   in_=hmT_s[c0_ * 128:(c0_ + 4) * 128, o0:o0 + TD].rearrange("(c p) t -> p c t", p=128))
                op("pool", "dma_start", reads=[D_haT], writes=[haT.b], dma=haT.d, out=haT.t[:, c0_:c0_ + 4, :],
                   in_=haT_s[c0_ * 128:(c0_ + 4) * 128, o0:o0 + TD].rearrange("(c p) t -> p c t", p=128))
            for wgi in range(4):
                wm = load_w(wt, wb["up_m"], 0, 8, wgi * 512, 512, dep=D_wb["up_m"], q="sp")
                wa = load_w(wt, wb["up_a"], 0, 8, wgi * 512, 512, dep=D_wb["up_a"], q="sp")
                for cc in range(4):
                    fc = wgi * 4 + cc
                    am = acc.next()
                    for k in range(8):
                        op("pe", "matmul", reads=[wm.b, hmT.b], writes=[am.b], out=am.t, lhsT=wm.t[:, k, cc * 128:(cc + 1) * 128], rhs=hmT.t[:, k, :], start=(k == 0), stop=(k == 7))
                    aa = acc.next()
                    for k in range(8):
                        op("pe", "matmul", reads=[wa.b, haT.b], writes=[aa.b], out=aa.t, lhsT=wa.t[:, k, cc * 128:(cc + 1) * 128], rhs=haT.t[:, k, :], start=(k == 0), stop=(k == 7))
                    s1 = sgr.next(); s2 = sgr.next(); t1 = tmpf.next(); t2 = tmpf.next()
                    op("pool", "dma_start", reads=[D_sgm], writes=[s1.b], dma=s1.d, out=s1.t, in_=sgmT_s[fc * 128:(fc + 1) * 128, o0:o0 + TD])
                    op("pool", "dma_start", reads=[D_sga], writes=[s2.b], dma=s2.d, out=s2.t, in_=sgaT_s[fc * 128:(fc + 1) * 128, o0:o0 + TD])
                    op("dve", "tensor_tensor", reads=[am.b, s1.b], writes=[t1.b], out=t1.t, in0=am.t, in1=s1.t, op=ALU.mult)
                    op("dve", "tensor_tensor", reads=[aa.b, s2.b], writes=[t2.b], out=t2.t, in0=aa.t, in1=s2.t, op=ALU.mult)
                    op("dve", "tensor_tensor", reads=[t1.b, t2.b], writes=[mT.b], out=mT.t[:, fc, :], in0=t1.t, in1=t2.t, op=ALU.add)
            for cgi in range(4):
                w = load_w(wt, wb["out"], 0, KC, cgi * 512, 512, dep=D_wb["out"], q="sp")
                for t in range(NTD):
                    a = acc.next()
                    for k in range(KC):
                        op("pe", "matmul", reads=[mT.b, w.b], writes=[a.b], out=a.t, lhsT=mT.t[:, k, t * 128:(t + 1) * 128], rhs=w.t[:, k, :], start=(k == 0), stop=(k == KC - 1))
                    xs_ = x2[t].t[:, cgi * 512:(cgi + 1) * 512]
                    op("dve", "tensor_tensor", reads=[a.b, x2[t].b], writes=[x2[t].b], out=xs_, in0=a.t, in1=xs_, op=ALU.add)
            sub1.__exit__(None, None, None)
            norm_transpose([(x2[t].t, x2[t].b) for t in range(NTD)], C_GMLP, hT, xn, 2)
            sub2 = Phase(P)
            sub2.__enter__()
            uT = P.slot("uT", [128, KC, TD], BF16)
            for qf in range(4):
                for wgi in range(4):
                    w1 = load_w(wt, wb["ff1"], 0, KC, qf * 2048 + wgi * 512, 512, dep=D_wb["ff1"], q="sp")
                    for cc in range(4):
                        fcl = wgi * 4 + cc
                        a = acc.next()
                        for k in range(KC):
                            op("pe", "matmul", reads=[w1.b, hT.b], writes=[a.b], out=a.t, lhsT=w1.t[:, k, cc * 128:(cc + 1) * 128], rhs=hT.t[:, k, :], start=(k == 0), stop=(k == KC - 1))
                        r_ = rlu.next()
                        op("act", "activation", reads=[a.b], writes=[r_.b], out=r_.t, in_=a.t, func=AF.Relu)
                        op("act", "activation", reads=[r_.b], writes=[uT.b], out=uT.t[:, fcl, :], in_=r_.t, func=AF.Square)
                for cgi in range(4):
                    w2 = load_w(wt, wb["ff2"], qf * 2048, KC, cgi * 512, 512, dep=D_wb["ff2"], q="sp")
                    for t in range(NTD):
                        a = acc.next()
                        for k in range(KC):
                            op("pe", "matmul", reads=[uT.b, w2.b], writes=[a.b], out=a.t, lhsT=uT.t[:, k, t * 128:(t + 1) * 128], rhs=w2.t[:, k, :], start=(k == 0), stop=(k == KC - 1))
                        xs_ = x2[t].t[:, cgi * 512:(cgi + 1) * 512]
                        op("dve", "tensor_tensor", reads=[a.b, x2[t].b], writes=[x2[t].b], out=xs_, in0=a.t, in1=xs_, op=ALU.add)
            sub2.__exit__(None, None, None)
            norm_transpose([(x2[t].t, x2[t].b) for t in range(NTD)], C_GPLE, hT, xn, 2)
            for t in range(NTD):
                pi = pin.next(); pb = pbf.next()
                op("pool", "dma_start", writes=[pi.b], dma=pi.d, out=pi.t, in_=pp_in[o0 + t * 128:o0 + (t + 1) * 128, :])
                op("act", "activation", reads=[pi.b], writes=[pb.b], out=pb.t, in_=pi.t, func=AF.Copy)
                tp = tps.next()
                for k in range(2):
                    op("pe", "transpose", reads=[pb.b, cbs.b], writes=[tp.b], out=tp.t[:, k * 128:(k + 1) * 128], in_=pb.t[:, k * 128:(k + 1) * 128], identity=ident)
                op("dve", "tensor_copy", reads=[tp.b], writes=[pT.b], out=mk(pT.t, t * 128, [[TD, 2], [1, 128]]), in_=mk(tp.t, 0, [[128, 2], [1, 128]]))
            for cgi in range(4):
                wgt = load_w(wt, wb["pg"], 0, KC, cgi * 512, 512, dep=D_wb["pg"], q="sp")
                wpp = load_w(wt, wb["pp"], 0, 2, cgi * 512, 512, dep=D_wb["pp"], q="sp")
                for t in range(NTD):
                    ag = acc.next()
                    for k in range(KC):
                        op("pe", "matmul", reads=[hT.b, wgt.b], writes=[ag.b], out=ag.t, lhsT=hT.t[:, k, t * 128:(t + 1) * 128], rhs=wgt.t[:, k, :], start=(k == 0), stop=(k == KC - 1))
                    ap_ = acc.next()
                    for k in range(2):
                        op("pe", "matmul", reads=[pT.b, wpp.b], writes=[ap_.b], out=ap_.t, lhsT=pT.t[:, k, t * 128:(t + 1) * 128], rhs=wpp.t[:, k, :], start=(k == 0), stop=(k == 1))
                    t1 = tmpf.next(); t2 = tmpf.next()
                    op("act", "activation", reads=[ag.b], writes=[t1.b], out=t1.t, in_=ag.t, func=AF.Sigmoid)
                    op("dve", "tensor_tensor", reads=[ap_.b, t1.b], writes=[t2.b], out=t2.t, in0=ap_.t, in1=t1.t, op=ALU.mult)
                    xs_ = x2[t].t[:, cgi * 512:(cgi + 1) * 512]
                    op("dve", "tensor_tensor", reads=[t2.b, x2[t].b], writes=[x2[t].b], out=xs_, in0=t2.t, in1=xs_, op=ALU.add)
            for t in range(NTD):
                n_ = xn.next()
                s = norm_stats(x2[t].t, x2[t].b, D, n_.t, n_.b)
                op("dve", "scalar_tensor_tensor", reads=[x2[t].b, s.b, gf.b], writes=[x2[t].b], out=x2[t].t, in0=x2[t].t, scalar=s.t[:, 3:4],
                   in1=gf.t, op0=ALU.mult, op1=ALU.mult)
                op("pool", "dma_start", reads=[x2[t].b], writes=[D_out], dma=x2[t].d, out=out[o0 + t * 128:o0 + (t + 1) * 128, :], in_=x2[t].t)
    return finish(P, nc, [D_out])


def finish(P, nc, dbufs):
    P.wait_all("sp", dbufs)
    stats = P.finalize()
    stats["arena_peak_bytes"] = P.apeak * 2
    P.stack.close()
    nc._stats = stats
    return nc


def host_inputs(T, x_b, p_b, pos_b, half, prm):
    S2 = 2 * T
    NT = S2 // 128
    NB = S2 // 256
    NBP = NB // 2
    NBO = NB - NBP
    if half == 1:
        xs = np.ascontiguousarray(x_b)
        ps = pos_b
    else:
        xs = np.concatenate([np.zeros((T, D), np.float32), x_b[:T]], axis=0)
        ps = np.concatenate([np.zeros((T,), np.int32), pos_b[:T]])
    p_own = np.ascontiguousarray(p_b[half * T:(half + 1) * T])
    NCST = C_PMASK + NBO * NB
    cst = np.zeros((128, NCST), np.float32)
    cst[:, C_GAT:C_GAT + 16] = prm["attn_norm"].reshape(16, 128).T
    cst[:, C_GMLP:C_GMLP + 16] = prm["mlp_norm"].reshape(16, 128).T
    cst[:, C_GPLE:C_GPLE + 16] = prm["ple_norm"].reshape(16, 128).T
    cst[:, C_GMO:C_GMO + 8] = prm["m_out_norm"].reshape(8, 128).T
    cw = prm["conv_w"].reshape(4, 8, 128)
    cst[:, C_CW:C_CW + 32] = cw.transpose(2, 0, 1).reshape(128, 32)
    cst[:, C_CB:C_CB + 8] = prm["conv_b"].reshape(8, 128).T
    cst[:, C_BIF:C_BIF + 8] = np.broadcast_to(prm["b_if"].reshape(1, 8), (128, 8))
    half_ = 16
    invf = (np.float32(500000.0) ** (-np.arange(half_, dtype=np.float32) * np.float32(2.0) / np.float32(32))).astype(np.float32)
    cst[:, C_INVF:C_INVF + 16] = invf[None, :]
    cst[:, C_PVAL] = float(half)
    tri, cb, cc = host_consts(T)
    cst[:, C_TRI:C_TRI + 128] = tri
    cst[:, C_ONES:C_ONES + 128] = 1.0
    pm = np.zeros((NBO, NB), np.float32)
    for j in range(NBO):
        pm[j, NBP + j:] = -1e30
        if half == 0:
            pm[j, :NBP] = -1e30
    cst[:, C_PMASK:] = pm.reshape(1, -1)
    d = dict(xs=xs, p=p_own, pos=np.ascontiguousarray(ps.reshape(NT, 128).T.astype(np.int32)), cst=cst, cstb=cb, cstc=cc,
             gfin=np.ascontiguousarray(np.broadcast_to(prm["final_norm"].reshape(1, D), (128, D))).astype(np.float32))
    return d


_NC_CACHE = {}


def kernel(x, p, positions, attn_norm, w_in, b_if, conv_w, conv_b, m_out_norm, w_up_m, w_up_a,
           w_out, mlp_norm, w_ff1, w_ff2, ple_norm, w_ple_gate, w_ple_proj, final_norm):
    x = np.asarray(x, np.float32); p = np.asarray(p, np.float32); positions = np.asarray(positions, np.int32)
    B, S, _ = x.shape
    T = S // 2
    prm = dict(attn_norm=np.asarray(attn_norm, np.float32)[0], mlp_norm=np.asarray(mlp_norm, np.float32)[0],
               ple_norm=np.asarray(ple_norm, np.float32)[0], m_out_norm=np.asarray(m_out_norm, np.float32)[0],
               conv_w=np.asarray(conv_w, np.float32)[0], conv_b=np.asarray(conv_b, np.float32)[0],
               b_if=np.asarray(b_if, np.float32)[0], final_norm=np.asarray(final_norm, np.float32))
    wts = dict(w_in=np.ascontiguousarray(np.asarray(w_in, np.float32)[0]), w_up_m=np.ascontiguousarray(np.asarray(w_up_m, np.float32)[0]),
               w_up_a=np.ascontiguousarray(np.asarray(w_up_a, np.float32)[0]), w_out=np.ascontiguousarray(np.asarray(w_out, np.float32)[0]),
               w_ff1=np.ascontiguousarray(np.asarray(w_ff1, np.float32)[0]), w_ff2=np.ascontiguousarray(np.asarray(w_ff2, np.float32)[0]),
               w_pg=np.ascontiguousarray(np.asarray(w_ple_gate, np.float32)[0]), w_pp=np.ascontiguousarray(np.asarray(w_ple_proj, np.float32)[0]))
    ncores = 2 * B
    if T not in _NC_CACHE:
        _NC_CACHE[T] = build(T)
    nc = _NC_CACHE[T]
    in_maps = []
    for c in range(ncores):
        b, half = c // 2, c % 2
        d = host_inputs(T, x[b], p[0, b], positions[b], half, prm)
        d.update(wts)
        in_maps.append(d)
    res = run_bass_kernel_spmd(nc, in_maps, core_ids=list(range(ncores)))
    outp = np.empty((B, S, D), np.float32)
    for c in range(ncores):
        b, half = c // 2, c % 2
        outp[b, half * T:(half + 1) * T] = res.results[c]["out"]
    return outp
```

```python
import contextlib
import math
import numpy as np
import concourse.bass as bass
import concourse.mybir as mybir
from concourse.bass_utils import run_bass_kernel_spmd

F32 = mybir.dt.float32
BF16 = mybir.dt.bfloat16
I32 = mybir.dt.int32
AF = mybir.ActivationFunctionType
ALU = mybir.AluOpType
AX = mybir.AxisListType

D = 2048
KC = 16
DFF = 8192
INW = 10248
EPS = 1e-6
NEGBIG = -30000.0
O_MQ, O_MK, O_MV, O_MO, O_MI, O_AQ, O_AK, O_AV, O_GM, O_GA = 0, 512, 1024, 2048, 3072, 3080, 4104, 5128, 6152, 8200


class Buf:
    __slots__ = ("name", "w", "r")

    def __init__(self, name=""):
        self.name = name
        self.w = None
        self.r = []


class DramBuf:
    __slots__ = ("name", "ws")

    def __init__(self, name=""):
        self.name = name
        self.ws = {}


class DmaSem:
    __slots__ = ("name", "count", "h")

    def __init__(self, name):
        self.name = name
        self.count = 0
        self.h = None


class Slot:
    __slots__ = ("t", "b", "d")

    def __init__(self, t, b, d):
        self.t, self.b, self.d = t, b, d


class Ring:
    def __init__(self, slots):
        self.slots = slots
        self.i = 0

    def next(self):
        s = self.slots[self.i % len(self.slots)]
        self.i += 1
        return s


def mk(base, off, dims):
    return bass.AP(base.tensor, base.offset + off, [[base.ap[0][0], 128]] + [list(d) for d in dims])


class Prog:
    ENGS = ("pe", "act", "dve", "pool", "sp")

    def __init__(self, nc):
        self.nc = nc
        self.ops = {e: [] for e in self.ENGS}
        self.serial = {e: 0 for e in self.ENGS}
        self.waited = {e: {} for e in self.ENGS}
        self.dsems = []
        self.stack = contextlib.ExitStack()
        self.nn = 0

    def init_arena(self, nbytes_per_part):
        self.arena_n = nbytes_per_part // 2
        self.arena = self.stack.enter_context(self.nc.sbuf_tensor("arena", [128, self.arena_n], BF16))
        self.aoff = 0
        self.apeak = 0

    def carve(self, shape, dt):
        n = 1
        for s_ in shape[1:]:
            n *= s_
        units = n if dt == BF16 else 2 * n
        self.aoff = (self.aoff + 15) // 16 * 16
        a = self.arena[:, self.aoff:self.aoff + units]
        self.aoff += units
        self.apeak = max(self.apeak, self.aoff)
        assert self.aoff <= self.arena_n, "SBUF arena overflow: %d > %d" % (self.aoff * 2, self.arena_n * 2)
        if dt != BF16:
            a = a.bitcast(dt)
        if len(shape) > 2:
            dims = []
            st_ = n
            for s_ in shape[1:]:
                st_ //= s_
                dims.append([st_, s_])
            a = bass.AP(a.tensor, a.offset, [[a.ap[0][0], 128]] + dims)
        return a

    def slot(self, name, shape, dt, dma=False):
        self.nn += 1
        nm = "%s_%d" % (name, self.nn)
        return Slot(self.carve(shape, dt), Buf(nm), self.dsem(nm) if dma else None)

    def ring(self, name, n, shape, dt, dma=False):
        return Ring([self.slot(name, shape, dt, dma) for _ in range(n)])

    def dsem(self, name):
        d = DmaSem(name)
        self.dsems.append(d)
        return d

    def _deps(self, reads, writes):
        deps = {}

        def add(tok):
            if tok is None:
                return
            k, v = tok
            if deps.get(k, 0) < v:
                deps[k] = v

        for b in reads:
            if isinstance(b, DramBuf):
                for k, v in b.ws.items():
                    add((k, v))
            else:
                add(b.w)
        for b in writes:
            if isinstance(b, DramBuf):
                continue
            add(b.w)
            for t in b.r:
                add(t)
        return deps

    def emit(self, eng, fn, reads=(), writes=(), dma=None):
        deps = self._deps(reads, writes)
        waits = []
        wd = self.waited[eng]
        for k, v in deps.items():
            if k == eng and eng in ("pe", "sp"):
                continue
            if wd.get(k, 0) >= v:
                continue
            wd[k] = v
            waits.append((k, v))
        if dma is not None:
            dma.count += 16
            tok = (dma, dma.count)
            ser = None
        else:
            self.serial[eng] += 1
            ser = self.serial[eng]
            tok = (eng, ser)
        self.ops[eng].append((waits, fn, ser, dma))
        for b in reads:
            if not isinstance(b, DramBuf):
                b.r.append(tok)
        for b in writes:
            if isinstance(b, DramBuf):
                k, v = tok
                if b.ws.get(k, 0) < v:
                    b.ws[k] = v
            else:
                b.w = tok
                b.r = []
        return tok

    def op(self, eng, method, reads=(), writes=(), dma=None, **kw):
        return self.emit(eng, lambda e, m=method, kw=kw: getattr(e, m)(**kw), reads, writes, dma)

    def wait_all(self, eng, bufs):
        deps = self._deps(bufs, ())
        waits = []
        wd = self.waited[eng]
        for k, v in deps.items():
            if wd.get(k, 0) >= v:
                continue
            wd[k] = v
            waits.append((k, v))
        self.ops[eng].append((waits, None, None, None))

    def finalize(self):
        nc = self.nc
        needed = {e: set() for e in self.ENGS}
        for e in self.ENGS:
            for waits, fn, ser, dma in self.ops[e]:
                for k, v in waits:
                    if isinstance(k, str):
                        needed[k].add(v)
        vmap = {e: {v: i + 1 for i, v in enumerate(sorted(needed[e]))} for e in self.ENGS}
        esem = {e: self.stack.enter_context(nc.semaphore("s_" + e)) for e in self.ENGS}
        for d in self.dsems:
            if d.count:
                d.h = self.stack.enter_context(nc.semaphore("d_" + d.name))
        handles = {"pe": "tensor", "act": "scalar", "dve": "vector", "pool": "gpsimd", "sp": "sync"}
        stats = {e: len(self.ops[e]) for e in self.ENGS}
        stats["ndsem"] = sum(1 for d in self.dsems if d.count)
        with nc.Block() as block:
            for e in self.ENGS:
                ops = self.ops[e]

                def body(h, ops=ops, e=e):
                    for waits, fn, ser, dma in ops:
                        for k, v in waits:
                            if isinstance(k, str):
                                h.wait_ge(esem[k], vmap[k][v])
                            else:
                                h.wait_ge(k.h, v)
                        if fn is None:
                            continue
                        ins = fn(h)
                        if dma is not None:
                            ins.then_inc(dma.h, 16)
                        elif ser in vmap[e]:
                            ins.then_inc(esem[e], 1)

                getattr(block, handles[e])(body)
        return stats


def ps_alloc(P, name, shape, dt=F32):
    return P.stack.enter_context(P.nc.psum_tensor(name, list(shape), dt))


def barrier(P, bufs):
    deps = {}
    for b in bufs:
        for tok in ([b.w] if b.w else []) + list(b.r):
            k, v = tok
            if deps.get(k, 0) < v:
                deps[k] = v
    for e in P.ENGS:
        waits = []
        wd = P.waited[e]
        for k, v in deps.items():
            if wd.get(k, 0) >= v:
                continue
            wd[k] = v
            waits.append((k, v))
        P.ops[e].append((waits, None, None, None))


class Phase:
    def __init__(self, P):
        self.P = P

    def __enter__(self):
        self.mark = self.P.aoff
        self.bufs = []
        self._slot = self.P.slot
        P = self.P

        def slot(name, shape, dt, dma=False, _o=self._slot, _s=self):
            s = _o(name, shape, dt, dma)
            _s.bufs.append(s.b)
            return s
        P.slot = slot
        return self

    def __exit__(self, *a):
        self.P.slot = self._slot
        barrier(self.P, self.bufs)
        self.P.aoff = self.mark
        return False


C_GAT, C_GMLP, C_GPLE, C_GMO, C_CW, C_CB, C_BIF, C_INVF, C_PVAL, C_TRI, C_ONES, C_PMASK = 0, 16, 32, 48, 56, 88, 96, 104, 120, 121, 249, 377
CB_ID, CB_TRI, CB_TOT = 0, 128, 256
CC_CM, CC_CMQ, CC_ESEL = 0, 512, 1024
CC_TOT = 1024 + 33 * 128


def host_consts(T):
    tri = (np.arange(128)[:, None] <= np.arange(128)[None, :]).astype(np.float32)
    cb = np.zeros((128, CB_TOT), np.float32)
    cb[:, CB_ID:CB_ID + 128] = np.eye(128, dtype=np.float32)
    cb[:, CB_TRI:CB_TRI + 128] = tri
    cc = np.zeros((128, CC_TOT), np.float32)
    for kt in range(2):
        key = kt * 128 + np.arange(128)[:, None]
        q = np.arange(256)[None, :]
        cc[:, CC_CM + kt * 256:CC_CM + (kt + 1) * 256] = np.where(key <= q, 0.0, NEGBIG)
        qq = kt * 128 + np.arange(128)[:, None]
        key2 = np.arange(256)[None, :]
        cc[:, CC_CMQ + kt * 256:CC_CMQ + (kt + 1) * 256] = np.where(key2 <= qq, 0.0, NEGBIG)
    for n in range(33):
        cc[n, CC_ESEL + n * 128:CC_ESEL + (n + 1) * 128] = 1.0
    return tri, cb, cc


def build(T, debug=None):
    S2 = 2 * T
    NT = S2 // 128
    NTO = T // 128
    TB = 1024
    NBLK = S2 // TB
    PBLK = NBLK // 2
    NB = S2 // 256
    NBP = NB // 2
    NBO = NB - NBP
    assert NB <= 32
    NCST = C_PMASK + NBO * NB
    okind = "ExternalOutput" if debug else "Internal"

    nc = bass.Bass("TRN2", target_bir_lowering=False)
    dt_in = lambda n, s, d=F32: nc.dram_tensor(n, list(s), d, kind="ExternalInput")
    xs = dt_in("xs", [S2, D]); pp_in = dt_in("p", [T, 256]); pos = dt_in("pos", [128, NT], I32)
    cst = dt_in("cst", [128, NCST]); cstb = dt_in("cstb", [128, CB_TOT]); cstc = dt_in("cstc", [128, CC_TOT])
    gfin = dt_in("gfin", [128, D])
    w_in = dt_in("w_in", [D, INW]); w_up_m = dt_in("w_up_m", [1024, D]); w_up_a = dt_in("w_up_a", [1024, D])
    w_out = dt_in("w_out", [D, D]); w_ff1 = dt_in("w_ff1", [D, DFF]); w_ff2 = dt_in("w_ff2", [DFF, D])
    w_pg = dt_in("w_pg", [D, D]); w_pp = dt_in("w_pp", [256, D])
    out = nc.dram_tensor("out", [T, D], F32, kind="ExternalOutput")
    scr = lambda n, s: nc.dram_tensor(n, list(s), BF16, kind=okind)
    mqT_s = scr("mqT_s", [512, T]); mkT_s = scr("mkT_s", [512, S2]); mv_s = scr("mv_s", [S2, 1024])
    smo_s = scr("smo_s", [T, 1024]); aqT_s = scr("aqT_s", [1024, T]); akT_s = scr("akT_s", [1024, S2])
    av_s = scr("av_s", [S2, 1024]); sgmT_s = scr("sgmT_s", [D, T]); sgaT_s = scr("sgaT_s", [D, T])
    hmT_s = scr("hmT_s", [1024, T]); haT_s = scr("haT_s", [1024, T])
    D_mqT, D_mkT, D_mv, D_smo, D_aqT, D_akT, D_av, D_sgm, D_sga, D_hmT, D_haT, D_out = [DramBuf(n) for n in
        ("mqT", "mkT", "mv", "smo", "aqT", "akT", "av", "sgm", "sga", "hmT", "haT", "out")]
    dbg_out = {}
    wsrc = dict(up_m=w_up_m, up_a=w_up_a, out=w_out, ff1=w_ff1, ff2=w_ff2, pg=w_pg, pp=w_pp)
    wb = {}
    D_wb = {}
    conv_jobs = []
    for nm_, W_ in wsrc.items():
        R_, C_ = W_.shape
        wb[nm_] = nc.dram_tensor("wb_" + nm_, [R_, C_], BF16, kind="Internal")
        D_wb[nm_] = DramBuf("wb_" + nm_)
        a_ = C_ // 2048
        srcf = W_.ap().rearrange("r (a b) -> (r a) b", b=2048) if a_ > 1 else W_.ap()
        dstf = wb[nm_].ap().rearrange("r (a b) -> (r a) b", b=2048) if a_ > 1 else wb[nm_].ap()
        rows = R_ * a_
        for r0 in range(0, rows, 512):
            r1 = min(rows, r0 + 512)
            conv_jobs.append((srcf[r0:r1, :], dstf[r0:r1, :], D_wb[nm_]))

    P = Prog(nc)
    op = P.op
    P.init_arena(172 * 1024)
    cv_sems = [P.dsem("cv%d" % i) for i in range(4)]
    cv_state = {"i": 0}

    def emit_conv_job(gate=()):
        i = cv_state["i"]
        if i >= len(conv_jobs):
            return False
        src_, dst_, db_ = conv_jobs[i]
        cv_state["i"] = i + 1
        op("pool", "dma_start", reads=list(gate), writes=[db_], dma=cv_sems[i % 4], out=dst_, in_=src_)
        return True

    cs = P.slot("cst", [128, NCST], F32, dma=True)
    cbs = P.slot("cstb", [128, CB_TOT], BF16, dma=True)
    CS, CBt = cs.t, cbs.t
    ident = CBt[:, CB_ID:CB_ID + 128]
    tri_bf = CBt[:, CB_TRI:CB_TRI + 128]
    tri_f = CS[:, C_TRI:C_TRI + 128]
    ones_f = CS[:, C_ONES:C_ONES + 128]
    G = P.slot("G", [128, NT, 8], F32)
    kmean = P.slot("kmean", [128, 8, NB], F32)
    kmean_b = P.slot("kmean_b", [128, 8, NB], BF16)
    halo = P.slot("halo", [128, 8, 3], F32)
    sm = P.ring("sm", 4, [128, 4], F32)
    psf = ps_alloc(P, "psf", [128, 6 * 512], F32)
    bank = lambda i: psf[:, i * 512:(i + 1) * 512]
    acc = Ring([Slot(bank(i), Buf("acc%d" % i), None) for i in range(4)])
    aux = [Slot(bank(4 + i), Buf("aux%d" % i), None) for i in range(2)]
    psb = ps_alloc(P, "psb", [128, 2048], BF16)[:, :]
    tps = Ring([Slot(psb[:, i * 1024:i * 1024 + 512], Buf("tp%d" % i), None) for i in range(2)])

    op("sp", "dma_start", writes=[cs.b], dma=cs.d, out=CS, in_=cst[:, :])
    op("pool", "dma_start", writes=[cbs.b], dma=cbs.d, out=CBt, in_=cstb[:, :])
    op("dve", "memset", writes=[halo.b], ap=halo.t, constant=0.0)
    op("dve", "memset", writes=[kmean.b], ap=kmean.t, constant=0.0)

    def load_w(ring, W, r0, nk, c0, ncol, dep=None, q="pool"):
        s = ring.next()
        for k0 in range(0, nk, 4):
            k1 = min(nk, k0 + 4)
            src = W[r0 + k0 * 128:r0 + k1 * 128, c0:c0 + ncol].rearrange("(k p) c -> p k c", p=128)
            op(q, "dma_start", reads=([dep] if dep is not None else []), writes=[s.b], dma=s.d, out=s.t[:, k0:k1, 0:ncol], in_=src)
        return s

    def norm_stats(xa, xb_, width, junk_ap, junk_b):
        s = sm.next()
        op("act", "activation", reads=[xb_], writes=[junk_b, s.b], out=junk_ap, in_=xa, func=AF.Square, accum_out=s.t[:, 0:1])
        op("dve", "tensor_scalar", reads=[s.b], writes=[s.b], out=s.t[:, 1:2], in0=s.t[:, 0:1], scalar1=1.0 / width, scalar2=EPS, op0=ALU.mult, op1=ALU.add)
        op("act", "activation", reads=[s.b], writes=[s.b], out=s.t[:, 2:3], in_=s.t[:, 1:2], func=AF.Sqrt)
        op("dve", "reciprocal", reads=[s.b], writes=[s.b], out=s.t[:, 3:4], in_=s.t[:, 2:3])
        return s

    def norm_transpose(tiles, gcol, hT, xn, grp):
        xns = []
        nt = len(tiles)
        for t, (xa, xb_) in enumerate(tiles):
            n_ = xn.next()
            s = norm_stats(xa, xb_, D, n_.t, n_.b)
            op("dve", "tensor_scalar", reads=[xb_, s.b], writes=[n_.b], out=n_.t, in0=xa, scalar1=s.t[:, 3:4], scalar2=None, op0=ALU.mult)
            xns.append(n_)
            if len(xns) == grp or t == nt - 1:
                t0 = t + 1 - len(xns)
                w = len(xns) * 128
                for k in range(KC):
                    tp = tps.next()
                    for i, n2 in enumerate(xns):
                        op("pe", "transpose", reads=[n2.b, cbs.b], writes=[tp.b], out=tp.t[:, i * 128:(i + 1) * 128],
                           in_=n2.t[:, k * 128:(k + 1) * 128], identity=ident)
                    op("dve", "tensor_tensor", reads=[tp.b, cs.b], writes=[hT.b], out=hT.t[:, k, t0 * 128:t0 * 128 + w],
                       in0=tp.t[:, 0:w], in1=mk(CS, gcol + k, [[0, w]]), op=ALU.mult)
                xns = []

    with Phase(P):
        NF = NT * 16
        sincos = P.slot("sincos", [128, 2, NF], F32)
        with Phase(P):
            posi = P.slot("posi", [128, NT], I32, dma=True)
            posf = P.slot("posf", [128, NT], F32)
            tb = P.slot("tab", [128, 4, NF], F32)
            TT = tb.t
            op("sp", "dma_start", writes=[posi.b], dma=posi.d, out=posi.t, in_=pos[:, :])
            op("dve", "tensor_copy", reads=[posi.b], writes=[posf.b], out=posf.t, in_=posi.t)
            a_pos = mk(posf.t, 0, [[1, NT], [0, 16]])
            a_inv = mk(CS, C_INVF, [[0, NT], [1, 16]])
            op("dve", "tensor_tensor", reads=[posf.b, cs.b], writes=[tb.b], out=mk(TT, 0, [[16, NT], [1, 16]]), in0=a_pos, in1=a_inv, op=ALU.mult)
            TWO_PI = 2.0 * math.pi
            C1 = 6.28125
            C2 = TWO_PI - C1
            MAGIC = 12582912.0
            PI_LO = 3.1415925
            for which, shift in ((0, 0.0), (1, math.pi / 2)):
                rw = dict(reads=[tb.b], writes=[tb.b])
                op("dve", "tensor_scalar", out=TT[:, 3, :], in0=TT[:, 0, :], scalar1=shift, scalar2=None, op0=ALU.add, **rw)
                op("dve", "tensor_scalar", out=TT[:, 1, :], in0=TT[:, 3, :], scalar1=1.0 / TWO_PI, scalar2=None, op0=ALU.mult, **rw)
                op("dve", "tensor_scalar", out=TT[:, 2, :], in0=TT[:, 1, :], scalar1=MAGIC, scalar2=None, op0=ALU.add, **rw)
                op("dve", "tensor_scalar", out=TT[:, 1, :], in0=TT[:, 2, :], scalar1=-MAGIC, scalar2=None, op0=ALU.add, **rw)
                op("dve", "scalar_tensor_tensor", out=TT[:, 2, :], in0=TT[:, 1, :], scalar=-C1, in1=TT[:, 3, :], op0=ALU.mult, op1=ALU.add, **rw)
                op("dve", "scalar_tensor_tensor", out=TT[:, 3, :], in0=TT[:, 1, :], scalar=-C2, in1=TT[:, 2, :], op0=ALU.mult, op1=ALU.add, **rw)
                op("dve", "tensor_scalar", out=TT[:, 3, :], in0=TT[:, 3, :], scalar1=PI_LO, scalar2=-PI_LO, op0=ALU.min, op1=ALU.max, **rw)
                op("act", "activation", reads=[tb.b], writes=[sincos.b], out=sincos.t[:, which, :], in_=TT[:, 3, :], func=AF.Sin)

        if debug == "A0":
            dbg = nc.dram_tensor("dbg_sincos", [128, 2 * NF], F32, kind="ExternalOutput")
            dd = P.dsem("dbg")
            op("sp", "dma_start", reads=[sincos.b], writes=[D_out], dma=dd, out=dbg[:, :], in_=mk(sincos.t, 0, [[1, 2 * NF]]))
            return finish(P, nc, [D_out])

        def tabv(which, tile, nh):
            return mk(sincos.t, which * NF + tile * 16, [[0, nh], [1, 16]])

        xt = P.ring("xt", 2, [128, D], F32, dma=True)
        xn = P.ring("xn", 4, [128, D], BF16)
        hT = P.slot("hT", [128, KC, TB], BF16)
        wt = P.ring("wt", 2, [128, KC, 512], BF16, dma=True)
        wg = P.ring("wg", 2, [128, KC, 8], BF16, dma=True)
        st = P.ring("st", 4, [128, 512], BF16, dma=True)
        rf = P.ring("rf", 2, [128, 512], F32)
        rtmp = P.ring("rtmp", 2, [128, 4, 4, 16], F32)
        rb = P.ring("rb", 2, [128, 512], BF16)
        qk_st = P.ring("qkst", 2, [128, 4, TB], BF16, dma=True)
        zp = P.ring("zp", 2, [128, 516], F32)
        cy = P.ring("cy", 2, [128, 512], F32)
        csg = P.ring("csg", 2, [128, 512], F32)
        cob = P.ring("cob", 2, [128, 512], BF16, dma=True)
        pending = []

        def flush_pending():
            while pending:
                pending.pop(0)()

        pre_x = []
        for blk in range(NBLK):
            own = blk >= PBLK
            tok0 = blk * TB
            otok0 = tok0 - T
            tiles = []
            for t in range(TB // 128):
                if pre_x:
                    xs_ = pre_x.pop(0)
                else:
                    xs_ = xt.next()
                    op("sp", "dma_start", writes=[xs_.b], dma=xs_.d, out=xs_.t, in_=xs[tok0 + t * 128:tok0 + (t + 1) * 128, :])
                tiles.append((xs_.t, xs_.b))
                if debug == "A1a" and len(tiles) == 2:
                    dbg = nc.dram_tensor("dbg_x", [128, D], F32, kind="ExternalOutput")
                    dd = P.dsem("dbg")
                    op("sp", "dma_start", reads=[xs_.b], writes=[D_out], dma=dd, out=dbg[:, :], in_=xs_.t)
                    return finish(P, nc, [D_out])
                if len(tiles) == 2:
                    t0 = t - 1
                    norm_transpose(tiles, C_GAT, Slot(mk(hT.t, t0 * 128, [[TB, KC], [1, 256]]), hT.b, None), xn, 2)
                    tiles = []
                    if debug == "A1c":
                        dbg = nc.dram_tensor("dbg_hT", [128, KC * TB], BF16, kind="ExternalOutput")
                        dd = P.dsem("dbg")
                        for k_ in range(KC):
                            op("sp", "dma_start", reads=[hT.b], writes=[D_out], dma=dd, out=dbg[:, k_ * TB:(k_ + 1) * TB], in_=hT.t[:, k_, :])
                        return finish(P, nc, [D_out])
            if debug == "A1":
                dbg = nc.dram_tensor("dbg_hT", [128, KC * TB], BF16, kind="ExternalOutput")
                dd = P.dsem("dbg")
                op("sp", "dma_start", reads=[hT.b], writes=[D_out], dma=dd, out=dbg[:, :], in_=mk(hT.t, 0, [[1, KC * TB]]))
                return finish(P, nc, [D_out])
            groups = []
            if blk >= PBLK - 1:
                groups.append(("conv", O_MQ, 0))
            groups.append(("conv", O_MK, 4))
            groups += [("v", O_MV, mv_s, D_mv, 0), ("v", O_MV + 512, mv_s, D_mv, 512)]
            if own:
                groups += [("sig", O_MO, smo_s, D_smo, 0), ("sig", O_MO + 512, smo_s, D_smo, 512)]
                groups += [("rope", O_AQ, aqT_s, D_aqT, 0, False), ("rope", O_AQ + 512, aqT_s, D_aqT, 4, False)]
            groups += [("rope", O_AK, akT_s, D_akT, 0, True), ("rope", O_AK + 512, akT_s, D_akT, 4, True)]
            groups += [("v", O_AV, av_s, D_av, 0), ("v", O_AV + 512, av_s, D_av, 512)]
            if own:
                for i in range(4):
                    groups.append(("sigT", O_GM + i * 512, sgmT_s, D_sgm, i * 512))
                for i in range(4):
                    groups.append(("sigT", O_GA + i * 512, sgaT_s, D_sga, i * 512))
            import os
            _kf = os.environ.get("KGROUPS")
            if _kf:
                groups = [g for g in groups if g[0] in _kf.split(",")]
            for g in groups:
                kind, c0 = g[0], g[1]
                w = load_w(wt, w_in, 0, KC, c0, 512)
                if g is groups[-1] and blk + 1 < NBLK:
                    for t_ in range(2):
                        xs_ = xt.next()
                        op("sp", "dma_start", writes=[xs_.b], dma=xs_.d, out=xs_.t, in_=xs[tok0 + TB + t_ * 128:tok0 + TB + (t_ + 1) * 128, :])
                        pre_x.append(xs_)
                if kind in ("v", "sig", "rope"):
                    gates = (kind == "v" and g[2] is mv_s and g[4] == 0)
                    if gates:
                        wgs = load_w(wg, w_in, 0, KC, O_MI, 8)
                    if kind == "rope":
                        qs = qk_st.next()
                    for t in range(TB // 128):
                        gt = blk * (TB // 128) + t
                        a = acc.next()
                        for k in range(KC):
                            op("pe", "matmul", reads=[hT.b, w.b], writes=[a.b], out=a.t, lhsT=hT.t[:, k, t * 128:(t + 1) * 128],
                               rhs=w.t[:, k, :], start=(k == 0), stop=(k == KC - 1))
                        if gates:
                            a2 = acc.next()
                            for k in range(KC):
                                op("pe", "matmul", reads=[hT.b, wgs.b], writes=[a2.b], out=a2.t[:, 0:8], lhsT=hT.t[:, k, t * 128:(t + 1) * 128],
                                   rhs=wgs.t[:, k, :], start=(k == 0), stop=(k == KC - 1))
                            op("dve", "tensor_tensor", reads=[a2.b, cs.b], writes=[G.b], out=G.t[:, gt, :], in0=a2.t[:, 0:8],
                               in1=CS[:, C_BIF:C_BIF + 8], op=ALU.add)
                        flush_pending()
                        if kind == "v":
                            s = st.next()
                            op("dve", "tensor_copy", reads=[a.b], writes=[s.b], out=s.t, in_=a.t)
                            op("sp", "dma_start", reads=[s.b], writes=[g[3]], dma=s.d,
                               out=g[2][tok0 + t * 128:tok0 + (t + 1) * 128, g[4]:g[4] + 512], in_=s.t)
                        elif kind == "sig":
                            s = st.next()
                            op("act", "activation", reads=[a.b], writes=[s.b], out=s.t, in_=a.t, func=AF.Sigmoid)
                            op("sp", "dma_start", reads=[s.b], writes=[g[3]], dma=s.d,
                               out=g[2][otok0 + t * 128:otok0 + (t + 1) * 128, g[4]:g[4] + 512], in_=s.t)
                        else:
                            r = rf.next(); tm = rtmp.next(); rbb = rb.next()
                            op("dve", "tensor_copy", reads=[a.b], writes=[r.b], out=r.t, in_=a.t)
                            x1 = mk(r.t, 0, [[128, 4], [1, 16]])
                            x2 = mk(r.t, 16, [[128, 4], [1, 16]])
                            cosv, sinv = tabv(1, gt, 4), tabv(0, gt, 4)
                            rr = dict(reads=[r.b, sincos.b], writes=[tm.b])
                            op("dve", "tensor_tensor", out=tm.t[:, 0], in0=x1, in1=cosv, op=ALU.mult, **rr)
                            op("dve", "tensor_tensor", out=tm.t[:, 1], in0=x2, in1=sinv, op=ALU.mult, **rr)
                            op("dve", "tensor_tensor", out=tm.t[:, 2], in0=x2, in1=cosv, op=ALU.mult, **rr)
                            op("dve", "tensor_tensor", out=tm.t[:, 3], in0=x1, in1=sinv, op=ALU.mult, **rr)
                            op("dve", "tensor_tensor", reads=[tm.b], writes=[r.b], out=x1, in0=tm.t[:, 0], in1=tm.t[:, 1], op=ALU.subtract)
                            op("dve", "tensor_tensor", reads=[tm.b], writes=[r.b], out=x2, in0=tm.t[:, 2], in1=tm.t[:, 3], op=ALU.add)
                            op("act", "activation", reads=[r.b], writes=[rbb.b], out=rbb.t, in_=r.t, func=AF.Copy)

                            def do_tr(rbb=rbb, qs=qs, t=t):
                                tp = tps.next()
                                for hh in range(4):
                                    op("pe", "transpose", reads=[rbb.b, cbs.b], writes=[tp.b], out=tp.t[:, hh * 128:(hh + 1) * 128],
                                       in_=rbb.t[:, hh * 128:(hh + 1) * 128], identity=ident)
                                dst = mk(qs.t, t * 128, [[TB, 4], [1, 128]])
                                src = mk(tp.t, 0, [[128, 4], [1, 128]])
                                op("act", "activation", reads=[tp.b], writes=[qs.b], out=dst, in_=src, func=AF.Copy)
                            pending.append(do_tr)
                    if kind == "rope":
                        flush_pending()
                        hb, is_k = g[4], g[5]
                        t0_ = tok0 if is_k else otok0
                        if is_k:
                            b0 = blk * 4
                            kmv = mk(kmean.t, hb * NB + b0, [[NB, 4], [1, 4]])
                            src = mk(qs.t, 0, [[TB, 4], [256, 4], [1, 256]])
                            op("dve", "tensor_reduce", reads=[qs.b], writes=[kmean.b], out=kmv, in_=src, axis=AX.X, op=ALU.add)
                        dstd = g[2][hb * 128:(hb + 4) * 128, t0_:t0_ + TB].rearrange("(h p) t -> p h t", p=128)
                        op("sp", "dma_start", reads=[qs.b], writes=[g[3]], dma=qs.d, out=dstd, in_=qs.t)
                else:
                    for cc in range(4):
                        for th in range(TB // 512):
                            a = acc.next()
                            for k in range(KC):
                                op("pe", "matmul", reads=[hT.b, w.b], writes=[a.b], out=a.t, lhsT=w.t[:, k, cc * 128:(cc + 1) * 128],
                                   rhs=hT.t[:, k, th * 512:(th + 1) * 512], start=(k == 0), stop=(k == KC - 1))
                            if kind == "sigT":
                                s = st.next()
                                op("act", "activation", reads=[a.b], writes=[s.b], out=s.t, in_=a.t, func=AF.Sigmoid)
                                r0 = g[4] + cc * 128
                                op("sp", "dma_start", reads=[s.b], writes=[g[3]], dma=s.d,
                                   out=g[2][r0:r0 + 128, otok0 + th * 512:otok0 + (th + 1) * 512], in_=s.t)
                            else:
                                hc = g[2] + cc
                                z = zp.next(); y = cy.next(); sg = csg.next(); ob = cob.next()
                                op("dve", "tensor_copy", reads=[halo.b], writes=[z.b], out=z.t[:, 0:3], in_=halo.t[:, hc, :])
                                op("dve", "tensor_copy", reads=[a.b], writes=[z.b], out=z.t[:, 3:515], in_=a.t)
                                op("dve", "tensor_copy", reads=[z.b], writes=[halo.b], out=halo.t[:, hc, :], in_=z.t[:, 512:515])
                                cwc = lambda j, hc=hc: CS[:, C_CW + j * 8 + hc:C_CW + j * 8 + hc + 1]
                                op("dve", "tensor_scalar", reads=[z.b, cs.b], writes=[y.b], out=y.t, in0=z.t[:, 0:512], scalar1=cwc(0),
                                   scalar2=CS[:, C_CB + hc:C_CB + hc + 1], op0=ALU.mult, op1=ALU.add)
                                for j in range(1, 4):
                                    op("dve", "scalar_tensor_tensor", reads=[z.b, cs.b, y.b], writes=[y.b], out=y.t, in0=z.t[:, j:j + 512],
                                       scalar=cwc(j), in1=y.t, op0=ALU.mult, op1=ALU.add)
                                op("act", "activation", reads=[y.b], writes=[sg.b], out=sg.t, in_=y.t, func=AF.Sigmoid)
                                scl = 1.0 if hc < 4 else 128.0 ** -0.5
                                op("dve", "scalar_tensor_tensor", reads=[y.b, sg.b], writes=[ob.b], out=ob.t, in0=y.t, scalar=scl,
                                   in1=sg.t, op0=ALU.mult, op1=ALU.mult)
                                if hc < 4:
                                    if own:
                                        op("sp", "dma_start", reads=[ob.b], writes=[D_mqT], dma=ob.d,
                                           out=mqT_s[hc * 128:(hc + 1) * 128, otok0 + th * 512:otok0 + (th + 1) * 512], in_=ob.t)
                                else:
                                    op("sp", "dma_start", reads=[ob.b], writes=[D_mkT], dma=ob.d,
                                       out=mkT_s[(hc - 4) * 128:(hc - 3) * 128, tok0 + th * 512:tok0 + (th + 1) * 512], in_=ob.t)
        op("dve", "tensor_copy", reads=[kmean.b], writes=[kmean_b.b], out=kmean_b.t, in_=kmean.t)
    if debug == "A":
        return finish(P, nc, [D_mqT, D_mkT, D_mv, D_smo, D_aqT, D_akT, D_av, D_sgm, D_sga])

    with Phase(P):
        NG = NT * 4
        gl = P.slot("gl", [128, NG], F32)
        gb = P.slot("gb", [128, NG], F32)
        gw = P.slot("gw", [128, NG], F32)
        gw2 = P.slot("gw2", [128, NG], F32)
        get = P.slot("get", [128, NG], F32)
        geL = P.slot("geL", [128, NG], F32)
        Gi = mk(G.t, 0, [[8, NT], [1, 4]])
        Gf = mk(G.t, 4, [[8, NT], [1, 4]])
        v2 = lambda s_: mk(s_.t, 0, [[4, NT], [1, 4]])
        op("act", "activation", reads=[G.b], writes=[gl.b], out=v2(gl), in_=Gf, func=AF.Exp, scale=-1.0)
        op("act", "activation", reads=[gl.b], writes=[gl.b], out=gl.t, in_=gl.t, func=AF.Ln, bias=1.0)
        a = acc.next()
        op("pe", "matmul", reads=[gl.b, cs.b], writes=[a.b], out=a.t[:, 0:NG], lhsT=tri_f, rhs=gl.t, start=True, stop=True)
        op("dve", "tensor_scalar", reads=[a.b], writes=[gb.b], out=gb.t, in0=a.t[:, 0:NG], scalar1=-1.0, scalar2=None, op0=ALU.mult)
        a = acc.next()
        op("pe", "matmul", reads=[gl.b, cs.b], writes=[a.b], out=a.t[:, 0:NG], lhsT=ones_f, rhs=gl.t, start=True, stop=True)
        op("act", "activation", reads=[a.b], writes=[geL.b], out=geL.t, in_=a.t[:, 0:NG], func=AF.Exp, scale=-1.0)
        op("act", "activation", reads=[gb.b], writes=[get.b], out=get.t, in_=gb.t, func=AF.Exp)
        op("dve", "tensor_tensor", reads=[G.b, gb.b], writes=[gw.b], out=v2(gw), in0=Gi, in1=v2(gb), op=ALU.subtract)
        op("act", "activation", reads=[gw.b], writes=[gw.b], out=gw.t, in_=gw.t, func=AF.Exp)
        op("dve", "tensor_scalar", reads=[gw.b, cs.b], writes=[gw.b], out=gw.t[:, 0:NG // 2], in0=gw.t[:, 0:NG // 2],
           scalar1=CS[:, C_PVAL:C_PVAL + 1], scalar2=None, op0=ALU.mult)
        op("dve", "tensor_tensor", reads=[gw.b, geL.b], writes=[gw2.b], out=gw2.t, in0=gw.t, in1=geL.t, op=ALU.mult)

        Cf = P.slot("Cf", [128, 4, 258], F32)
        Cb = P.slot("Cb", [128, 4, 258], BF16)
        op("dve", "memset", writes=[Cf.b], ap=Cf.t, constant=0.0)
        op("dve", "memset", writes=[Cb.b], ap=Cb.t, constant=0.0)
        CH = 8
        kTr = P.ring("kTr", 2, [128, 4, CH * 128], BF16, dma=True)
        qTr = P.ring("qTr", 2, [128, 4, CH * 128], BF16, dma=True)
        Vr = P.ring("Vr", 2, [128, CH, 4, 258], BF16, dma=True)
        smr = P.ring("smr", 2, [128, CH, 1024], BF16, dma=True)
        hst = P.ring("hst", 2, [128, 8, CH * 128], BF16, dma=True)
        for s_ in Vr.slots:
            op("dve", "memset", writes=[s_.b], ap=s_.t, constant=1.0)
        ktok = P.ring("ktok", 3, [128, 128], BF16)
        V1 = P.ring("V1", 3, [128, 258], BF16)
        V2 = P.ring("V2", 3, [128, 258], BF16)
        Smr = P.ring("Sm", 3, [128, 128], BF16)
        hmb = P.ring("hmb", 10, [128, 256], BF16)
        rawsb = P.ring("rawsb", 3, [128, 4, 258], F32)
        jk = P.ring("jk", 2, [128, 256], BF16)
        sc8 = P.ring("sc8", 4, [128, 32], F32)
        pendB = []

        def flushB(keep=0):
            while len(pendB) > keep:
                pendB.pop(0)()

        for cg in range(NT // CH):
            ownc = cg * CH >= NT // 2
            t0 = cg * CH * 128
            ot0 = t0 - T
            kT = kTr.next(); Vt = Vr.next()
            op("sp", "dma_start", reads=[D_mkT], writes=[kT.b], dma=kT.d, out=kT.t,
               in_=mkT_s[:, t0:t0 + CH * 128].rearrange("(h p) t -> p h t", p=128))
            for n_ in range(CH):
                op("sp", "dma_start", reads=[D_mv], writes=[Vt.b], dma=Vt.d, out=Vt.t[:, n_, :, 0:256],
                   in_=mv_s[t0 + n_ * 128:t0 + (n_ + 1) * 128, :].rearrange("p (h c) -> p h c", h=4))
            if ownc:
                qT = qTr.next(); smo = smr.next(); hs = hst.next()
                op("sp", "dma_start", reads=[D_mqT], writes=[qT.b], dma=qT.d, out=qT.t,
                   in_=mqT_s[:, ot0:ot0 + CH * 128].rearrange("(h p) t -> p h t", p=128))
                for n_ in range(CH):
                    op("sp", "dma_start", reads=[D_smo], writes=[smo.b], dma=smo.d, out=smo.t[:, n_, :],
                       in_=smo_s[ot0 + n_ * 128:ot0 + (n_ + 1) * 128, :])
            for cl in range(CH):
                c = cg * CH + cl
                if ownc:
                    rwc = rawsb.next(); s8c = sc8.next()
                for h in range(4):
                    col = c * 4 + h
                    flushB(4)
                    gcol = lambda s_, col=col: s_.t[:, col:col + 1]
                    ksl = kT.t[:, h, cl * 128:(cl + 1) * 128]
                    tp = tps.next()
                    op("pe", "transpose", reads=[kT.b, cbs.b], writes=[tp.b], out=tp.t[:, 0:128], in_=ksl, identity=ident)
                    kk = ktok.next()
                    op("act", "activation", reads=[tp.b], writes=[kk.b], out=kk.t, in_=tp.t[:, 0:128], func=AF.Copy)
                    vv2 = V2.next()
                    op("dve", "tensor_scalar", reads=[Vt.b, gw2.b], writes=[vv2.b], out=vv2.t[:, 0:257], in0=Vt.t[:, cl, h, 0:257],
                       scalar1=gcol(gw2), scalar2=None, op0=ALU.mult)
                    if ownc:
                        qsl = qT.t[:, h, cl * 128:(cl + 1) * 128]
                        vv1 = V1.next()
                        op("dve", "tensor_scalar", reads=[Vt.b, gw.b], writes=[vv1.b], out=vv1.t[:, 0:257], in0=Vt.t[:, cl, h, 0:257],
                           scalar1=gcol(gw), scalar2=None, op0=ALU.mult)
                        a = acc.next()
                        op("pe", "matmul", reads=[kT.b, qT.b], writes=[a.b], out=a.t[:, 0:128], lhsT=ksl, rhs=qsl, start=True, stop=True)
                        sm_ = Smr.next()
                        op("dve", "tensor_tensor", reads=[a.b, cbs.b], writes=[sm_.b], out=sm_.t, in0=a.t[:, 0:128], in1=tri_bf, op=ALU.mult)
                        a2 = acc.next()
                        op("pe", "matmul", reads=[qT.b, Cb.b], writes=[a2.b], out=a2.t[:, 0:257], lhsT=qsl, rhs=Cb.t[:, h, 0:257], start=True, stop=False)
                        op("pe", "matmul", reads=[sm_.b, vv1.b], writes=[a2.b], out=a2.t[:, 0:257], lhsT=sm_.t, rhs=vv1.t[:, 0:257], start=False, stop=True)
                        pass
                    a3 = acc.next()
                    op("pe", "matmul", reads=[kk.b, vv2.b], writes=[a3.b], out=a3.t[:, 0:257], lhsT=kk.t, rhs=vv2.t[:, 0:257], start=True, stop=True)
                    op("dve", "scalar_tensor_tensor", reads=[Cf.b, geL.b, a3.b], writes=[Cf.b], out=Cf.t[:, h, 0:257], in0=Cf.t[:, h, 0:257],
                       scalar=gcol(geL), in1=a3.t[:, 0:257], op0=ALU.mult, op1=ALU.add)
                    op("act", "activation", reads=[Cf.b], writes=[Cb.b], out=Cb.t[:, h, 0:257], in_=Cf.t[:, h, 0:257], func=AF.Copy)
                    if ownc:
                        op("dve", "tensor_copy", reads=[a2.b], writes=[rwc.b], out=rwc.t[:, h, 0:257], in_=a2.t[:, 0:257])
                        j_ = jk.next()
                        op("act", "activation", reads=[rwc.b], writes=[j_.b, s8c.b], out=j_.t, in_=rwc.t[:, h, 0:256], func=AF.Square,
                           accum_out=s8c.t[:, h:h + 1])
                if ownc:
                    R = lambda r: s8c.t[:, r * 4:(r + 1) * 4]
                    den4 = mk(rwc.t, 256, [[258, 4]])
                    get4 = get.t[:, c * 4:(c + 1) * 4]
                    rw8 = dict(reads=[s8c.b], writes=[s8c.b])
                    op("dve", "tensor_tensor", reads=[rwc.b, get.b], writes=[s8c.b], out=R(1), in0=den4, in1=get4, op=ALU.mult)
                    op("dve", "tensor_scalar", out=R(2), in0=R(1), scalar1=-1.0, scalar2=None, op0=ALU.mult, **rw8)
                    op("dve", "tensor_tensor", out=R(1), in0=R(1), in1=R(2), op=ALU.max, **rw8)
                    op("dve", "tensor_scalar", out=R(1), in0=R(1), scalar1=1.0, scalar2=None, op0=ALU.max, **rw8)
                    op("dve", "reciprocal", out=R(2), in_=R(1), **rw8)
                    op("dve", "tensor_tensor", reads=[s8c.b, get.b], writes=[s8c.b], out=R(3), in0=R(2), in1=get4, op=ALU.mult)
                    op("dve", "tensor_tensor", out=R(4), in0=R(0), in1=R(3), op=ALU.mult, **rw8)
                    op("dve", "tensor_tensor", out=R(4), in0=R(4), in1=R(3), op=ALU.mult, **rw8)
                    op("dve", "tensor_scalar", out=R(4), in0=R(4), scalar1=1.0 / 256, scalar2=EPS, op0=ALU.mult, op1=ALU.add, **rw8)
                    op("act", "activation", out=R(5), in_=R(4), func=AF.Sqrt, **rw8)
                    op("dve", "reciprocal", out=R(6), in_=R(5), **rw8)
                    op("dve", "tensor_tensor", out=R(7), in0=R(6), in1=R(3), op=ALU.mult, **rw8)
                    for h in range(4):
                        hb_ = hmb.next()
                        op("dve", "scalar_tensor_tensor", reads=[rwc.b, s8c.b, smo.b], writes=[hb_.b], out=hb_.t, in0=rwc.t[:, h, 0:256],
                           scalar=s8c.t[:, 28 + h:29 + h], in1=smo.t[:, cl, h * 256:(h + 1) * 256], op0=ALU.mult, op1=ALU.mult)

                        def _tr(hb_=hb_, hs=hs, h=h, cl=cl):
                            tp2 = tps.next()
                            for j in range(2):
                                op("pe", "transpose", reads=[hb_.b, cbs.b], writes=[tp2.b], out=tp2.t[:, j * 128:(j + 1) * 128],
                                   in_=hb_.t[:, j * 128:(j + 1) * 128], identity=ident)
                            for j in range(2):
                                fc = h * 2 + j
                                op("dve", "tensor_tensor", reads=[tp2.b, cs.b], writes=[hs.b], out=hs.t[:, fc, cl * 128:(cl + 1) * 128],
                                   in0=tp2.t[:, j * 128:(j + 1) * 128], in1=mk(CS, C_GMO + fc, [[0, 128]]), op=ALU.mult)
                        pendB.append(_tr)
            flushB()
            if ownc:
                for fc in range(8):
                    op("sp", "dma_start", reads=[hs.b], writes=[D_hmT], dma=hs.d,
                       out=hmT_s[fc * 128:(fc + 1) * 128, ot0:ot0 + CH * 128], in_=hs.t[:, fc, :])
    if debug == "B":
        return finish(P, nc, [D_hmT])

    with Phase(P):
        ccs = P.slot("cstc", [128, CC_TOT], BF16, dma=True)
        op("pool", "dma_start", writes=[ccs.b], dma=ccs.d, out=ccs.t, in_=cstc[:, :])
        CM = lambda kt: ccs.t[:, CC_CM + kt * 256:CC_CM + (kt + 1) * 256]
        CMQ = lambda hf: ccs.t[:, CC_CMQ + hf * 256:CC_CMQ + (hf + 1) * 256]
        ESEL = lambda n: ccs.t[:, CC_ESEL + n * 128:CC_ESEL + (n + 1) * 128]
        kTh = P.ring("kTh", 2, [128, S2], BF16, dma=True)
        qTh = P.ring("qTh", 2, [128, T], BF16, dma=True)
        Vh = P.ring("Vh", 2, [128, NT, 130], BF16, dma=True)
        for s_ in Vh.slots:
            op("dve", "memset", writes=[s_.b], ap=s_.t, constant=1.0)
        hast = P.ring("hast", 2, [128, T], BF16, dma=True)
        scm = P.ring("scm", 6, [128, 32], F32)
        c8 = P.ring("c8", 8, [128, 16], F32)
        Pt = P.ring("Pt", 3, [128, 512], BF16)
        hq = P.ring("hq", 2, [128, 128], BF16)
        oac = P.ring("oac", 4, [128, 130], F32)
        pvr = Ring([aux[0], aux[1]])
        SCALE = 128.0 ** -0.5
        heads = {}

        def load_head(h):
            kT = kTh.next(); qT = qTh.next(); Vt = Vh.next(); ha = hast.next()
            op("sp", "dma_start", reads=[D_akT], writes=[kT.b], dma=kT.d, out=kT.t, in_=akT_s[h * 128:(h + 1) * 128, :])
            op("sp", "dma_start", reads=[D_aqT], writes=[qT.b], dma=qT.d, out=qT.t, in_=aqT_s[h * 128:(h + 1) * 128, :])
            for n0 in range(0, NT, 4):
                op("sp", "dma_start", reads=[D_av], writes=[Vt.b], dma=Vt.d, out=Vt.t[:, n0:n0 + 4, 0:128],
                   in_=av_s[n0 * 128:(n0 + 4) * 128, h * 128:(h + 1) * 128].rearrange("(n p) c -> p n c", p=128))
            heads[h] = (kT, qT, Vt, ha)

        def prologue(h, j):
            kT, qT, Vt, ha = heads[h]
            q0 = j * 256
            sels = []
            for hf in range(2):
                qsl = qT.t[:, q0 + hf * 128:q0 + (hf + 1) * 128]
                a = acc.next()
                op("pe", "matmul", reads=[qT.b, kmean_b.b], writes=[a.b], out=a.t[:, 0:NB], lhsT=qsl, rhs=kmean_b.t[:, h, :], start=True, stop=True)
                sc = scm.next(); cc8 = c8.next()
                op("dve", "tensor_tensor", reads=[a.b, cs.b], writes=[sc.b], out=sc.t[:, 0:NB], in0=a.t[:, 0:NB],
                   in1=CS[:, C_PMASK + j * NB:C_PMASK + (j + 1) * NB], op=ALU.add)
                op("dve", "max", reads=[sc.b], writes=[cc8.b], out=cc8.t[:, 0:8], in_=sc.t[:, 0:NB])
                op("dve", "tensor_scalar", reads=[cc8.b], writes=[cc8.b], out=cc8.t[:, 8:9], in0=cc8.t[:, 2:3], scalar1=-1e29, scalar2=None, op0=ALU.max)
                op("dve", "tensor_scalar", reads=[sc.b, cc8.b], writes=[sc.b], out=sc.t[:, 0:NB], in0=sc.t[:, 0:NB], scalar1=cc8.t[:, 8:9],
                   scalar2=None, op0=ALU.is_ge)
                sels.append(sc)
            return sels

        seq = [(h, j) for h in range(8) for j in range(NBO)]
        load_head(0)
        sels_cur = prologue(0, 0)
        for idx, (h, j) in enumerate(seq):
            if j == 0 and h + 1 < 8:
                load_head(h + 1)
            kT, qT, Vt, ha = heads[h]
            q0 = j * 256
            sels = sels_cur
            nblk = NBP + j + 1
            oo = [oac.next(), oac.next()]
            qs2 = qT.t[:, q0:q0 + 256]
            nxt = seq[idx + 1] if idx + 1 < len(seq) else None

            def emit_S(n, kT=kT, qT=qT, nblk=nblk, qs2=qs2):
                is_own = n == nblk - 1
                a = acc.next()
                for kt in range(2):
                    kti = n * 2 + kt
                    o_ = a.t[:, kt * 256:(kt + 1) * 256]
                    op("pe", "matmul", reads=[kT.b, qT.b], writes=[a.b], out=o_, lhsT=kT.t[:, kti * 128:(kti + 1) * 128], rhs=qs2, start=True, stop=not is_own)
                    if is_own:
                        op("pe", "matmul", reads=[cbs.b, ccs.b], writes=[a.b], out=o_, lhsT=ident, rhs=CM(kt), start=False, stop=True)
                pt = Pt.next()
                op("act", "activation", reads=[a.b], writes=[pt.b], out=pt.t, in_=a.t, func=AF.Exp, scale=SCALE)
                return pt

            def emit_PV(n, pt, Vt=Vt, nblk=nblk, oo=oo, sels=sels):
                is_own = n == nblk - 1
                pv = pvr.next()
                for hf in range(2):
                    for kt in range(2):
                        kti = n * 2 + kt
                        op("pe", "matmul", reads=[pt.b, Vt.b], writes=[pv.b], out=pv.t[:, hf * 130:hf * 130 + 129],
                           lhsT=pt.t[:, kt * 256 + hf * 128:kt * 256 + (hf + 1) * 128], rhs=Vt.t[:, kti, 0:129], start=(kt == 0), stop=(kt == 1))
                for hf in range(2):
                    o = oo[hf]
                    src = pv.t[:, hf * 130:hf * 130 + 129]
                    if n == 0:
                        wsc = 1.0 if is_own else sels[hf].t[:, 0:1]
                        op("dve", "tensor_scalar", reads=[pv.b, sels[hf].b], writes=[o.b], out=o.t[:, 0:129], in0=src, scalar1=wsc, scalar2=None, op0=ALU.mult)
                    else:
                        wsc = 1.0 if is_own else sels[hf].t[:, n:n + 1]
                        op("dve", "scalar_tensor_tensor", reads=[pv.b, sels[hf].b, o.b], writes=[o.b], out=o.t[:, 0:129], in0=src, scalar=wsc,
                           in1=o.t[:, 0:129], op0=ALU.mult, op1=ALU.add)

            emit_conv_job(gate=[sels[0].b])
            prev = None
            for n in range(nblk):
                pt = emit_S(n)
                if n == 1 and nxt is not None:
                    sels_cur = prologue(*nxt)
                if prev is not None:
                    emit_PV(*prev)
                prev = (n, pt)
            emit_PV(*prev)
            for hf in range(2):
                o = oo[hf]
                cc8 = c8.next(); hq_ = hq.next()
                op("dve", "reciprocal", reads=[o.b], writes=[cc8.b], out=cc8.t[:, 0:1], in_=o.t[:, 128:129])
                op("dve", "tensor_scalar", reads=[o.b, cc8.b], writes=[hq_.b], out=hq_.t, in0=o.t[:, 0:128], scalar1=cc8.t[:, 0:1], scalar2=None, op0=ALU.mult)
                tp = tps.next()
                op("pe", "transpose", reads=[hq_.b, cbs.b], writes=[tp.b], out=tp.t[:, 0:128], in_=hq_.t, identity=ident)
                op("act", "activation", reads=[tp.b], writes=[ha.b], out=ha.t[:, q0 + hf * 128:q0 + (hf + 1) * 128], in_=tp.t[:, 0:128], func=AF.Copy)
            if j == NBO - 1:
                op("sp", "dma_start", reads=[ha.b], writes=[D_haT], dma=ha.d, out=haT_s[h * 128:(h + 1) * 128, :], in_=ha.t)
    while emit_conv_job():
        pass
    if debug == "C":
        return finish(P, nc, [D_haT, D_hmT])

    with Phase(P):
        TD = 512
        NTD = TD // 128
        x2 = [P.slot("x2", [128, D], F32, dma=True) for _ in range(NTD)]
        hT = P.slot("hTd", [128, KC, TD], BF16)
        wt = P.ring("wtd", 3, [128, KC, 512], BF16, dma=True)
        sgr = P.ring("sgr", 4, [128, 512], BF16, dma=True)
        tmpf = P.ring("tmpf", 3, [128, 512], F32)
        rlu = P.ring("rlu", 2, [128, 512], BF16)
        xn = P.ring("xnd", 2, [128, D], BF16)
        gf = P.slot("gfin", [128, D], F32, dma=True)
        pT = P.slot("pT", [128, 2, TD], BF16)
        pin = P.ring("pin", 2, [128, 256], F32, dma=True)
        pbf = P.ring("pbf", 2, [128, 256], BF16)
        op("pool", "dma_start", writes=[gf.b], dma=gf.d, out=gf.t, in_=gfin[:, :])
        for blk in range(T // TD):
            o0 = blk * TD
            for t in range(NTD):
                op("pool", "dma_start", writes=[x2[t].b], dma=x2[t].d, out=x2[t].t, in_=xs[T + o0 + t * 128:T + o0 + (t + 1) * 128, :])
            sub1 = Phase(P)
            sub1.__enter__()
            mT = P.slot("mT", [128, KC, TD], BF16)
            hmT = P.slot("hmTd", [128, 8, TD], BF16, dma=True)
            haT = P.slot("haTd", [128, 8, TD], BF16, dma=True)
            for c0_ in (0, 4):
                op("pool", "dma_start", reads=[D_hmT], writes=[hmT.b], dma=hmT.d, out=hmT.t[:, c0_:c0_ + 4, :],
                   in_=hmT_s[c0_ * 128:(c0_ + 4) * 128, o0:o0 + TD].rearrange("(c p) t -> p c t", p=128))
                op("pool", "dma_start", reads=[D_haT], writes=[haT.b], dma=haT.d, out=haT.t[:, c0_:c0_ + 4, :],
                   in_=haT_s[c0_ * 128:(c0_ + 4) * 128, o0:o0 + TD].rearrange("(c p) t -> p c t", p=128))
            for wgi in range(4):
                wm = load_w(wt, wb["up_m"], 0, 8, wgi * 512, 512, dep=D_wb["up_m"], q="sp")
                wa = load_w(wt, wb["up_a"], 0, 8, wgi * 512, 512, dep=D_wb["up_a"], q="sp")
                for cc in range(4):
                    fc = wgi * 4 + cc
                    am = acc.next()
                    for k in range(8):
                        op("pe", "matmul", reads=[wm.b, hmT.b], writes=[am.b], out=am.t, lhsT=wm.t[:, k, cc * 128:(cc + 1) * 128], rhs=hmT.t[:, k, :], start=(k == 0), stop=(k == 7))
                    aa = acc.next()
                    for k in range(8):
                        op("pe", "matmul", reads=[wa.b, haT.b], writes=[aa.b], out=aa.t, lhsT=wa.t[:, k, cc * 128:(cc + 1) * 128], rhs=haT.t[:, k, :], start=(k == 0), stop=(k == 7))
                    s1 = sgr.next(); s2 = sgr.next(); t1 = tmpf.next(); t2 = tmpf.next()
                    op("pool", "dma_start", reads=[D_sgm], writes=[s1.b], dma=s1.d, out=s1.t, in_=sgmT_s[fc * 128:(fc + 1) * 128, o0:o0 + TD])
                    op("pool", "dma_start", reads=[D_sga], writes=[s2.b], dma=s2.d, out=s2.t, in_=sgaT_s[fc * 128:(fc + 1) * 128, o0:o0 + TD])
                    op("dve", "tensor_tensor", reads=[am.b, s1.b], writes=[t1.b], out=t1.t, in0=am.t, in1=s1.t, op=ALU.mult)
                    op("dve", "tensor_tensor", reads=[aa.b, s2.b], writes=[t2.b], out=t2.t, in0=aa.t, in1=s2.t, op=ALU.mult)
                    op("dve", "tensor_tensor", reads=[t1.b, t2.b], writes=[mT.b], out=mT.t[:, fc, :], in0=t1.t, in1=t2.t, op=ALU.add)
            for cgi in range(4):
                w = load_w(wt, wb["out"], 0, KC, cgi * 512, 512, dep=D_wb["out"], q="sp")
                for t in range(NTD):
                    a = acc.next()
                    for k in range(KC):
                        op("pe", "matmul", reads=[mT.b, w.b], writes=[a.b], out=a.t, lhsT=mT.t[:, k, t * 128:(t + 1) * 128], rhs=w.t[:, k, :], start=(k == 0), stop=(k == KC - 1))
                    xs_ = x2[t].t[:, cgi * 512:(cgi + 1) * 512]
                    op("dve", "tensor_tensor", reads=[a.b, x2[t].b], writes=[x2[t].b], out=xs_, in0=a.t, in1=xs_, op=ALU.add)
            sub1.__exit__(None, None, None)
            norm_transpose([(x2[t].t, x2[t].b) for t in range(NTD)], C_GMLP, hT, xn, 2)
            sub2 = Phase(P)
            sub2.__enter__()
            uT = P.slot("uT", [128, KC, TD], BF16)
            for qf in range(4):
                for wgi in range(4):
                    w1 = load_w(wt, wb["ff1"], 0, KC, qf * 2048 + wgi * 512, 512, dep=D_wb["ff1"], q="sp")
                    for cc in range(4):
                        fcl = wgi * 4 + cc
                        a = acc.next()
                        for k in range(KC):
                            op("pe", "matmul", reads=[w1.b, hT.b], writes=[a.b], out=a.t, lhsT=w1.t[:, k, cc * 128:(cc + 1) * 128], rhs=hT.t[:, k, :], start=(k == 0), stop=(k == KC - 1))
                        r_ = rlu.next()
                        op("act", "activation", reads=[a.b], writes=[r_.b], out=r_.t, in_=a.t, func=AF.Relu)
                        op("act", "activation", reads=[r_.b], writes=[uT.b], out=uT.t[:, fcl, :], in_=r_.t, func=AF.Square)
                for cgi in range(4):
                    w2 = load_w(wt, wb["ff2"], qf * 2048, KC, cgi * 512, 512, dep=D_wb["ff2"], q="sp")
                    for t in range(NTD):
                        a = acc.next()
                        for k in range(KC):
                            op("pe", "matmul", reads=[uT.b, w2.b], writes=[a.b], out=a.t, lhsT=uT.t[:, k, t * 128:(t + 1) * 128], rhs=w2.t[:, k, :], start=(k == 0), stop=(k == KC - 1))
                        xs_ = x2[t].t[:, cgi * 512:(cgi + 1) * 512]
                        op("dve", "tensor_tensor", reads=[a.b, x2[t].b], writes=[x2[t].b], out=xs_, in0=a.t, in1=xs_, op=ALU.add)
            sub2.__exit__(None, None, None)
            norm_transpose([(x2[t].t, x2[t].b) for t in range(NTD)], C_GPLE, hT, xn, 2)
            for t in range(NTD):
                pi = pin.next(); pb = pbf.next()
                op("pool", "dma_start", writes=[pi.b], dma=pi.d, out=pi.t, in_=pp_in[o0 + t * 128:o0 + (t + 1) * 128, :])
                op("act", "activation", reads=[pi.b], writes=[pb.b], out=pb.t, in_=pi.t, func=AF.Copy)
                tp = tps.next()
                for k in range(2):
                    op("pe", "transpose", reads=[pb.b, cbs.b], writes=[tp.b], out=tp.t[:, k * 128:(k + 1) * 128], in_=pb.t[:, k * 128:(k + 1) * 128], identity=ident)
                op("dve", "tensor_copy", reads=[tp.b], writes=[pT.b], out=mk(pT.t, t * 128, [[TD, 2], [1, 128]]), in_=mk(tp.t, 0, [[128, 2], [1, 128]]))
            for cgi in range(4):
                wgt = load_w(wt, wb["pg"], 0, KC, cgi * 512, 512, dep=D_wb["pg"], q="sp")
                wpp = load_w(wt, wb["pp"], 0, 2, cgi * 512, 512, dep=D_wb["pp"], q="sp")
                for t in range(NTD):
                    ag = acc.next()
                    for k in range(KC):
                        op("pe", "matmul", reads=[hT.b, wgt.b], writes=[ag.b], out=ag.t, lhsT=hT.t[:, k, t * 128:(t + 1) * 128], rhs=wgt.t[:, k, :], start=(k == 0), stop=(k == KC - 1))
                    ap_ = acc.next()
                    for k in range(2):
                        op("pe", "matmul", reads=[pT.b, wpp.b], writes=[ap_.b], out=ap_.t, lhsT=pT.t[:, k, t * 128:(t + 1) * 128], rhs=wpp.t[:, k, :], start=(k == 0), stop=(k == 1))
                    t1 = tmpf.next(); t2 = tmpf.next()
                    op("act", "activation", reads=[ag.b], writes=[t1.b], out=t1.t, in_=ag.t, func=AF.Sigmoid)
                    op("dve", "tensor_tensor", reads=[ap_.b, t1.b], writes=[t2.b], out=t2.t, in0=ap_.t, in1=t1.t, op=ALU.mult)
                    xs_ = x2[t].t[:, cgi * 512:(cgi + 1) * 512]
                    op("dve", "tensor_tensor", reads=[t2.b, x2[t].b], writes=[x2[t].b], out=xs_, in0=t2.t, in1=xs_, op=ALU.add)
            for t in range(NTD):
                n_ = xn.next()
                s = norm_stats(x2[t].t, x2[t].b, D, n_.t, n_.b)
                op("dve", "scalar_tensor_tensor", reads=[x2[t].b, s.b, gf.b], writes=[x2[t].b], out=x2[t].t, in0=x2[t].t, scalar=s.t[:, 3:4],
                   in1=gf.t, op0=ALU.mult, op1=ALU.mult)
                op("pool", "dma_start", reads=[x2[t].b], writes=[D_out], dma=x2[t].d, out=out[o0 + t * 128:o0 + (t + 1) * 128, :], in_=x2[t].t)
    return finish(P, nc, [D_out])


def finish(P, nc, dbufs):
    P.wait_all("sp", dbufs)
    stats = P.finalize()
    stats["arena_peak_bytes"] = P.apeak * 2
    P.stack.close()
    nc._stats = stats
    return nc


def host_inputs(T, x_b, p_b, pos_b, half, prm):
    S2 = 2 * T
    NT = S2 // 128
    NB = S2 // 256
    NBP = NB // 2
    NBO = NB - NBP
    if half == 1:
        xs = np.ascontiguousarray(x_b)
        ps = pos_b
    else:
        xs = np.concatenate([np.zeros((T, D), np.float32), x_b[:T]], axis=0)
        ps = np.concatenate([np.zeros((T,), np.int32), pos_b[:T]])
    p_own = np.ascontiguousarray(p_b[half * T:(half + 1) * T])
    NCST = C_PMASK + NBO * NB
    cst = np.zeros((128, NCST), np.float32)
    cst[:, C_GAT:C_GAT + 16] = prm["attn_norm"].reshape(16, 128).T
    cst[:, C_GMLP:C_GMLP + 16] = prm["mlp_norm"].reshape(16, 128).T
    cst[:, C_GPLE:C_GPLE + 16] = prm["ple_norm"].reshape(16, 128).T
    cst[:, C_GMO:C_GMO + 8] = prm["m_out_norm"].reshape(8, 128).T
    cw = prm["conv_w"].reshape(4, 8, 128)
    cst[:, C_CW:C_CW + 32] = cw.transpose(2, 0, 1).reshape(128, 32)
    cst[:, C_CB:C_CB + 8] = prm["conv_b"].reshape(8, 128).T
    cst[:, C_BIF:C_BIF + 8] = np.broadcast_to(prm["b_if"].reshape(1, 8), (128, 8))
    half_ = 16
    invf = (np.float32(500000.0) ** (-np.arange(half_, dtype=np.float32) * np.float32(2.0) / np.float32(32))).astype(np.float32)
    cst[:, C_INVF:C_INVF + 16] = invf[None, :]
    cst[:, C_PVAL] = float(half)
    tri, cb, cc = host_consts(T)
    cst[:, C_TRI:C_TRI + 128] = tri
    cst[:, C_ONES:C_ONES + 128] = 1.0
    pm = np.zeros((NBO, NB), np.float32)
    for j in range(NBO):
        pm[j, NBP + j:] = -1e30
        if half == 0:
            pm[j, :NBP] = -1e30
    cst[:, C_PMASK:] = pm.reshape(1, -1)
    d = dict(xs=xs, p=p_own, pos=np.ascontiguousarray(ps.reshape(NT, 128).T.astype(np.int32)), cst=cst, cstb=cb, cstc=cc,
             gfin=np.ascontiguousarray(np.broadcast_to(prm["final_norm"].reshape(1, D), (128, D))).astype(np.float32))
    return d


_NC_CACHE = {}


def kernel(x, p, positions, attn_norm, w_in, b_if, conv_w, conv_b, m_out_norm, w_up_m, w_up_a,
           w_out, mlp_norm, w_ff1, w_ff2, ple_norm, w_ple_gate, w_ple_proj, final_norm):
    x = np.asarray(x, np.float32); p = np.asarray(p, np.float32); positions = np.asarray(positions, np.int32)
    B, S, _ = x.shape
    T = S // 2
    prm = dict(attn_norm=np.asarray(attn_norm, np.float32)[0], mlp_norm=np.asarray(mlp_norm, np.float32)[0],
               ple_norm=np.asarray(ple_norm, np.float32)[0], m_out_norm=np.asarray(m_out_norm, np.float32)[0],
               conv_w=np.asarray(conv_w, np.float32)[0], conv_b=np.asarray(conv_b, np.float32)[0],
               b_if=np.asarray(b_if, np.float32)[0], final_norm=np.asarray(final_norm, np.float32))
    wts = dict(w_in=np.ascontiguousarray(np.asarray(w_in, np.float32)[0]), w_up_m=np.ascontiguousarray(np.asarray(w_up_m, np.float32)[0]),
               w_up_a=np.ascontiguousarray(np.asarray(w_up_a, np.float32)[0]), w_out=np.ascontiguousarray(np.asarray(w_out, np.float32)[0]),
               w_ff1=np.ascontiguousarray(np.asarray(w_ff1, np.float32)[0]), w_ff2=np.ascontiguousarray(np.asarray(w_ff2, np.float32)[0]),
               w_pg=np.ascontiguousarray(np.asarray(w_ple_gate, np.float32)[0]), w_pp=np.ascontiguousarray(np.asarray(w_ple_proj, np.float32)[0]))
    ncores = 2 * B
    if T not in _NC_CACHE:
        _NC_CACHE[T] = build(T)
    nc = _NC_CACHE[T]
    in_maps = []
    for c in range(ncores):
        b, half = c // 2, c % 2
        d = host_inputs(T, x[b], p[0, b], positions[b], half, prm)
        d.update(wts)
        in_maps.append(d)
    res = run_bass_kernel_spmd(nc, in_maps, core_ids=list(range(ncores)))
    outp = np.empty((B, S, D), np.float32)
    for c in range(ncores):
        b, half = c // 2, c % 2
        outp[b, half * T:(half + 1) * T] = res.results[c]["out"]
    return outp
```

```python
import contextlib
import math
import numpy as np
import concourse.bass as bass
import concourse.mybir as mybir
from concourse.bass_utils import run_bass_kernel_spmd

F32 = mybir.dt.float32
BF16 = mybir.dt.bfloat16
I32 = mybir.dt.int32
AF = mybir.ActivationFunctionType
ALU = mybir.AluOpType
AX = mybir.AxisListType

D = 2048
KC = 16
DFF = 8192
INW = 10248
EPS = 1e-6
NEGBIG = -30000.0
O_MQ, O_MK, O_MV, O_MO, O_MI, O_AQ, O_AK, O_AV, O_GM, O_GA = 0, 512, 1024, 2048, 3072, 3080, 4104, 5128, 6152, 8200


class Buf:
    __slots__ = ("name", "w", "r")

    def __init__(self, name=""):
        self.name = name
        self.w = None
        self.r = []


class DramBuf:
    __slots__ = ("name", "ws")

    def __init__(self, name=""):
        self.name = name
        self.ws = {}


class DmaSem:
    __slots__ = ("name", "count", "h")

    def __init__(self, name):
        self.name = name
        self.count = 0
        self.h = None


class Slot:
    __slots__ = ("t", "b", "d")

    def __init__(self, t, b, d):
        self.t, self.b, self.d = t, b, d


class Ring:
    def __init__(self, slots):
        self.slots = slots
        self.i = 0

    def next(self):
        s = self.slots[self.i % len(self.slots)]
        self.i += 1
        return s


def mk(base, off, dims):
    return bass.AP(base.tensor, base.offset + off, [[base.ap[0][0], 128]] + [list(d) for d in dims])


class Prog:
    ENGS = ("pe", "act", "dve", "pool", "sp")

    def __init__(self, nc):
        self.nc = nc
        self.ops = {e: [] for e in self.ENGS}
        self.serial = {e: 0 for e in self.ENGS}
        self.waited = {e: {} for e in self.ENGS}
        self.dsems = []
        self.stack = contextlib.ExitStack()
        self.nn = 0

    def init_arena(self, nbytes_per_part):
        self.arena_n = nbytes_per_part // 2
        self.arena = self.stack.enter_context(self.nc.sbuf_tensor("arena", [128, self.arena_n], BF16))
        self.aoff = 0
        self.apeak = 0

    def carve(self, shape, dt):
        n = 1
        for s_ in shape[1:]:
            n *= s_
        units = n if dt == BF16 else 2 * n
        self.aoff = (self.aoff + 15) // 16 * 16
        a = self.arena[:, self.aoff:self.aoff + units]
        self.aoff += units
        self.apeak = max(self.apeak, self.aoff)
        assert self.aoff <= self.arena_n, "SBUF arena overflow: %d > %d" % (self.aoff * 2, self.arena_n * 2)
        if dt != BF16:
            a = a.bitcast(dt)
        if len(shape) > 2:
            dims = []
            st_ = n
            for s_ in shape[1:]:
                st_ //= s_
                dims.append([st_, s_])
            a = bass.AP(a.tensor, a.offset, [[a.ap[0][0], 128]] + dims)
        return a

    def slot(self, name, shape, dt, dma=False):
        self.nn += 1
        nm = "%s_%d" % (name, self.nn)
        return Slot(self.carve(shape, dt), Buf(nm), self.dsem(nm) if dma else None)

    def ring(self, name, n, shape, dt, dma=False):
        return Ring([self.slot(name, shape, dt, dma) for _ in range(n)])

    def dsem(self, name):
        d = DmaSem(name)
        self.dsems.append(d)
        return d

    def _deps(self, reads, writes):
        deps = {}

        def add(tok):
            if tok is None:
                return
            k, v = tok
            if deps.get(k, 0) < v:
                deps[k] = v

        for b in reads:
            if isinstance(b, DramBuf):
                for k, v in b.ws.items():
                    add((k, v))
            else:
                add(b.w)
        for b in writes:
            if isinstance(b, DramBuf):
                continue
            add(b.w)
            for t in b.r:
                add(t)
        return deps

    def emit(self, eng, fn, reads=(), writes=(), dma=None):
        deps = self._deps(reads, writes)
        waits = []
        wd = self.waited[eng]
        for k, v in deps.items():
            if k == eng and eng in ("pe", "sp"):
                continue
            if wd.get(k, 0) >= v:
                continue
            wd[k] = v
            waits.append((k, v))
        if dma is not None:
            dma.count += 16
            tok = (dma, dma.count)
            ser = None
        else:
            self.serial[eng] += 1
            ser = self.serial[eng]
            tok = (eng, ser)
        self.ops[eng].append((waits, fn, ser, dma))
        for b in reads:
            if not isinstance(b, DramBuf):
                b.r.append(tok)
        for b in writes:
            if isinstance(b, DramBuf):
                k, v = tok
                if b.ws.get(k, 0) < v:
                    b.ws[k] = v
            else:
                b.w = tok
                b.r = []
        return tok

    def op(self, eng, method, reads=(), writes=(), dma=None, **kw):
        return self.emit(eng, lambda e, m=method, kw=kw: getattr(e, m)(**kw), reads, writes, dma)

    def wait_all(self, eng, bufs):
        deps = self._deps(bufs, ())
        waits = []
        wd = self.waited[eng]
        for k, v in deps.items():
            if wd.get(k, 0) >= v:
                continue
            wd[k] = v
            waits.append((k, v))
        self.ops[eng].append((waits, None, None, None))

    def finalize(self):
        nc = self.nc
        needed = {e: set() for e in self.ENGS}
        for e in self.ENGS:
            for waits, fn, ser, dma in self.ops[e]:
                for k, v in waits:
                    if isinstance(k, str):
                        needed[k].add(v)
        vmap = {e: {v: i + 1 for i, v in enumerate(sorted(needed[e]))} for e in self.ENGS}
        esem = {e: self.stack.enter_context(nc.semaphore("s_" + e)) for e in self.ENGS}
        for d in self.dsems:
            if d.count:
                d.h = self.stack.enter_context(nc.semaphore("d_" + d.name))
        handles = {"pe": "tensor", "act": "scalar", "dve": "vector", "pool": "gpsimd", "sp": "sync"}
        stats = {e: len(self.ops[e]) for e in self.ENGS}
        stats["ndsem"] = sum(1 for d in self.dsems if d.count)
        with nc.Block() as block:
            for e in self.ENGS:
                ops = self.ops[e]

                def body(h, ops=ops, e=e):
                    for waits, fn, ser, dma in ops:
                        for k, v in waits:
                            if isinstance(k, str):
                                h.wait_ge(esem[k], vmap[k][v])
                            else:
                                h.wait_ge(k.h, v)
                        if fn is None:
                            continue
                        ins = fn(h)
                        if dma is not None:
                            ins.then_inc(dma.h, 16)
                        elif ser in vmap[e]:
                            ins.then_inc(esem[e], 1)

                getattr(block, handles[e])(body)
        return stats


def ps_alloc(P, name, shape, dt=F32):
    return P.stack.enter_context(P.nc.psum_tensor(name, list(shape), dt))


def barrier(P, bufs):
    deps = {}
    for b in bufs:
        for tok in ([b.w] if b.w else []) + list(b.r):
            k, v = tok
            if deps.get(k, 0) < v:
                deps[k] = v
    for e in P.ENGS:
        waits = []
        wd = P.waited[e]
        for k, v in deps.items():
            if wd.get(k, 0) >= v:
                continue
            wd[k] = v
            waits.append((k, v))
        P.ops[e].append((waits, None, None, None))


class Phase:
    def __init__(self, P):
        self.P = P

    def __enter__(self):
        self.mark = self.P.aoff
        self.bufs = []
        self._slot = self.P.slot
        P = self.P

        def slot(name, shape, dt, dma=False, _o=self._slot, _s=self):
            s = _o(name, shape, dt, dma)
            _s.bufs.append(s.b)
            return s
        P.slot = slot
        return self

    def __exit__(self, *a):
        self.P.slot = self._slot
        barrier(self.P, self.bufs)
        self.P.aoff = self.mark
        return False


C_GAT, C_GMLP, C_GPLE, C_GMO, C_CW, C_CB, C_BIF, C_INVF, C_PVAL, C_TRI, C_ONES, C_PMASK = 0, 16, 32, 48, 56, 88, 96, 104, 120, 121, 249, 377
CB_ID, CB_TRI, CB_TOT = 0, 128, 256
CC_CM, CC_CMQ, CC_ESEL = 0, 512, 1024
CC_TOT = 1024 + 33 * 128


def host_consts(T):
    tri = (np.arange(128)[:, None] <= np.arange(128)[None, :]).astype(np.float32)
    cb = np.zeros((128, CB_TOT), np.float32)
    cb[:, CB_ID:CB_ID + 128] = np.eye(128, dtype=np.float32)
    cb[:, CB_TRI:CB_TRI + 128] = tri
    cc = np.zeros((128, CC_TOT), np.float32)
    for kt in range(2):
        key = kt * 128 + np.arange(128)[:, None]
        q = np.arange(256)[None, :]
        cc[:, CC_CM + kt * 256:CC_CM + (kt + 1) * 256] = np.where(key <= q, 0.0, NEGBIG)
        qq = kt * 128 + np.arange(128)[:, None]
        key2 = np.arange(256)[None, :]
        cc[:, CC_CMQ + kt * 256:CC_CMQ + (kt + 1) * 256] = np.where(key2 <= qq, 0.0, NEGBIG)
    for n in range(33):
        cc[n, CC_ESEL + n * 128:CC_ESEL + (n + 1) * 128] = 1.0
    return tri, cb, cc


def build(T, debug=None):
    S2 = 2 * T
    NT = S2 // 128
    NTO = T // 128
    TB = 1024
    NBLK = S2 // TB
    PBLK = NBLK // 2
    NB = S2 // 256
    NBP = NB // 2
    NBO = NB - NBP
    assert NB <= 32
    NCST = C_PMASK + NBO * NB
    okind = "ExternalOutput" if debug else "Internal"

    nc = bass.Bass("TRN2", target_bir_lowering=False)
    dt_in = lambda n, s, d=F32: nc.dram_tensor(n, list(s), d, kind="ExternalInput")
    xs = dt_in("xs", [S2, D]); pp_in = dt_in("p", [T, 256]); pos = dt_in("pos", [128, NT], I32)
    cst = dt_in("cst", [128, NCST]); cstb = dt_in("cstb", [128, CB_TOT]); cstc = dt_in("cstc", [128, CC_TOT])
    gfin = dt_in("gfin", [128, D])
    w_in = dt_in("w_in", [D, INW]); w_up_m = dt_in("w_up_m", [1024, D]); w_up_a = dt_in("w_up_a", [1024, D])
    w_out = dt_in("w_out", [D, D]); w_ff1 = dt_in("w_ff1", [D, DFF]); w_ff2 = dt_in("w_ff2", [DFF, D])
    w_pg = dt_in("w_pg", [D, D]); w_pp = dt_in("w_pp", [256, D])
    out = nc.dram_tensor("out", [T, D], F32, kind="ExternalOutput")
    scr = lambda n, s: nc.dram_tensor(n, list(s), BF16, kind=okind)
    mqT_s = scr("mqT_s", [512, T]); mkT_s = scr("mkT_s", [512, S2]); mv_s = scr("mv_s", [S2, 1024])
    smo_s = scr("smo_s", [T, 1024]); aqT_s = scr("aqT_s", [1024, T]); akT_s = scr("akT_s", [1024, S2])
    av_s = scr("av_s", [S2, 1024]); sgmT_s = scr("sgmT_s", [D, T]); sgaT_s = scr("sgaT_s", [D, T])
    hmT_s = scr("hmT_s", [1024, T]); haT_s = scr("haT_s", [1024, T])
    D_mqT, D_mkT, D_mv, D_smo, D_aqT, D_akT, D_av, D_sgm, D_sga, D_hmT, D_haT, D_out = [DramBuf(n) for n in
        ("mqT", "mkT", "mv", "smo", "aqT", "akT", "av", "sgm", "sga", "hmT", "haT", "out")]
    dbg_out = {}
    wsrc = dict(up_m=w_up_m, up_a=w_up_a, out=w_out, ff1=w_ff1, ff2=w_ff2, pg=w_pg, pp=w_pp)
    wb = {}
    D_wb = {}
    conv_jobs = []
    for nm_, W_ in wsrc.items():
        R_, C_ = W_.shape
        wb[nm_] = nc.dram_tensor("wb_" + nm_, [R_, C_], BF16, kind="Internal")
        D_wb[nm_] = DramBuf("wb_" + nm_)
        a_ = C_ // 2048
        srcf = W_.ap().rearrange("r (a b) -> (r a) b", b=2048) if a_ > 1 else W_.ap()
        dstf = wb[nm_].ap().rearrange("r (a b) -> (r a) b", b=2048) if a_ > 1 else wb[nm_].ap()
        rows = R_ * a_
        for r0 in range(0, rows, 512):
            r1 = min(rows, r0 + 512)
            conv_jobs.append((srcf[r0:r1, :], dstf[r0:r1, :], D_wb[nm_]))

    P = Prog(nc)
    op = P.op
    P.init_arena(172 * 1024)
    cv_sems = [P.dsem("cv%d" % i) for i in range(4)]
    cv_state = {"i": 0}

    def emit_conv_job(gate=()):
        i = cv_state["i"]
        if i >= len(conv_jobs):
            return False
        src_, dst_, db_ = conv_jobs[i]
        cv_state["i"] = i + 1
        op("pool", "dma_start", reads=list(gate), writes=[db_], dma=cv_sems[i % 4], out=dst_, in_=src_)
        return True

    cs = P.slot("cst", [128, NCST], F32, dma=True)
    cbs = P.slot("cstb", [128, CB_TOT], BF16, dma=True)
    CS, CBt = cs.t, cbs.t
    ident = CBt[:, CB_ID:CB_ID + 128]
    tri_bf = CBt[:, CB_TRI:CB_TRI + 128]
    tri_f = CS[:, C_TRI:C_TRI + 128]
    ones_f = CS[:, C_ONES:C_ONES + 128]
    G = P.slot("G", [128, NT, 8], F32)
    kmean = P.slot("kmean", [128, 8, NB], F32)
    kmean_b = P.slot("kmean_b", [128, 8, NB], BF16)
    halo = P.slot("halo", [128, 8, 3], F32)
    sm = P.ring("sm", 4, [128, 4], F32)
    psf = ps_alloc(P, "psf", [128, 6 * 512], F32)
    bank = lambda i: psf[:, i * 512:(i + 1) * 512]
    acc = Ring([Slot(bank(i), Buf("acc%d" % i), None) for i in range(4)])
    aux = [Slot(bank(4 + i), Buf("aux%d" % i), None) for i in range(2)]
    psb = ps_alloc(P, "psb", [128, 2048], BF16)[:, :]
    tps = Ring([Slot(psb[:, i * 1024:i * 1024 + 512], Buf("tp%d" % i), None) for i in range(2)])

    op("sp", "dma_start", writes=[cs.b], dma=cs.d, out=CS, in_=cst[:, :])
    op("pool", "dma_start", writes=[cbs.b], dma=cbs.d, out=CBt, in_=cstb[:, :])
    op("dve", "memset", writes=[halo.b], ap=halo.t, constant=0.0)
    op("dve", "memset", writes=[kmean.b], ap=kmean.t, constant=0.0)

    def load_w(ring, W, r0, nk, c0, ncol, dep=None, q="pool"):
        s = ring.next()
        for k0 in range(0, nk, 4):
            k1 = min(nk, k0 + 4)
            src = W[r0 + k0 * 128:r0 + k1 * 128, c0:c0 + ncol].rearrange("(k p) c -> p k c", p=128)
            op(q, "dma_start", reads=([dep] if dep is not None else []), writes=[s.b], dma=s.d, out=s.t[:, k0:k1, 0:ncol], in_=src)
        return s

    def norm_stats(xa, xb_, width, junk_ap, junk_b):
        s = sm.next()
        op("act", "activation", reads=[xb_], writes=[junk_b, s.b], out=junk_ap, in_=xa, func=AF.Square, accum_out=s.t[:, 0:1])
        op("dve", "tensor_scalar", reads=[s.b], writes=[s.b], out=s.t[:, 1:2], in0=s.t[:, 0:1], scalar1=1.0 / width, scalar2=EPS, op0=ALU.mult, op1=ALU.add)
        op("act", "activation", reads=[s.b], writes=[s.b], out=s.t[:, 2:3], in_=s.t[:, 1:2], func=AF.Sqrt)
        op("dve", "reciprocal", reads=[s.b], writes=[s.b], out=s.t[:, 3:4], in_=s.t[:, 2:3])
        return s

    def norm_transpose(tiles, gcol, hT, xn, grp):
        xns = []
        nt = len(tiles)
        for t, (xa, xb_) in enumerate(tiles):
            n_ = xn.next()
            s = norm_stats(xa, xb_, D, n_.t, n_.b)
            op("dve", "tensor_scalar", reads=[xb_, s.b], writes=[n_.b], out=n_.t, in0=xa, scalar1=s.t[:, 3:4], scalar2=None, op0=ALU.mult)
            xns.append(n_)
            if len(xns) == grp or t == nt - 1:
                t0 = t + 1 - len(xns)
                w = len(xns) * 128
                for k in range(KC):
                    tp = tps.next()
                    for i, n2 in enumerate(xns):
                        op("pe", "transpose", reads=[n2.b, cbs.b], writes=[tp.b], out=tp.t[:, i * 128:(i + 1) * 128],
                           in_=n2.t[:, k * 128:(k + 1) * 128], identity=ident)
                    op("dve", "tensor_tensor", reads=[tp.b, cs.b], writes=[hT.b], out=hT.t[:, k, t0 * 128:t0 * 128 + w],
                       in0=tp.t[:, 0:w], in1=mk(CS, gcol + k, [[0, w]]), op=ALU.mult)
                xns = []

    with Phase(P):
        NF = NT * 16
        sincos = P.slot("sincos", [128, 2, NF], F32)
        with Phase(P):
            posi = P.slot("posi", [128, NT], I32, dma=True)
            posf = P.slot("posf", [128, NT], F32)
            tb = P.slot("tab", [128, 4, NF], F32)
            TT = tb.t
            op("sp", "dma_start", writes=[posi.b], dma=posi.d, out=posi.t, in_=pos[:, :])
            op("dve", "tensor_copy", reads=[posi.b], writes=[posf.b], out=posf.t, in_=posi.t)
            a_pos = mk(posf.t, 0, [[1, NT], [0, 16]])
            a_inv = mk(CS, C_INVF, [[0, NT], [1, 16]])
            op("dve", "tensor_tensor", reads=[posf.b, cs.b], writes=[tb.b], out=mk(TT, 0, [[16, NT], [1, 16]]), in0=a_pos, in1=a_inv, op=ALU.mult)
            TWO_PI = 2.0 * math.pi
            C1 = 6.28125
            C2 = TWO_PI - C1
            MAGIC = 12582912.0
            PI_LO = 3.1415925
            for which, shift in ((0, 0.0), (1, math.pi / 2)):
                rw = dict(reads=[tb.b], writes=[tb.b])
                op("dve", "tensor_scalar", out=TT[:, 3, :], in0=TT[:, 0, :], scalar1=shift, scalar2=None, op0=ALU.add, **rw)
                op("dve", "tensor_scalar", out=TT[:, 1, :], in0=TT[:, 3, :], scalar1=1.0 / TWO_PI, scalar2=None, op0=ALU.mult, **rw)
                op("dve", "tensor_scalar", out=TT[:, 2, :], in0=TT[:, 1, :], scalar1=MAGIC, scalar2=None, op0=ALU.add, **rw)
                op("dve", "tensor_scalar", out=TT[:, 1, :], in0=TT[:, 2, :], scalar1=-MAGIC, scalar2=None, op0=ALU.add, **rw)
                op("dve", "scalar_tensor_tensor", out=TT[:, 2, :], in0=TT[:, 1, :], scalar=-C1, in1=TT[:, 3, :], op0=ALU.mult, op1=ALU.add, **rw)
                op("dve", "scalar_tensor_tensor", out=TT[:, 3, :], in0=TT[:, 1, :], scalar=-C2, in1=TT[:, 2, :], op0=ALU.mult, op1=ALU.add, **rw)
                op("dve", "tensor_scalar", out=TT[:, 3, :], in0=TT[:, 3, :], scalar1=PI_LO, scalar2=-PI_LO, op0=ALU.min, op1=ALU.max, **rw)
                op("act", "activation", reads=[tb.b], writes=[sincos.b], out=sincos.t[:, which, :], in_=TT[:, 3, :], func=AF.Sin)

        if debug == "A0":
            dbg = nc.dram_tensor("dbg_sincos", [128, 2 * NF], F32, kind="ExternalOutput")
            dd = P.dsem("dbg")
            op("sp", "dma_start", reads=[sincos.b], writes=[D_out], dma=dd, out=dbg[:, :], in_=mk(sincos.t, 0, [[1, 2 * NF]]))
            return finish(P, nc, [D_out])

        def tabv(which, tile, nh):
            return mk(sincos.t, which * NF + tile * 16, [[0, nh], [1, 16]])

        xt = P.ring("xt", 2, [128, D], F32, dma=True)
        xn = P.ring("xn", 4, [128, D], BF16)
        hT = P.slot("hT", [128, KC, TB], BF16)
        wt = P.ring("wt", 2, [128, KC, 512], BF16, dma=True)
        wg = P.ring("wg", 2, [128, KC, 8], BF16, dma=True)
        st = P.ring("st", 4, [128, 512], BF16, dma=True)
        rf = P.ring("rf", 2, [128, 512], F32)
        rtmp = P.ring("rtmp", 2, [128, 4, 4, 16], F32)
        rb = P.ring("rb", 2, [128, 512], BF16)
        qk_st = P.ring("qkst", 2, [128, 4, TB], BF16, dma=True)
        zp = P.ring("zp", 2, [128, 516], F32)
        cy = P.ring("cy", 2, [128, 512], F32)
        csg = P.ring("csg", 2, [128, 512], F32)
        cob = P.ring("cob", 2, [128, 512], BF16, dma=True)
        pending = []

        def flush_pending():
            while pending:
                pending.pop(0)()

        pre_x = []
        for blk in range(NBLK):
            own = blk >= PBLK
            tok0 = blk * TB
            otok0 = tok0 - T
            tiles = []
            for t in range(TB // 128):
                if pre_x:
                    xs_ = pre_x.pop(0)
                else:
                    xs_ = xt.next()
                    op("sp", "dma_start", writes=[xs_.b], dma=xs_.d, out=xs_.t, in_=xs[tok0 + t * 128:tok0 + (t + 1) * 128, :])
                tiles.append((xs_.t, xs_.b))
                if debug == "A1a" and len(tiles) == 2:
                    dbg = nc.dram_tensor("dbg_x", [128, D], F32, kind="ExternalOutput")
                    dd = P.dsem("dbg")
                    op("sp", "dma_start", reads=[xs_.b], writes=[D_out], dma=dd, out=dbg[:, :], in_=xs_.t)
                    return finish(P, nc, [D_out])
                if len(tiles) == 2:
                    t0 = t - 1
                    norm_transpose(tiles, C_GAT, Slot(mk(hT.t, t0 * 128, [[TB, KC], [1, 256]]), hT.b, None), xn, 2)
                    tiles = []
                    if debug == "A1c":
                        dbg = nc.dram_tensor("dbg_hT", [128, KC * TB], BF16, kind="ExternalOutput")
                        dd = P.dsem("dbg")
                        for k_ in range(KC):
                            op("sp", "dma_start", reads=[hT.b], writes=[D_out], dma=dd, out=dbg[:, k_ * TB:(k_ + 1) * TB], in_=hT.t[:, k_, :])
                        return finish(P, nc, [D_out])
            if debug == "A1":
                dbg = nc.dram_tensor("dbg_hT", [128, KC * TB], BF16, kind="ExternalOutput")
                dd = P.dsem("dbg")
                op("sp", "dma_start", reads=[hT.b], writes=[D_out], dma=dd, out=dbg[:, :], in_=mk(hT.t, 0, [[1, KC * TB]]))
                return finish(P, nc, [D_out])
            groups = []
            if blk >= PBLK - 1:
                groups.append(("conv", O_MQ, 0))
            groups.append(("conv", O_MK, 4))
            groups += [("v", O_MV, mv_s, D_mv, 0), ("v", O_MV + 512, mv_s, D_mv, 512)]
            if own:
                groups += [("sig", O_MO, smo_s, D_smo, 0), ("sig", O_MO + 512, smo_s, D_smo, 512)]
                groups += [("rope", O_AQ, aqT_s, D_aqT, 0, False), ("rope", O_AQ + 512, aqT_s, D_aqT, 4, False)]
            groups += [("rope", O_AK, akT_s, D_akT, 0, True), ("rope", O_AK + 512, akT_s, D_akT, 4, True)]
            groups += [("v", O_AV, av_s, D_av, 0), ("v", O_AV + 512, av_s, D_av, 512)]
            if own:
                for i in range(4):
                    groups.append(("sigT", O_GM + i * 512, sgmT_s, D_sgm, i * 512))
                for i in range(4):
                    groups.append(("sigT", O_GA + i * 512, sgaT_s, D_sga, i * 512))
            import os
            _kf = os.environ.get("KGROUPS")
            if _kf:
                groups = [g for g in groups if g[0] in _kf.split(",")]
            for g in groups:
                kind, c0 = g[0], g[1]
                w = load_w(wt, w_in, 0, KC, c0, 512)
                if g is groups[-1] and blk + 1 < NBLK:
                    for t_ in range(2):
                        xs_ = xt.next()
                        op("sp", "dma_start", writes=[xs_.b], dma=xs_.d, out=xs_.t, in_=xs[tok0 + TB + t_ * 128:tok0 + TB + (t_ + 1) * 128, :])
                        pre_x.append(xs_)
                if kind in ("v", "sig", "rope"):
                    gates = (kind == "v" and g[2] is mv_s and g[4] == 0)
                    if gates:
                        wgs = load_w(wg, w_in, 0, KC, O_MI, 8)
                    if kind == "rope":
                        qs = qk_st.next()
                    for t in range(TB // 128):
                        gt = blk * (TB // 128) + t
                        a = acc.next()
                        for k in range(KC):
                            op("pe", "matmul", reads=[hT.b, w.b], writes=[a.b], out=a.t, lhsT=hT.t[:, k, t * 128:(t + 1) * 128],
                               rhs=w.t[:, k, :], start=(k == 0), stop=(k == KC - 1))
                        if gates:
                            a2 = acc.next()
                            for k in range(KC):
                                op("pe", "matmul", reads=[hT.b, wgs.b], writes=[a2.b], out=a2.t[:, 0:8], lhsT=hT.t[:, k, t * 128:(t + 1) * 128],
                                   rhs=wgs.t[:, k, :], start=(k == 0), stop=(k == KC - 1))
                            op("dve", "tensor_tensor", reads=[a2.b, cs.b], writes=[G.b], out=G.t[:, gt, :], in0=a2.t[:, 0:8],
                               in1=CS[:, C_BIF:C_BIF + 8], op=ALU.add)
                        flush_pending()
                        if kind == "v":
                            s = st.next()
                            op("dve", "tensor_copy", reads=[a.b], writes=[s.b], out=s.t, in_=a.t)
                            op("sp", "dma_start", reads=[s.b], writes=[g[3]], dma=s.d,
                               out=g[2][tok0 + t * 128:tok0 + (t + 1) * 128, g[4]:g[4] + 512], in_=s.t)
                        elif kind == "sig":
                            s = st.next()
                            op("act", "activation", reads=[a.b], writes=[s.b], out=s.t, in_=a.t, func=AF.Sigmoid)
                            op("sp", "dma_start", reads=[s.b], writes=[g[3]], dma=s.d,
                               out=g[2][otok0 + t * 128:otok0 + (t + 1) * 128, g[4]:g[4] + 512], in_=s.t)
                        else:
                            r = rf.next(); tm = rtmp.next(); rbb = rb.next()
                            op("dve", "tensor_copy", reads=[a.b], writes=[r.b], out=r.t, in_=a.t)
                            x1 = mk(r.t, 0, [[128, 4], [1, 16]])
                            x2 = mk(r.t, 16, [[128, 4], [1, 16]])
                            cosv, sinv = tabv(1, gt, 4), tabv(0, gt, 4)
                            rr = dict(reads=[r.b, sincos.b], writes=[tm.b])
                            op("dve", "tensor_tensor", out=tm.t[:, 0], in0=x1, in1=cosv, op=ALU.mult, **rr)
                            op("dve", "tensor_tensor", out=tm.t[:, 1], in0=x2, in1=sinv, op=ALU.mult, **rr)
                            op("dve", "tensor_tensor", out=tm.t[:, 2], in0=x2, in1=cosv, op=ALU.mult, **rr)
                            op("dve", "tensor_tensor", out=tm.t[:, 3], in0=x1, in1=sinv, op=ALU.mult, **rr)
                            op("dve", "tensor_tensor", reads=[tm.b], writes=[r.b], out=x1, in0=tm.t[:, 0], in1=tm.t[:, 1], op=ALU.subtract)
                            op("dve", "tensor_tensor", reads=[tm.b], writes=[r.b], out=x2, in0=tm.t[:, 2], in1=tm.t[:, 3], op=ALU.add)
                            op("act", "activation", reads=[r.b], writes=[rbb.b], out=rbb.t, in_=r.t, func=AF.Copy)

                            def do_tr(rbb=rbb, qs=qs, t=t):
                                tp = tps.next()
                                for hh in range(4):
                                    op("pe", "transpose", reads=[rbb.b, cbs.b], writes=[tp.b], out=tp.t[:, hh * 128:(hh + 1) * 128],
                                       in_=rbb.t[:, hh * 128:(hh + 1) * 128], identity=ident)
                                dst = mk(qs.t, t * 128, [[TB, 4], [1, 128]])
                                src = mk(tp.t, 0, [[128, 4], [1, 128]])
                                op("act", "activation", reads=[tp.b], writes=[qs.b], out=dst, in_=src, func=AF.Copy)
                            pending.append(do_tr)
                    if kind == "rope":
                        flush_pending()
                        hb, is_k = g[4], g[5]
                        t0_ = tok0 if is_k else otok0
                        if is_k:
                            b0 = blk * 4
                            kmv = mk(kmean.t, hb * NB + b0, [[NB, 4], [1, 4]])
                            src = mk(qs.t, 0, [[TB, 4], [256, 4], [1, 256]])
                            op("dve", "tensor_reduce", reads=[qs.b], writes=[kmean.b], out=kmv, in_=src, axis=AX.X, op=ALU.add)
                        dstd = g[2][hb * 128:(hb + 4) * 128, t0_:t0_ + TB].rearrange("(h p) t -> p h t", p=128)
                        op("sp", "dma_start", reads=[qs.b], writes=[g[3]], dma=qs.d, out=dstd, in_=qs.t)
                else:
                    for cc in range(4):
                        for th in range(TB // 512):
                            a = acc.next()
                            for k in range(KC):
                                op("pe", "matmul", reads=[hT.b, w.b], writes=[a.b], out=a.t, lhsT=w.t[:, k, cc * 128:(cc + 1) * 128],
                                   rhs=hT.t[:, k, th * 512:(th + 1) * 512], start=(k == 0), stop=(k == KC - 1))
                            if kind == "sigT":
                                s = st.next()
                                op("act", "activation", reads=[a.b], writes=[s.b], out=s.t, in_=a.t, func=AF.Sigmoid)
                                r0 = g[4] + cc * 128
                                op("sp", "dma_start", reads=[s.b], writes=[g[3]], dma=s.d,
                                   out=g[2][r0:r0 + 128, otok0 + th * 512:otok0 + (th + 1) * 512], in_=s.t)
                            else:
                                hc = g[2] + cc
                                z = zp.next(); y = cy.next(); sg = csg.next(); ob = cob.next()
                                op("dve", "tensor_copy", reads=[halo.b], writes=[z.b], out=z.t[:, 0:3], in_=halo.t[:, hc, :])
                                op("dve", "tensor_copy", reads=[a.b], writes=[z.b], out=z.t[:, 3:515], in_=a.t)
                                op("dve", "tensor_copy", reads=[z.b], writes=[halo.b], out=halo.t[:, hc, :], in_=z.t[:, 512:515])
                                cwc = lambda j, hc=hc: CS[:, C_CW + j * 8 + hc:C_CW + j * 8 + hc + 1]
                                op("dve", "tensor_scalar", reads=[z.b, cs.b], writes=[y.b], out=y.t, in0=z.t[:, 0:512], scalar1=cwc(0),
                                   scalar2=CS[:, C_CB + hc:C_CB + hc + 1], op0=ALU.mult, op1=ALU.add)
                                for j in range(1, 4):
                                    op("dve", "scalar_tensor_tensor", reads=[z.b, cs.b, y.b], writes=[y.b], out=y.t, in0=z.t[:, j:j + 512],
                                       scalar=cwc(j), in1=y.t, op0=ALU.mult, op1=ALU.add)
                                op("act", "activation", reads=[y.b], writes=[sg.b], out=sg.t, in_=y.t, func=AF.Sigmoid)
                                scl = 1.0 if hc < 4 else 128.0 ** -0.5
                                op("dve", "scalar_tensor_tensor", reads=[y.b, sg.b], writes=[ob.b], out=ob.t, in0=y.t, scalar=scl,
                                   in1=sg.t, op0=ALU.mult, op1=ALU.mult)
                                if hc < 4:
                                    if own:
                                        op("sp", "dma_start", reads=[ob.b], writes=[D_mqT], dma=ob.d,
                                           out=mqT_s[hc * 128:(hc + 1) * 128, otok0 + th * 512:otok0 + (th + 1) * 512], in_=ob.t)
                                else:
                                    op("sp", "dma_start", reads=[ob.b], writes=[D_mkT], dma=ob.d,
                                       out=mkT_s[(hc - 4) * 128:(hc - 3) * 128, tok0 + th * 512:tok0 + (th + 1) * 512], in_=ob.t)
        op("dve", "tensor_copy", reads=[kmean.b], writes=[kmean_b.b], out=kmean_b.t, in_=kmean.t)
    if debug == "A":
        return finish(P, nc, [D_mqT, D_mkT, D_mv, D_smo, D_aqT, D_akT, D_av, D_sgm, D_sga])

    with Phase(P):
        NG = NT * 4
        gl = P.slot("gl", [128, NG], F32)
        gb = P.slot("gb", [128, NG], F32)
        gw = P.slot("gw", [128, NG], F32)
        gw2 = P.slot("gw2", [128, NG], F32)
        get = P.slot("get", [128, NG], F32)
        geL = P.slot("geL", [128, NG], F32)
        Gi = mk(G.t, 0, [[8, NT], [1, 4]])
        Gf = mk(G.t, 4, [[8, NT], [1, 4]])
        v2 = lambda s_: mk(s_.t, 0, [[4, NT], [1, 4]])
        op("act", "activation", reads=[G.b], writes=[gl.b], out=v2(gl), in_=Gf, func=AF.Exp, scale=-1.0)
        op("act", "activation", reads=[gl.b], writes=[gl.b], out=gl.t, in_=gl.t, func=AF.Ln, bias=1.0)
        a = acc.next()
        op("pe", "matmul", reads=[gl.b, cs.b], writes=[a.b], out=a.t[:, 0:NG], lhsT=tri_f, rhs=gl.t, start=True, stop=True)
        op("dve", "tensor_scalar", reads=[a.b], writes=[gb.b], out=gb.t, in0=a.t[:, 0:NG], scalar1=-1.0, scalar2=None, op0=ALU.mult)
        a = acc.next()
        op("pe", "matmul", reads=[gl.b, cs.b], writes=[a.b], out=a.t[:, 0:NG], lhsT=ones_f, rhs=gl.t, start=True, stop=True)
        op("act", "activation", reads=[a.b], writes=[geL.b], out=geL.t, in_=a.t[:, 0:NG], func=AF.Exp, scale=-1.0)
        op("act", "activation", reads=[gb.b], writes=[get.b], out=get.t, in_=gb.t, func=AF.Exp)
        op("dve", "tensor_tensor", reads=[G.b, gb.b], writes=[gw.b], out=v2(gw), in0=Gi, in1=v2(gb), op=ALU.subtract)
        op("act", "activation", reads=[gw.b], writes=[gw.b], out=gw.t, in_=gw.t, func=AF.Exp)
        op("dve", "tensor_scalar", reads=[gw.b, cs.b], writes=[gw.b], out=gw.t[:, 0:NG // 2], in0=gw.t[:, 0:NG // 2],
           scalar1=CS[:, C_PVAL:C_PVAL + 1], scalar2=None, op0=ALU.mult)
        op("dve", "tensor_tensor", reads=[gw.b, geL.b], writes=[gw2.b], out=gw2.t, in0=gw.t, in1=geL.t, op=ALU.mult)

        Cf = P.slot("Cf", [128, 4, 258], F32)
        Cb = P.slot("Cb", [128, 4, 258], BF16)
        op("dve", "memset", writes=[Cf.b], ap=Cf.t, constant=0.0)
        op("dve", "memset", writes=[Cb.b], ap=Cb.t, constant=0.0)
        CH = 8
        kTr = P.ring("kTr", 2, [128, 4, CH * 128], BF16, dma=True)
        qTr = P.ring("qTr", 2, [128, 4, CH * 128], BF16, dma=True)
        Vr = P.ring("Vr", 2, [128, CH, 4, 258], BF16, dma=True)
        smr = P.ring("smr", 2, [128, CH, 1024], BF16, dma=True)
        hst = P.ring("hst", 2, [128, 8, CH * 128], BF16, dma=True)
        for s_ in Vr.slots:
            op("dve", "memset", writes=[s_.b], ap=s_.t, constant=1.0)
        ktok = P.ring("ktok", 3, [128, 128], BF16)
        V1 = P.ring("V1", 3, [128, 258], BF16)
        V2 = P.ring("V2", 3, [128, 258], BF16)
        Smr = P.ring("Sm", 3, [128, 128], BF16)
        hmb = P.ring("hmb", 10, [128, 256], BF16)
        rawsb = P.ring("rawsb", 3, [128, 4, 258], F32)
        jk = P.ring("jk", 2, [128, 256], BF16)
        sc8 = P.ring("sc8", 4, [128, 32], F32)
        pendB = []

        def flushB(keep=0):
            while len(pendB) > keep:
                pendB.pop(0)()

        for cg in range(NT // CH):
            ownc = cg * CH >= NT // 2
            t0 = cg * CH * 128
            ot0 = t0 - T
            kT = kTr.next(); Vt = Vr.next()
            op("sp", "dma_start", reads=[D_mkT], writes=[kT.b], dma=kT.d, out=kT.t,
               in_=mkT_s[:, t0:t0 + CH * 128].rearrange("(h p) t -> p h t", p=128))
            for n_ in range(CH):
                op("sp", "dma_start", reads=[D_mv], writes=[Vt.b], dma=Vt.d, out=Vt.t[:, n_, :, 0:256],
                   in_=mv_s[t0 + n_ * 128:t0 + (n_ + 1) * 128, :].rearrange("p (h c) -> p h c", h=4))
            if ownc:
                qT = qTr.next(); smo = smr.next(); hs = hst.next()
                op("sp", "dma_start", reads=[D_mqT], writes=[qT.b], dma=qT.d, out=qT.t,
                   in_=mqT_s[:, ot0:ot0 + CH * 128].rearrange("(h p) t -> p h t", p=128))
                for n_ in range(CH):
                    op("sp", "dma_start", reads=[D_smo], writes=[smo.b], dma=smo.d, out=smo.t[:, n_, :],
                       in_=smo_s[ot0 + n_ * 128:ot0 + (n_ + 1) * 128, :])
            for cl in range(CH):
                c = cg * CH + cl
                if ownc:
                    rwc = rawsb.next(); s8c = sc8.next()
                for h in range(4):
                    col = c * 4 + h
                    flushB(4)
                    gcol = lambda s_, col=col: s_.t[:, col:col + 1]
                    ksl = kT.t[:, h, cl * 128:(cl + 1) * 128]
                    tp = tps.next()
                    op("pe", "transpose", reads=[kT.b, cbs.b], writes=[tp.b], out=tp.t[:, 0:128], in_=ksl, identity=ident)
                    kk = ktok.next()
                    op("act", "activation", reads=[tp.b], writes=[kk.b], out=kk.t, in_=tp.t[:, 0:128], func=AF.Copy)
                    vv2 = V2.next()
                    op("dve", "tensor_scalar", reads=[Vt.b, gw2.b], writes=[vv2.b], out=vv2.t[:, 0:257], in0=Vt.t[:, cl, h, 0:257],
                       scalar1=gcol(gw2), scalar2=None, op0=ALU.mult)
                    if ownc:
                        qsl = qT.t[:, h, cl * 128:(cl + 1) * 128]
                        vv1 = V1.next()
                        op("dve", "tensor_scalar", reads=[Vt.b, gw.b], writes=[vv1.b], out=vv1.t[:, 0:257], in0=Vt.t[:, cl, h, 0:257],
                           scalar1=gcol(gw), scalar2=None, op0=ALU.mult)
                        a = acc.next()
                        op("pe", "matmul", reads=[kT.b, qT.b], writes=[a.b], out=a.t[:, 0:128], lhsT=ksl, rhs=qsl, start=True, stop=True)
                        sm_ = Smr.next()
                        op("dve", "tensor_tensor", reads=[a.b, cbs.b], writes=[sm_.b], out=sm_.t, in0=a.t[:, 0:128], in1=tri_bf, op=ALU.mult)
                        a2 = acc.next()
                        op("pe", "matmul", reads=[qT.b, Cb.b], writes=[a2.b], out=a2.t[:, 0:257], lhsT=qsl, rhs=Cb.t[:, h, 0:257], start=True, stop=False)
                        op("pe", "matmul", reads=[sm_.b, vv1.b], writes=[a2.b], out=a2.t[:, 0:257], lhsT=sm_.t, rhs=vv1.t[:, 0:257], start=False, stop=True)
                        pass
                    a3 = acc.next()
                    op("pe", "matmul", reads=[kk.b, vv2.b], writes=[a3.b], out=a3.t[:, 0:257], lhsT=kk.t, rhs=vv2.t[:, 0:257], start=True, stop=True)
                    op("dve", "scalar_tensor_tensor", reads=[Cf.b, geL.b, a3.b], writes=[Cf.b], out=Cf.t[:, h, 0:257], in0=Cf.t[:, h, 0:257],
                       scalar=gcol(geL), in1=a3.t[:, 0:257], op0=ALU.mult, op1=ALU.add)
                    op("act", "activation", reads=[Cf.b], writes=[Cb.b], out=Cb.t[:, h, 0:257], in_=Cf.t[:, h, 0:257], func=AF.Copy)
                    if ownc:
                        op("dve", "tensor_copy", reads=[a2.b], writes=[rwc.b], out=rwc.t[:, h, 0:257], in_=a2.t[:, 0:257])
                        j_ = jk.next()
                        op("act", "activation", reads=[rwc.b], writes=[j_.b, s8c.b], out=j_.t, in_=rwc.t[:, h, 0:256], func=AF.Square,
                           accum_out=s8c.t[:, h:h + 1])
                if ownc:
                    R = lambda r: s8c.t[:, r * 4:(r + 1) * 4]
                    den4 = mk(rwc.t, 256, [[258, 4]])
                    get4 = get.t[:, c * 4:(c + 1) * 4]
                    rw8 = dict(reads=[s8c.b], writes=[s8c.b])
                    op("dve", "tensor_tensor", reads=[rwc.b, get.b], writes=[s8c.b], out=R(1), in0=den4, in1=get4, op=ALU.mult)
                    op("dve", "tensor_scalar", out=R(2), in0=R(1), scalar1=-1.0, scalar2=None, op0=ALU.mult, **rw8)
                    op("dve", "tensor_tensor", out=R(1), in0=R(1), in1=R(2), op=ALU.max, **rw8)
                    op("dve", "tensor_scalar", out=R(1), in0=R(1), scalar1=1.0, scalar2=None, op0=ALU.max, **rw8)
                    op("dve", "reciprocal", out=R(2), in_=R(1), **rw8)
                    op("dve", "tensor_tensor", reads=[s8c.b, get.b], writes=[s8c.b], out=R(3), in0=R(2), in1=get4, op=ALU.mult)
                    op("dve", "tensor_tensor", out=R(4), in0=R(0), in1=R(3), op=ALU.mult, **rw8)
                    op("dve", "tensor_tensor", out=R(4), in0=R(4), in1=R(3), op=ALU.mult, **rw8)
                    op("dve", "tensor_scalar", out=R(4), in0=R(4), scalar1=1.0 / 256, scalar2=EPS, op0=ALU.mult, op1=ALU.add, **rw8)
                    op("act", "activation", out=R(5), in_=R(4), func=AF.Sqrt, **rw8)
                    op("dve", "reciprocal", out=R(6), in_=R(5), **rw8)
                    op("dve", "tensor_tensor", out=R(7), in0=R(6), in1=R(3), op=ALU.mult, **rw8)
                    for h in range(4):
                        hb_ = hmb.next()
                        op("dve", "scalar_tensor_tensor", reads=[rwc.b, s8c.b, smo.b], writes=[hb_.b], out=hb_.t, in0=rwc.t[:, h, 0:256],
                           scalar=s8c.t[:, 28 + h:29 + h], in1=smo.t[:, cl, h * 256:(h + 1) * 256], op0=ALU.mult, op1=ALU.mult)

                        def _tr(hb_=hb_, hs=hs, h=h, cl=cl):
                            tp2 = tps.next()
                            for j in range(2):
                                op("pe", "transpose", reads=[hb_.b, cbs.b], writes=[tp2.b], out=tp2.t[:, j * 128:(j + 1) * 128],
                                   in_=hb_.t[:, j * 128:(j + 1) * 128], identity=ident)
                            for j in range(2):
                                fc = h * 2 + j
                                op("dve", "tensor_tensor", reads=[tp2.b, cs.b], writes=[hs.b], out=hs.t[:, fc, cl * 128:(cl + 1) * 128],
                                   in0=tp2.t[:, j * 128:(j + 1) * 128], in1=mk(CS, C_GMO + fc, [[0, 128]]), op=ALU.mult)
                        pendB.append(_tr)
            flushB()
            if ownc:
                for fc in range(8):
                    op("sp", "dma_start", reads=[hs.b], writes=[D_hmT], dma=hs.d,
                       out=hmT_s[fc * 128:(fc + 1) * 128, ot0:ot0 + CH * 128], in_=hs.t[:, fc, :])
    if debug == "B":
        return finish(P, nc, [D_hmT])

    with Phase(P):
        ccs = P.slot("cstc", [128, CC_TOT], BF16, dma=True)
        op("pool", "dma_start", writes=[ccs.b], dma=ccs.d, out=ccs.t, in_=cstc[:, :])
        CM = lambda kt: ccs.t[:, CC_CM + kt * 256:CC_CM + (kt + 1) * 256]
        CMQ = lambda hf: ccs.t[:, CC_CMQ + hf * 256:CC_CMQ + (hf + 1) * 256]
        ESEL = lambda n: ccs.t[:, CC_ESEL + n * 128:CC_ESEL + (n + 1) * 128]
        kTh = P.ring("kTh", 2, [128, S2], BF16, dma=True)
        qTh = P.ring("qTh", 2, [128, T], BF16, dma=True)
        Vh = P.ring("Vh", 2, [128, NT, 130], BF16, dma=True)
        for s_ in Vh.slots:
            op("dve", "memset", writes=[s_.b], ap=s_.t, constant=1.0)
        hast = P.ring("hast", 2, [128, T], BF16, dma=True)
        scm = P.ring("scm", 6, [128, 32], F32)
        c8 = P.ring("c8", 8, [128, 16], F32)
        Pt = P.ring("Pt", 3, [128, 512], BF16)
        hq = P.ring("hq", 2, [128, 128], BF16)
        oac = P.ring("oac", 8, [128, 130], F32)
        accC = Ring(acc.slots[0:3])
        pvr = Ring([acc.slots[3], aux[0], aux[1]])
        SCALE = 128.0 ** -0.5
        heads = {}

        def load_head(h):
            kT = kTh.next(); qT = qTh.next(); Vt = Vh.next(); ha = hast.next()
            op("sp", "dma_start", reads=[D_akT], writes=[kT.b], dma=kT.d, out=kT.t, in_=akT_s[h * 128:(h + 1) * 128, :])
            op("sp", "dma_start", reads=[D_aqT], writes=[qT.b], dma=qT.d, out=qT.t, in_=aqT_s[h * 128:(h + 1) * 128, :])
            for n0 in range(0, NT, 4):
                op("sp", "dma_start", reads=[D_av], writes=[Vt.b], dma=Vt.d, out=Vt.t[:, n0:n0 + 4, 0:128],
                   in_=av_s[n0 * 128:(n0 + 4) * 128, h * 128:(h + 1) * 128].rearrange("(n p) c -> p n c", p=128))
            heads[h] = (kT, qT, Vt, ha)

        def prologue(h, j):
            kT, qT, Vt, ha = heads[h]
            q0 = j * 256
            sels = []
            for hf in range(2):
                qsl = qT.t[:, q0 + hf * 128:q0 + (hf + 1) * 128]
                a = accC.next()
                op("pe", "matmul", reads=[qT.b, kmean_b.b], writes=[a.b], out=a.t[:, 0:NB], lhsT=qsl, rhs=kmean_b.t[:, h, :], start=True, stop=True)
                sc = scm.next(); cc8 = c8.next()
                op("dve", "tensor_tensor", reads=[a.b, cs.b], writes=[sc.b], out=sc.t[:, 0:NB], in0=a.t[:, 0:NB],
                   in1=CS[:, C_PMASK + j * NB:C_PMASK + (j + 1) * NB], op=ALU.add)
                op("dve", "max", reads=[sc.b], writes=[cc8.b], out=cc8.t[:, 0:8], in_=sc.t[:, 0:NB])
                op("dve", "tensor_scalar", reads=[cc8.b], writes=[cc8.b], out=cc8.t[:, 8:9], in0=cc8.t[:, 2:3], scalar1=-1e29, scalar2=None, op0=ALU.max)
                op("dve", "tensor_scalar", reads=[sc.b, cc8.b], writes=[sc.b], out=sc.t[:, 0:NB], in0=sc.t[:, 0:NB], scalar1=cc8.t[:, 8:9],
                   scalar2=None, op0=ALU.is_ge)
                sels.append(sc)
            return sels

        seq = [(h, j) for h in range(8) for j in range(NBO)]
        load_head(0)
        sels_cur = prologue(0, 0)
        for idx, (h, j) in enumerate(seq):
            if j == 0 and h + 1 < 8:
                load_head(h + 1)
            kT, qT, Vt, ha = heads[h]
            q0 = j * 256
            sels = sels_cur
            nblk = NBP + j + 1
            oo = [[oac.next(), oac.next()], [oac.next(), oac.next()]]
            qs2 = qT.t[:, q0:q0 + 256]
            nxt = seq[idx + 1] if idx + 1 < len(seq) else None

            def emit_S(n, kT=kT, qT=qT, nblk=nblk, qs2=qs2):
                is_own = n == nblk - 1
                a = accC.next()
                for kt in range(2):
                    kti = n * 2 + kt
                    o_ = a.t[:, kt * 256:(kt + 1) * 256]
                    op("pe", "matmul", reads=[kT.b, qT.b], writes=[a.b], out=o_, lhsT=kT.t[:, kti * 128:(kti + 1) * 128], rhs=qs2, start=True, stop=not is_own)
                    if is_own:
                        op("pe", "matmul", reads=[cbs.b, ccs.b], writes=[a.b], out=o_, lhsT=ident, rhs=CM(kt), start=False, stop=True)
                pt = Pt.next()
                op("act", "activation", reads=[a.b], writes=[pt.b], out=pt.t, in_=a.t, func=AF.Exp, scale=SCALE)
                return pt

            def emit_PV(n, pt, Vt=Vt, nblk=nblk, oo=oo, sels=sels):
                is_own = n == nblk - 1
                pv = pvr.next()
                for hf in range(2):
                    for kt in range(2):
                        kti = n * 2 + kt
                        op("pe", "matmul", reads=[pt.b, Vt.b], writes=[pv.b], out=pv.t[:, hf * 130:hf * 130 + 129],
                           lhsT=pt.t[:, kt * 256 + hf * 128:kt * 256 + (hf + 1) * 128], rhs=Vt.t[:, kti, 0:129], start=(kt == 0), stop=(kt == 1))
                for hf in range(2):
                    o = oo[hf][n % 2]
                    src = pv.t[:, hf * 130:hf * 130 + 129]
                    if n < 2:
                        wsc = 1.0 if is_own else sels[hf].t[:, n:n + 1]
                        op("dve", "tensor_scalar", reads=[pv.b, sels[hf].b], writes=[o.b], out=o.t[:, 0:129], in0=src, scalar1=wsc, scalar2=None, op0=ALU.mult)
                    else:
                        wsc = 1.0 if is_own else sels[hf].t[:, n:n + 1]
                        op("dve", "scalar_tensor_tensor", reads=[pv.b, sels[hf].b, o.b], writes=[o.b], out=o.t[:, 0:129], in0=src, scalar=wsc,
                           in1=o.t[:, 0:129], op0=ALU.mult, op1=ALU.add)

            emit_conv_job(gate=[sels[0].b])
            prev = None
            for n in range(nblk):
                pt = emit_S(n)
                if n == 1 and nxt is not None:
                    sels_cur = prologue(*nxt)
                if prev is not None:
                    emit_PV(*prev)
                prev = (n, pt)
            emit_PV(*prev)
            for hf in range(2):
                o = oo[hf][0]
                op("dve", "tensor_tensor", reads=[oo[hf][0].b, oo[hf][1].b], writes=[o.b], out=o.t[:, 0:129], in0=oo[hf][0].t[:, 0:129],
                   in1=oo[hf][1].t[:, 0:129], op=ALU.add)
                cc8 = c8.next(); hq_ = hq.next()
                op("dve", "reciprocal", reads=[o.b], writes=[cc8.b], out=cc8.t[:, 0:1], in_=o.t[:, 128:129])
                op("dve", "tensor_scalar", reads=[o.b, cc8.b], writes=[hq_.b], out=hq_.t, in0=o.t[:, 0:128], scalar1=cc8.t[:, 0:1], scalar2=None, op0=ALU.mult)
                tp = tps.next()
                op("pe", "transpose", reads=[hq_.b, cbs.b], writes=[tp.b], out=tp.t[:, 0:128], in_=hq_.t, identity=ident)
                op("act", "activation", reads=[tp.b], writes=[ha.b], out=ha.t[:, q0 + hf * 128:q0 + (hf + 1) * 128], in_=tp.t[:, 0:128], func=AF.Copy)
            if j == NBO - 1:
                op("sp", "dma_start", reads=[ha.b], writes=[D_haT], dma=ha.d, out=haT_s[h * 128:(h + 1) * 128, :], in_=ha.t)
    while emit_conv_job():
        pass
    if debug == "C":
        return finish(P, nc, [D_haT, D_hmT])

    with Phase(P):
        TD = 512
        NTD = TD // 128
        x2 = [P.slot("x2", [128, D], F32, dma=True) for _ in range(NTD)]
        hT = P.slot("hTd", [128, KC, TD], BF16)
        wt = P.ring("wtd", 3, [128, KC, 512], BF16, dma=True)
        sgr = P.ring("sgr", 4, [128, 512], BF16, dma=True)
        tmpf = P.ring("tmpf", 3, [128, 512], F32)
        rlu = P.ring("rlu", 2, [128, 512], BF16)
        xn = P.ring("xnd", 2, [128, D], BF16)
        gf = P.slot("gfin", [128, D], F32, dma=True)
        pT = P.slot("pT", [128, 2, TD], BF16)
        pin = P.ring("pin", 2, [128, 256], F32, dma=True)
        pbf = P.ring("pbf", 2, [128, 256], BF16)
        op("pool", "dma_start", writes=[gf.b], dma=gf.d, out=gf.t, in_=gfin[:, :])
        for blk in range(T // TD):
            o0 = blk * TD
            for t in range(NTD):
                op("pool", "dma_start", writes=[x2[t].b], dma=x2[t].d, out=x2[t].t, in_=xs[T + o0 + t * 128:T + o0 + (t + 1) * 128, :])
            sub1 = Phase(P)
            sub1.__enter__()
            mT = P.slot("mT", [128, KC, TD], BF16)
            hmT = P.slot("hmTd", [128, 8, TD], BF16, dma=True)
            haT = P.slot("haTd", [128, 8, TD], BF16, dma=True)
            for c0_ in (0, 4):
                op("pool", "dma_start", reads=[D_hmT], writes=[hmT.b], dma=hmT.d, out=hmT.t[:, c0_:c0_ + 4, :],
                   in_=hmT_s[c0_ * 128:(c0_ + 4) * 128, o0:o0 + TD].rearrange("(c p) t -> p c t", p=128))
                op("pool", "dma_start", reads=[D_haT], writes=[haT.b], dma=haT.d, out=haT.t[:, c0_:c0_ + 4, :],
                   in_=haT_s[c0_ * 128:(c0_ + 4) * 128, o0:o0 + TD].rearrange("(c p) t -> p c t", p=128))
            for wgi in range(4):
                wm = load_w(wt, wb["up_m"], 0, 8, wgi * 512, 512, dep=D_wb["up_m"], q="sp")
                wa = load_w(wt, wb["up_a"], 0, 8, wgi * 512, 512, dep=D_wb["up_a"], q="sp")
                for cc in range(4):
                    fc = wgi * 4 + cc
                    am = acc.next()
                    for k in range(8):
                        op("pe", "matmul", reads=[wm.b, hmT.b], writes=[am.b], out=am.t, lhsT=wm.t[:, k, cc * 128:(cc + 1) * 128], rhs=hmT.t[:, k, :], start=(k == 0), stop=(k == 7))
                    aa = acc.next()
                    for k in range(8):
                        op("pe", "matmul", reads=[wa.b, haT.b], writes=[aa.b], out=aa.t, lhsT=wa.t[:, k, cc * 128:(cc + 1) * 128], rhs=haT.t[:, k, :], start=(k == 0), stop=(k == 7))
                    s1 = sgr.next(); s2 = sgr.next(); t1 = tmpf.next(); t2 = tmpf.next()
                    op("pool", "dma_start", reads=[D_sgm], writes=[s1.b], dma=s1.d, out=s1.t, in_=sgmT_s[fc * 128:(fc + 1) * 128, o0:o0 + TD])
                    op("pool", "dma_start", reads=[D_sga], writes=[s2.b], dma=s2.d, out=s2.t, in_=sgaT_s[fc * 128:(fc + 1) * 128, o0:o0 + TD])
                    op("dve", "tensor_tensor", reads=[am.b, s1.b], writes=[t1.b], out=t1.t, in0=am.t, in1=s1.t, op=ALU.mult)
                    op("dve", "tensor_tensor", reads=[aa.b, s2.b], writes=[t2.b], out=t2.t, in0=aa.t, in1=s2.t, op=ALU.mult)
                    op("dve", "tensor_tensor", reads=[t1.b, t2.b], writes=[mT.b], out=mT.t[:, fc, :], in0=t1.t, in1=t2.t, op=ALU.add)
            for cgi in range(4):
                w = load_w(wt, wb["out"], 0, KC, cgi * 512, 512, dep=D_wb["out"], q="sp")
                for t in range(NTD):
                    a = acc.next()
                    for k in range(KC):
                        op("pe", "matmul", reads=[mT.b, w.b], writes=[a.b], out=a.t, lhsT=mT.t[:, k, t * 128:(t + 1) * 128], rhs=w.t[:, k, :], start=(k == 0), stop=(k == KC - 1))
                    xs_ = x2[t].t[:, cgi * 512:(cgi + 1) * 512]
                    op("dve", "tensor_tensor", reads=[a.b, x2[t].b], writes=[x2[t].b], out=xs_, in0=a.t, in1=xs_, op=ALU.add)
            sub1.__exit__(None, None, None)
            norm_transpose([(x2[t].t, x2[t].b) for t in range(NTD)], C_GMLP, hT, xn, 2)
            sub2 = Phase(P)
            sub2.__enter__()
            uT = P.slot("uT", [128, KC, TD], BF16)
            for qf in range(4):
                for wgi in range(4):
                    w1 = load_w(wt, wb["ff1"], 0, KC, qf * 2048 + wgi * 512, 512, dep=D_wb["ff1"], q="sp")
                    for cc in range(4):
                        fcl = wgi * 4 + cc
                        a = acc.next()
                        for k in range(KC):
                            op("pe", "matmul", reads=[w1.b, hT.b], writes=[a.b], out=a.t, lhsT=w1.t[:, k, cc * 128:(cc + 1) * 128], rhs=hT.t[:, k, :], start=(k == 0), stop=(k == KC - 1))
                        r_ = rlu.next()
                        op("act", "activation", reads=[a.b], writes=[r_.b], out=r_.t, in_=a.t, func=AF.Relu)
                        op("act", "activation", reads=[r_.b], writes=[uT.b], out=uT.t[:, fcl, :], in_=r_.t, func=AF.Square)
                for cgi in range(4):
                    w2 = load_w(wt, wb["ff2"], qf * 2048, KC, cgi * 512, 512, dep=D_wb["ff2"], q="sp")
                    for t in range(NTD):
                        a = acc.next()
                        for k in range(KC):
                            op("pe", "matmul", reads=[uT.b, w2.b], writes=[a.b], out=a.t, lhsT=uT.t[:, k, t * 128:(t + 1) * 128], rhs=w2.t[:, k, :], start=(k == 0), stop=(k == KC - 1))
                        xs_ = x2[t].t[:, cgi * 512:(cgi + 1) * 512]
                        op("dve", "tensor_tensor", reads=[a.b, x2[t].b], writes=[x2[t].b], out=xs_, in0=a.t, in1=xs_, op=ALU.add)
            sub2.__exit__(None, None, None)
            norm_transpose([(x2[t].t, x2[t].b) for t in range(NTD)], C_GPLE, hT, xn, 2)
            for t in range(NTD):
                pi = pin.next(); pb = pbf.next()
                op("pool", "dma_start", writes=[pi.b], dma=pi.d, out=pi.t, in_=pp_in[o0 + t * 128:o0 + (t + 1) * 128, :])
                op("act", "activation", reads=[pi.b], writes=[pb.b], out=pb.t, in_=pi.t, func=AF.Copy)
                tp = tps.next()
                for k in range(2):
                    op("pe", "transpose", reads=[pb.b, cbs.b], writes=[tp.b], out=tp.t[:, k * 128:(k + 1) * 128], in_=pb.t[:, k * 128:(k + 1) * 128], identity=ident)
                op("dve", "tensor_copy", reads=[tp.b], writes=[pT.b], out=mk(pT.t, t * 128, [[TD, 2], [1, 128]]), in_=mk(tp.t, 0, [[128, 2], [1, 128]]))
            for cgi in range(4):
                wgt = load_w(wt, wb["pg"], 0, KC, cgi * 512, 512, dep=D_wb["pg"], q="sp")
                wpp = load_w(wt, wb["pp"], 0, 2, cgi * 512, 512, dep=D_wb["pp"], q="sp")
                for t in range(NTD):
                    ag = acc.next()
                    for k in range(KC):
                        op("pe", "matmul", reads=[hT.b, wgt.b], writes=[ag.b], out=ag.t, lhsT=hT.t[:, k, t * 128:(t + 1) * 128], rhs=wgt.t[:, k, :], start=(k == 0), stop=(k == KC - 1))
                    ap_ = acc.next()
                    for k in range(2):
                        op("pe", "matmul", reads=[pT.b, wpp.b], writes=[ap_.b], out=ap_.t, lhsT=pT.t[:, k, t * 128:(t + 1) * 128], rhs=wpp.t[:, k, :], start=(k == 0), stop=(k == 1))
                    t1 = tmpf.next(); t2 = tmpf.next()
                    op("act", "activation", reads=[ag.b], writes=[t1.b], out=t1.t, in_=ag.t, func=AF.Sigmoid)
                    op("dve", "tensor_tensor", reads=[ap_.b, t1.b], writes=[t2.b], out=t2.t, in0=ap_.t, in1=t1.t, op=ALU.mult)
                    xs_ = x2[t].t[:, cgi * 512:(cgi + 1) * 512]
                    op("dve", "tensor_tensor", reads=[t2.b, x2[t].b], writes=[x2[t].b], out=xs_, in0=t2.t, in1=xs_, op=ALU.add)
            for t in range(NTD):
                n_ = xn.next()
                s = norm_stats(x2[t].t, x2[t].b, D, n_.t, n_.b)
                op("dve", "scalar_tensor_tensor", reads=[x2[t].b, s.b, gf.b], writes=[x2[t].b], out=x2[t].t, in0=x2[t].t, scalar=s.t[:, 3:4],
                   in1=gf.t, op0=ALU.mult, op1=ALU.mult)
                op("pool", "dma_start", reads=[x2[t].b], writes=[D_out], dma=x2[t].d, out=out[o0 + t * 128:o0 + (t + 1) * 128, :], in_=x2[t].t)
    return finish(P, nc, [D_out])


def finish(P, nc, dbufs):
    P.wait_all("sp", dbufs)
    stats = P.finalize()
    stats["arena_peak_bytes"] = P.apeak * 2
    P.stack.close()
    nc._stats = stats
    return nc


def host_inputs(T, x_b, p_b, pos_b, half, prm):
    S2 = 2 * T
    NT = S2 // 128
    NB = S2 // 256
    NBP = NB // 2
    NBO = NB - NBP
    if half == 1:
        xs = np.ascontiguousarray(x_b)
        ps = pos_b
    else:
        xs = np.concatenate([np.zeros((T, D), np.float32), x_b[:T]], axis=0)
        ps = np.concatenate([np.zeros((T,), np.int32), pos_b[:T]])
    p_own = np.ascontiguousarray(p_b[half * T:(half + 1) * T])
    NCST = C_PMASK + NBO * NB
    cst = np.zeros((128, NCST), np.float32)
    cst[:, C_GAT:C_GAT + 16] = prm["attn_norm"].reshape(16, 128).T
    cst[:, C_GMLP:C_GMLP + 16] = prm["mlp_norm"].reshape(16, 128).T
    cst[:, C_GPLE:C_GPLE + 16] = prm["ple_norm"].reshape(16, 128).T
    cst[:, C_GMO:C_GMO + 8] = prm["m_out_norm"].reshape(8, 128).T
    cw = prm["conv_w"].reshape(4, 8, 128)
    cst[:, C_CW:C_CW + 32] = cw.transpose(2, 0, 1).reshape(128, 32)
    cst[:, C_CB:C_CB + 8] = prm["conv_b"].reshape(8, 128).T
    cst[:, C_BIF:C_BIF + 8] = np.broadcast_to(prm["b_if"].reshape(1, 8), (128, 8))
    half_ = 16
    invf = (np.float32(500000.0) ** (-np.arange(half_, dtype=np.float32) * np.float32(2.0) / np.float32(32))).astype(np.float32)
    cst[:, C_INVF:C_INVF + 16] = invf[None, :]
    cst[:, C_PVAL] = float(half)
    tri, cb, cc = host_consts(T)
    cst[:, C_TRI:C_TRI + 128] = tri
    cst[:, C_ONES:C_ONES + 128] = 1.0
    pm = np.zeros((NBO, NB), np.float32)
    for j in range(NBO):
        pm[j, NBP + j:] = -1e30
        if half == 0:
            pm[j, :NBP] = -1e30
    cst[:, C_PMASK:] = pm.reshape(1, -1)
    d = dict(xs=xs, p=p_own, pos=np.ascontiguousarray(ps.reshape(NT, 128).T.astype(np.int32)), cst=cst, cstb=cb, cstc=cc,
             gfin=np.ascontiguousarray(np.broadcast_to(prm["final_norm"].reshape(1, D), (128, D))).astype(np.float32))
    return d


_NC_CACHE = {}


def kernel(x, p, positions, attn_norm, w_in, b_if, conv_w, conv_b, m_out_norm, w_up_m, w_up_a,
           w_out, mlp_norm, w_ff1, w_ff2, ple_norm, w_ple_gate, w_ple_proj, final_norm):
    x = np.asarray(x, np.float32); p = np.asarray(p, np.float32); positions = np.asarray(positions, np.int32)
    B, S, _ = x.shape
    T = S // 2
    prm = dict(attn_norm=np.asarray(attn_norm, np.float32)[0], mlp_norm=np.asarray(mlp_norm, np.float32)[0],
               ple_norm=np.asarray(ple_norm, np.float32)[0], m_out_norm=np.asarray(m_out_norm, np.float32)[0],
               conv_w=np.asarray(conv_w, np.float32)[0], conv_b=np.asarray(conv_b, np.float32)[0],
               b_if=np.asarray(b_if, np.float32)[0], final_norm=np.asarray(final_norm, np.float32))
    wts = dict(w_in=np.ascontiguousarray(np.asarray(w_in, np.float32)[0]), w_up_m=np.ascontiguousarray(np.asarray(w_up_m, np.float32)[0]),
               w_up_a=np.ascontiguousarray(np.asarray(w_up_a, np.float32)[0]), w_out=np.ascontiguousarray(np.asarray(w_out, np.float32)[0]),
               w_ff1=np.ascontiguousarray(np.asarray(w_ff1, np.float32)[0]), w_ff2=np.ascontiguousarray(np.asarray(w_ff2, np.float32)[0]),
               w_pg=np.ascontiguousarray(np.asarray(w_ple_gate, np.float32)[0]), w_pp=np.ascontiguousarray(np.asarray(w_ple_proj, np.float32)[0]))
    ncores = 2 * B
    if T not in _NC_CACHE:
        _NC_CACHE[T] = build(T)
    nc = _NC_CACHE[T]
    in_maps = []
    for c in range(ncores):
        b, half = c // 2, c % 2
        d = host_inputs(T, x[b], p[0, b], positions[b], half, prm)
        d.update(wts)
        in_maps.append(d)
    res = run_bass_kernel_spmd(nc, in_maps, core_ids=list(range(ncores)))
    outp = np.empty((B, S, D), np.float32)
    for c in range(ncores):
        b, half = c // 2, c % 2
        outp[b, half * T:(half + 1) * T] = res.results[c]["out"]
    return outp
```
